# Optimizing a Trainium2 kernel written in Bass

```python
import jax
import jax.numpy as jnp
from jax import lax
import numpy as np

D_MODEL = 1024
BATCH = 4
SEQ = 4096
DEPTH = 2

N_HEADS = 16
N_KV_GROUPS = 4
HEAD_DIM = 64
HEADS_PER_GROUP = N_HEADS // N_KV_GROUPS
L_CMP = 32
CMP_STRIDE = 16
CMP_HIDDEN = 256
L_SEL = 64
N_SELECT = 16
WINDOW = 512
Q_BLOCK = 64
NSA_Q = N_HEADS * HEAD_DIM
NSA_KV = N_KV_GROUPS * HEAD_DIM
NSA_IN = NSA_Q + 6 * NSA_KV + 3 * N_HEADS

D_RNN = 1408
LRU_BLOCKS = 8
LRU_BLOCK_W = D_RNN // LRU_BLOCKS
LRU_C = 8.0
CONV_W = 4

D_FF = 2816
EPS = 1e-6

kernel_name = "nsa_rglru_interleaved_hybrid"


def rmsnorm(x, g):
    xf = x.astype(jnp.float32)
    y = xf * lax.rsqrt(jnp.mean(xf * xf, axis=-1, keepdims=True) + EPS)
    return (y * g.astype(jnp.float32)).astype(x.dtype)


def masked_softmax(scores, mask):
    s = jnp.where(mask, scores.astype(jnp.float32), -jnp.inf)
    m = jnp.max(s, axis=-1, keepdims=True)
    m = jnp.where(jnp.isfinite(m), m, 0.0)
    e = jnp.exp(s - m)
    d = jnp.sum(e, axis=-1, keepdims=True)
    return e / jnp.where(d > 0, d, 1.0)


def alibi_slopes():
    return 2.0 ** (-8.0 * jnp.arange(1, N_HEADS + 1, dtype=jnp.float32) / N_HEADS)


def compress_blocks(raw, pos, w1, b1, w2, b2):
    b, s = raw.shape[0], raw.shape[1]
    n_cmp = (s - L_CMP) // CMP_STRIDE + 1
    idx = jnp.arange(n_cmp)[:, None] * CMP_STRIDE + jnp.arange(L_CMP)[None, :]
    blocks = raw[:, idx] + pos[None, None, :, None, :]
    flat = blocks.transpose(0, 1, 3, 2, 4).reshape(b, n_cmp, N_KV_GROUPS, L_CMP * HEAD_DIM)
    hid = jax.nn.gelu(flat @ w1 + b1)
    return hid @ w2 + b2


def nsa_mixer(h, w_in, b_gate, cmp_pos, cmp_w1, cmp_b1, cmp_w2, cmp_b2, w_out):
    b, s, _ = h.shape
    G, R, dh = N_KV_GROUPS, HEADS_PER_GROUP, HEAD_DIM
    proj = h @ w_in
    cuts = [NSA_Q + i * NSA_KV for i in range(7)]
    q, kc_raw, vc_raw, k_sel_all, v_sel_all, k_win, v_win, g_logit = jnp.split(proj, cuts, axis=-1)
    q = q.reshape(b, s, G, R, dh) * (HEAD_DIM ** -0.5)
    kv = lambda z: z.reshape(b, s, G, dh)
    kc_raw, vc_raw, k_sel_all, v_sel_all, k_win, v_win = map(
        kv, (kc_raw, vc_raw, k_sel_all, v_sel_all, k_win, v_win))
    gates = jax.nn.sigmoid((g_logit + b_gate).astype(jnp.float32)).reshape(b, s, G, R, 3)

    kc = compress_blocks(kc_raw, cmp_pos[0], cmp_w1[0], cmp_b1[0], cmp_w2[0], cmp_b2[0])
    vc = compress_blocks(vc_raw, cmp_pos[1], cmp_w1[1], cmp_b1[1], cmp_w2[1], cmp_b2[1])
    n_cmp = kc.shape[1]
    cmp_start = jnp.arange(n_cmp) * CMP_STRIDE
    cmp_end = cmp_start + L_CMP - 1

    n_sb = s // L_SEL
    n_sel = min(N_SELECT, n_sb)
    sb_start = jnp.arange(n_sb) * L_SEL
    overlap = ((cmp_start[:, None] < sb_start[None, :] + L_SEL)
               & (cmp_start[:, None] + L_CMP > sb_start[None, :])).astype(jnp.float32)
    ks_t = k_sel_all.transpose(0, 2, 1, 3)
    vs_t = v_sel_all.transpose(0, 2, 1, 3)
    gather = jax.vmap(jax.vmap(lambda src, ix: src[ix]))
    sel_off = jnp.arange(L_SEL)
    jb = jnp.arange(n_sb)

    kw_pad = jnp.pad(k_win, ((0, 0), (WINDOW, 0), (0, 0), (0, 0)))
    vw_pad = jnp.pad(v_win, ((0, 0), (WINDOW, 0), (0, 0), (0, 0)))
    slopes = alibi_slopes().reshape(G, R)[None, None, :, :, None]

    def block(qb):
        start = qb * Q_BLOCK
        t = start + jnp.arange(Q_BLOCK)
        qq = lax.dynamic_slice_in_dim(q, start, Q_BLOCK, axis=1)
        gg = lax.dynamic_slice_in_dim(gates, start, Q_BLOCK, axis=1)

        dist_c = t[:, None] - cmp_end[None, :]
        s_c = (jnp.einsum('bqgrd,bngd->bqgrn', qq, kc).astype(jnp.float32)
               - slopes * jnp.abs(dist_c).astype(jnp.float32)[None, :, None, None, :])
        p_c = masked_softmax(s_c, (dist_c >= 0)[None, :, None, None, :])
        o_c = jnp.einsum('bqgrn,bngd->bqgrd', p_c.astype(vc.dtype), vc)

        imp = jnp.einsum('bqgrn,nj->bqgj', p_c, overlap)
        valid_b = (sb_start[None, :] <= t[:, None])[None, :, None, :]
        cur = (t // L_SEL)[:, None]
        forced = ((jb[None, :] == 0) | (jb[None, :] == cur) | (jb[None, :] == cur - 1))[None, :, None, :]
        score = jnp.where(valid_b, jnp.where(forced, jnp.inf, imp), -jnp.inf)
        top_val, top_idx = lax.top_k(score, n_sel)
        tok = (top_idx[..., None] * L_SEL + sel_off).reshape(b, Q_BLOCK, G, n_sel * L_SEL)
        blk_ok = jnp.broadcast_to((top_val > -jnp.inf)[..., None],
                                  top_val.shape + (L_SEL,)).reshape(tok.shape)
        tok_t = tok.transpose(0, 2, 1, 3)
        k_sel = gather(ks_t, tok_t)
        v_sel = gather(vs_t, tok_t)
        dist_s = t[None, :, None, None] - tok
        s_s = (jnp.einsum('bqgrd,bgqtd->bqgrt', qq, k_sel).astype(jnp.float32)
               - slopes * jnp.abs(dist_s).astype(jnp.float32)[:, :, :, None, :])
        p_s = masked_softmax(s_s, (blk_ok & (dist_s >= 0))[:, :, :, None, :])
        o_s = jnp.einsum('bqgrt,bgqtd->bqgrd', p_s.astype(v_sel.dtype), v_sel)

        kw_b = lax.dynamic_slice_in_dim(kw_pad, start, WINDOW + Q_BLOCK, axis=1)
        vw_b = lax.dynamic_slice_in_dim(vw_pad, start, WINDOW + Q_BLOCK, axis=1)
        s_pos = start - WINDOW + jnp.arange(WINDOW + Q_BLOCK)
        dist_w = t[:, None] - s_pos[None, :]
        mask_w = (dist_w >= 0) & (dist_w < WINDOW) & (s_pos[None, :] >= 0)
        s_w = (jnp.einsum('bqgrd,bkgd->bqgrk', qq, kw_b).astype(jnp.float32)
               - slopes * jnp.abs(dist_w).astype(jnp.float32)[None, :, None, None, :])
        p_w = masked_softmax(s_w, mask_w[None, :, None, None, :])
        o_w = jnp.einsum('bqgrk,bkgd->bqgrd', p_w.astype(vw_b.dtype), vw_b)

        out = (gg[..., 0:1] * o_c.astype(jnp.float32)
               + gg[..., 1:2] * o_s.astype(jnp.float32)
               + gg[..., 2:3] * o_w.astype(jnp.float32))
        return out.astype(h.dtype).reshape(b, Q_BLOCK, NSA_Q)

    outs = lax.map(block, jnp.arange(s // Q_BLOCK))
    y = outs.transpose(1, 0, 2, 3).reshape(b, s, NSA_Q)
    return y @ w_out


def lru_mixer(h, w_in, conv_w, conv_b, w_a, b_a, w_x, b_x, lam, w_out):
    b, s, _ = h.shape
    proj = h @ w_in
    gate_br, rec = jnp.split(proj, [D_RNN], axis=-1)
    gate = jax.nn.gelu(gate_br, approximate=True)
    xr = lax.conv_general_dilated(rec, conv_w[:, None, :], window_strides=(1,),
                                  padding=[(CONV_W - 1, 0)],
                                  dimension_numbers=('NWC', 'WIO', 'NWC'),
                                  feature_group_count=D_RNN) + conv_b
    xb = xr.reshape(b, s, LRU_BLOCKS, LRU_BLOCK_W)
    r = jax.nn.sigmoid((jnp.einsum('bsnc,ncd->bsnd', xb, w_a).reshape(b, s, D_RNN) + b_a).astype(jnp.float32))
    i = jax.nn.sigmoid((jnp.einsum('bsnc,ncd->bsnd', xb, w_x).reshape(b, s, D_RNN) + b_x).astype(jnp.float32))
    log_a = -LRU_C * r * jax.nn.softplus(-lam.astype(jnp.float32))
    a = jnp.exp(log_a)
    u = jnp.sqrt(jnp.maximum(-jnp.expm1(2.0 * log_a), 0.0)) * (i * xr.astype(jnp.float32))

    def combine(left, right):
        a1, b1 = left
        a2, b2 = right
        return a1 * a2, a2 * b1 + b2

    _, hs = lax.associative_scan(combine, (a, u), axis=1)
    return (hs.astype(h.dtype) * gate) @ w_out


def swiglu(h, w_in, w_out):
    g, u = jnp.split(h @ w_in, [D_FF], axis=-1)
    return (jax.nn.silu(g) * u) @ w_out


def setup_inputs(seed: int = 0) -> dict:
    key = jax.random.key(seed)
    ks = jax.random.split(key, 24)
    n_a = (DEPTH + 1) // 2
    n_b = DEPTH // 2
    f32 = jnp.float32

    def nrm(k, shape, scale):
        return jax.random.normal(k, shape, f32) * scale

    x = nrm(ks[0], (BATCH, SEQ, D_MODEL), 1.0)
    norm_mix = 1.0 + nrm(ks[1], (DEPTH, D_MODEL), 0.02)
    norm_ffn = 1.0 + nrm(ks[2], (DEPTH, D_MODEL), 0.02)
    norm_final = 1.0 + nrm(ks[3], (D_MODEL,), 0.02)
    nsa_w_in = nrm(ks[4], (n_a, D_MODEL, NSA_IN), D_MODEL ** -0.5)
    nsa_b_gate = nrm(ks[5], (n_a, 3 * N_HEADS), 0.02)
    nsa_cmp_pos = nrm(ks[6], (n_a, 2, L_CMP, HEAD_DIM), 0.02)
    nsa_cmp_w1 = nrm(ks[7], (n_a, 2, L_CMP * HEAD_DIM, CMP_HIDDEN), (L_CMP * HEAD_DIM) ** -0.5)
    nsa_cmp_b1 = nrm(ks[8], (n_a, 2, CMP_HIDDEN), 0.02)
    nsa_cmp_w2 = nrm(ks[9], (n_a, 2, CMP_HIDDEN, HEAD_DIM), CMP_HIDDEN ** -0.5)
    nsa_cmp_b2 = nrm(ks[10], (n_a, 2, HEAD_DIM), 0.02)
    nsa_w_out = nrm(ks[11], (n_a, NSA_Q, D_MODEL), NSA_Q ** -0.5)
    lru_w_in = nrm(ks[12], (n_b, D_MODEL, 2 * D_RNN), D_MODEL ** -0.5)
    lru_conv_w = nrm(ks[13], (n_b, CONV_W, D_RNN), CONV_W ** -0.5)
    lru_conv_b = nrm(ks[14], (n_b, D_RNN), 0.02)
    lru_w_a = nrm(ks[15], (n_b, LRU_BLOCKS, LRU_BLOCK_W, LRU_BLOCK_W), LRU_BLOCK_W ** -0.5)
    lru_b_a = nrm(ks[16], (n_b, D_RNN), 0.02)
    lru_w_x = nrm(ks[17], (n_b, LRU_BLOCKS, LRU_BLOCK_W, LRU_BLOCK_W), LRU_BLOCK_W ** -0.5)
    lru_b_x = nrm(ks[18], (n_b, D_RNN), 0.02)
    a_c = jax.random.uniform(ks[19], (n_b, D_RNN), f32, 0.9, 0.999)
    sig = a_c ** (1.0 / LRU_C)
    lru_lambda = jnp.log(sig) - jnp.log1p(-sig)
    lru_w_out = nrm(ks[20], (n_b, D_RNN, D_MODEL), D_RNN ** -0.5)
    ffn_w_in = nrm(ks[21], (DEPTH, D_MODEL, 2 * D_FF), D_MODEL ** -0.5)
    ffn_w_out = nrm(ks[22], (DEPTH, D_FF, D_MODEL), D_FF ** -0.5)
    return {
        "x": x, "norm_mix": norm_mix, "norm_ffn": norm_ffn, "norm_final": norm_final,
        "nsa_w_in": nsa_w_in, "nsa_b_gate": nsa_b_gate, "nsa_cmp_pos": nsa_cmp_pos,
        "nsa_cmp_w1": nsa_cmp_w1, "nsa_cmp_b1": nsa_cmp_b1, "nsa_cmp_w2": nsa_cmp_w2,
        "nsa_cmp_b2": nsa_cmp_b2, "nsa_w_out": nsa_w_out,
        "lru_w_in": lru_w_in, "lru_conv_w": lru_conv_w, "lru_conv_b": lru_conv_b,
        "lru_w_a": lru_w_a, "lru_b_a": lru_b_a, "lru_w_x": lru_w_x, "lru_b_x": lru_b_x,
        "lru_lambda": lru_lambda, "lru_w_out": lru_w_out,
        "ffn_w_in": ffn_w_in, "ffn_w_out": ffn_w_out,
    }


def reference(x, norm_mix, norm_ffn, norm_final,
              nsa_w_in, nsa_b_gate, nsa_cmp_pos, nsa_cmp_w1, nsa_cmp_b1, nsa_cmp_w2,
              nsa_cmp_b2, nsa_w_out,
              lru_w_in, lru_conv_w, lru_conv_b, lru_w_a, lru_b_a, lru_w_x, lru_b_x,
              lru_lambda, lru_w_out,
              ffn_w_in, ffn_w_out):
    for i in range(DEPTH):
        hn = rmsnorm(x, norm_mix[i])
        j = i // 2
        if i % 2 == 0:
            x = x + nsa_mixer(hn, nsa_w_in[j], nsa_b_gate[j], nsa_cmp_pos[j], nsa_cmp_w1[j],
                              nsa_cmp_b1[j], nsa_cmp_w2[j], nsa_cmp_b2[j], nsa_w_out[j])
        else:
            x = x + lru_mixer(hn, lru_w_in[j], lru_conv_w[j], lru_conv_b[j], lru_w_a[j],
                              lru_b_a[j], lru_w_x[j], lru_b_x[j], lru_lambda[j], lru_w_out[j])
        x = x + swiglu(rmsnorm(x, norm_ffn[i]), ffn_w_in[i], ffn_w_out[i])
    return rmsnorm(x, norm_final)
```

```python
import contextlib
import numpy as np
import ml_dtypes
import concourse.bass as bass
import concourse.mybir as mybir
from concourse.bass_utils import run_bass_kernel_spmd

F32 = mybir.dt.float32
BF16 = mybir.dt.bfloat16
AF = mybir.ActivationFunctionType
ALU = mybir.AluOpType
AX = mybir.AxisListType

S = 4096
D = 1024
NT = S // 128
NSA_IN = 2608
D_RNN = 1408
D_FF = 2816
EPS = 1e-6
SLOPES = [2.0 ** (-8.0 * (h + 1) / 16) for h in range(16)]
BIGD = 1.0e6


class Dom:
    def __init__(self, fw, name, unit):
        self.sem = fw.es.enter_context(fw.nc.semaphore(name))
        self.unit = unit
        self.count = 0


class T:
    __slots__ = ("w", "r", "dd")

    def __init__(self):
        self.w = None
        self.r = {}
        self.dd = None


class Eng:
    def __init__(self, fw, name, eng, is_pe=False, has_dom=True):
        self.name = name
        self.eng = eng
        self.is_pe = is_pe
        self.dom = Dom(fw, "c_" + name, 1) if has_dom else None
        self.known = {}


class FW:
    def __init__(self, nc):
        self.nc = nc
        self.es = contextlib.ExitStack()
        self.pe = Eng(self, "pe", nc.tensor, is_pe=True)
        self.act = Eng(self, "act", nc.scalar)
        self.dve = Eng(self, "dve", nc.vector)
        self.pool = Eng(self, "pool", nc.gpsimd)
        self.sp = Eng(self, "sp", nc.sync, has_dom=False)
        self.dma_doms = []
        self.free_doms = []
        self.uid = 0

    def sbuf(self, st, name, shape, dt):
        self.uid += 1
        return st.enter_context(self.nc.sbuf_tensor("%s_%d" % (name, self.uid), list(shape), dt))

    def _waits(self, E, r, w):
        deps = {}
        for t in r:
            if t.w is not None and deps.get(t.w[0], 0) < t.w[1]:
                deps[t.w[0]] = t.w[1]
        for t in w:
            if t.w is not None and deps.get(t.w[0], 0) < t.w[1]:
                deps[t.w[0]] = t.w[1]
            for d, s in t.r.items():
                if deps.get(d, 0) < s:
                    deps[d] = s
        for d, s in deps.items():
            if E.is_pe and d is E.dom:
                continue
            if E.known.get(d, 0) >= s:
                continue
            E.eng.wait_ge(d.sem, s * d.unit)
            E.known[d] = s

    def op(self, E, fn, r=(), w=()):
        self._waits(E, r, w)
        ins = fn(E.eng)
        d = E.dom
        d.count += 1
        ins.then_inc(d.sem, 1)
        for t in r:
            t.r[d] = d.count
        for t in w:
            t.w = (d, d.count)
            t.r = {}
        return ins

    def dma(self, E, out, in_, r=(), w=(), **kw):
        self._waits(E, r, w)
        t0 = w[0] if len(w) else r[0]
        if t0.dd is None:
            t0.dd = Dom(self, "d%d" % len(self.dma_doms), 16)
            self.dma_doms.append(t0.dd)
        d = t0.dd
        ins = E.eng.dma_start(out=out, in_=in_, **kw)
        d.count += 1
        ins.then_inc(d.sem, 16)
        for t in r:
            t.r[d] = d.count
        for t in w:
            t.w = (d, d.count)
            t.r = {}
        return ins

    def barrier(self):
        doms = [d for d in self.dma_doms if d.count] + [X.dom for X in (self.pe, self.act, self.dve, self.pool) if X.dom.count]
        for E in (self.pe, self.act, self.dve, self.pool, self.sp):
            for d in doms:
                if E.known.get(d, 0) < d.count:
                    E.eng.wait_ge(d.sem, d.count * d.unit)
                    E.known[d] = d.count

    def finish(self):
        E = self.sp
        for d in self.dma_doms:
            if d.count:
                E.eng.wait_ge(d.sem, d.count * d.unit)
        for X in (self.pe, self.act, self.dve, self.pool):
            if X.dom.count:
                E.eng.wait_ge(X.dom.sem, X.dom.count)


def host_consts():
    c = {}
    c["ident_bf"] = np.eye(128, dtype=np.float32).astype(ml_dtypes.bfloat16)
    c["ident_f"] = np.eye(128, dtype=np.float32)
    e = np.zeros((64, S), np.float32)
    for j in range(64):
        e[j, j * 64:(j + 1) * 64] = 256.0
    c["e256"] = e.astype(ml_dtypes.bfloat16)
    sr = np.arange(128)[:, None].astype(np.float32)
    tr = np.arange(128)[None, :].astype(np.float32)
    d0 = tr - sr
    c["d0"] = np.stack([np.where(d0 >= 0, d0, BIGD), d0, np.where(d0 < 0, d0, BIGD)]).astype(np.float32)
    i = np.arange(128)[:, None]
    m = np.arange(-248, 256)[None, :]
    dc = (i - 16 * m - 31).astype(np.float32)
    c["distc"] = np.where(dc >= 0, dc, BIGD).astype(np.float32)
    jp = np.arange(-62, 64)[None, :]
    ci = (np.arange(128)[:, None] // 64)
    fb = np.zeros((128, 126), np.float32)
    fb = np.where((jp == ci) | (jp == ci - 1), 100.0, fb)
    fb = np.where(jp > ci, -100.0, fb)
    c["fbias"] = fb.astype(np.float32)
    ov = np.zeros((256, 64), np.float32)
    for n in range(255):
        for j in range(64):
            if 16 * n < 64 * j + 64 and 16 * n + 32 > 64 * j:
                ov[n, j] = 1.0
    c["ovl"] = ov.reshape(2, 128, 64).astype(ml_dtypes.bfloat16)
    return c


def build(debug=False, stage=99):
    nc = bass.Bass("TRN2", target_bir_lowering=False)
    fw = FW(nc)
    pe, act, dve, pool, sp = fw.pe, fw.act, fw.dve, fw.pool, fw.sp

    def din(name, shape, dt=F32):
        return nc.dram_tensor(name, list(shape), dt, kind="ExternalInput").ap()

    def dscr(name, shape, dt):
        return nc.dram_tensor(name, list(shape), dt, kind="ExternalOutput" if debug else "Internal").ap()

    x_in = din("x", [S, D])
    norm_mix = din("norm_mix", [2, D])
    norm_ffn = din("norm_ffn", [2, D])
    norm_final = din("norm_final", [D])
    nsa_w_in = din("nsa_w_in", [D, NSA_IN])
    nsa_b_gate = din("nsa_b_gate", [48])
    nsa_cmp_pos = din("nsa_cmp_pos", [2, 32, 64])
    nsa_cmp_w1 = din("nsa_cmp_w1", [2, 2048, 256])
    nsa_cmp_b1 = din("nsa_cmp_b1", [2, 256])
    nsa_cmp_w2 = din("nsa_cmp_w2", [2, 256, 64])
    nsa_cmp_b2 = din("nsa_cmp_b2", [2, 64])
    nsa_w_out = din("nsa_w_out", [D, D])
    lru_w_in = din("lru_w_in", [D, 2 * D_RNN])
    lru_conv_w = din("lru_conv_w", [4, D_RNN])
    lru_conv_b = din("lru_conv_b", [D_RNN])
    lru_w_a = din("lru_w_a", [8, 176, 176])
    lru_b_a = din("lru_b_a", [D_RNN])
    lru_w_x = din("lru_w_x", [8, 176, 176])
    lru_b_x = din("lru_b_x", [D_RNN])
    lru_lambda = din("lru_lambda", [D_RNN])
    lru_w_out = din("lru_w_out", [D_RNN, D])
    ffn_w_in = din("ffn_w_in", [2, D, 2 * D_FF])
    ffn_w_out = din("ffn_w_out", [2, D_FF, D])
    c_ident_bf = din("ident_bf", [128, 128], BF16)
    c_ident_f = din("ident_f", [128, 128])
    c_e256 = din("e256", [64, S], BF16)
    c_d0 = din("d0", [3, 128, 128])
    c_distc = din("distc", [128, 504])
    c_fbias = din("fbias", [128, 126])
    c_ovl = din("ovl", [2, 128, 64], BF16)

    out_ap = nc.dram_tensor("out", [S, D], F32, kind="ExternalOutput").ap()

    QT = dscr("QT", [1024, S], BF16)
    KcT = dscr("KcT", [256, S], BF16)
    VcT = dscr("VcT", [256, S], BF16)
    KsT = dscr("KsT", [256, S], BF16)
    KwT = dscr("KwT", [256, S], BF16)
    Vs = dscr("Vs", [S, 256], BF16)
    Vw = dscr("Vw", [S, 256], BF16)
    G = dscr("G", [S, 48], F32)
    Y = dscr("Y", [S, D], BF16)
    X1 = dscr("X1", [S, D], F32)
    tQT, tKcT, tVcT, tKsT, tKwT, tVs, tVw, tG, tY, tX1 = [T() for _ in range(10)]

    with fw.es:
        gst = fw.es
        ps = [gst.enter_context(nc.psum_tensor("ps%d" % i, [128, 512], F32)) for i in range(8)]
        pst = [T() for _ in range(8)]
        ident_bf = fw.sbuf(gst, "identbf", [128, 128], BF16)
        ident_f = fw.sbuf(gst, "identf", [128, 128], F32)
        t_ident = T()
        fw.dma(sp, ident_bf[:], c_ident_bf, w=[t_ident])
        fw.dma(sp, ident_f[:], c_ident_f, w=[t_ident])
        epsc = fw.sbuf(gst, "epsc", [128, 1], F32)
        t_eps = T()
        fw.op(dve, lambda e: e.memset(epsc[:], EPS), w=[t_eps])

        rr = [0]

        def evac(out, in_, r, w, scale=None):
            rr[0] += 1
            if rr[0] % 2 == 0:
                if scale is None:
                    fw.op(act, lambda e: e.activation(out=out, in_=in_, func=AF.Copy), r=r, w=w)
                else:
                    fw.op(act, lambda e: e.activation(out=out, in_=in_, func=AF.Copy, scale=scale), r=r, w=w)
            else:
                if scale is None:
                    fw.op(dve, lambda e: e.tensor_copy(out=out, in_=in_), r=r, w=w)
                else:
                    fw.op(dve, lambda e: e.tensor_scalar(out=out, in0=in_, scalar1=scale, scalar2=None, op0=ALU.mult), r=r, w=w)

        def load_gain(st, g_ap, name):
            gt = fw.sbuf(st, name, [128, 8], F32)
            tg = T()
            fw.dma(sp, gt[:], g_ap.rearrange("(k p) -> p k", p=128), w=[tg], allow_slow_non_contiguous=True)
            return gt, tg

        cast_rr = [0]

        def load_cast(st, dst, tdst, src, nrows, ncols, gain=None, tgain=None, stg=None, tstg=None):
            CH = stg[0].shape[1]
            for c0 in range(0, ncols, CH):
                cw = min(CH, ncols - c0)
                k = cast_rr[0] % len(stg)
                cast_rr[0] += 1
                s_, ts_ = stg[k], tstg[k]
                fw.dma((sp, pool, act, sp)[k % 4], s_[0:nrows, 0:cw], src[:, c0:c0 + cw], w=[ts_])
                if k % 2 == 0:
                    if gain is None:
                        fw.op(dve, lambda e: e.tensor_copy(out=dst[:, c0:c0 + cw], in_=s_[0:nrows, 0:cw]), r=[ts_], w=[tdst])
                    else:
                        fw.op(dve, lambda e: e.tensor_scalar(out=dst[:, c0:c0 + cw], in0=s_[0:nrows, 0:cw], scalar1=gain,
                                                             scalar2=None, op0=ALU.mult), r=[ts_, tgain], w=[tdst])
                else:
                    if gain is None:
                        fw.op(act, lambda e: e.activation(out=dst[:, c0:c0 + cw], in_=s_[0:nrows, 0:cw], func=AF.Copy), r=[ts_], w=[tdst])
                    else:
                        fw.op(act, lambda e: e.activation(out=dst[:, c0:c0 + cw], in_=s_[0:nrows, 0:cw], func=AF.Copy, scale=gain),
                              r=[ts_, tgain], w=[tdst])

        def rmsnorm_bf(xt, tx, hn, thn, junk, tjunk, st_small, tsm):
            ss, sd, rs = st_small
            fw.op(act, lambda e: e.activation(out=junk, in_=xt, func=AF.Square, accum_out=ss), r=[tx], w=[tjunk, tsm])
            fw.op(act, lambda e: e.activation(out=sd, in_=ss, func=AF.Sqrt, bias=epsc[:], scale=1.0 / D), r=[tsm, t_eps], w=[tsm])
            fw.op(dve, lambda e: e.reciprocal(out=rs, in_=sd), r=[tsm], w=[tsm])
            fw.op(dve, lambda e: e.tensor_scalar(out=hn, in0=xt, scalar1=rs, scalar2=None, op0=ALU.mult), r=[tx, tsm], w=[thn])

        def transpose_to(hn, thn, dstT, tdst, nchunk, pbank, tpbank, col0, rows=128):
            pv = pbank[:].bitcast(BF16)
            for k0 in range(0, nchunk, 8):
                kn = min(8, nchunk - k0)
                for k in range(kn):
                    fw.op(pe, lambda e: e.transpose(out=pv[:, k * 128:(k + 1) * 128], in_=hn[:, (k0 + k) * 128:(k0 + k + 1) * 128],
                                                    identity=ident_bf[:]), r=[thn, t_ident], w=[tpbank])
                evac(dstT[:, k0:k0 + kn, col0:col0 + 128], pv[:, 0:kn * 128].rearrange("p (k t) -> p k t", k=kn), r=[tpbank], w=[tdst])

        if stage >= 1:
            with contextlib.ExitStack() as st:
                w_in = fw.sbuf(st, "w_in", [128, 8, NSA_IN], BF16)
                t_w = T()
                g0, tg0 = load_gain(st, norm_mix[0], "g0")
                stg = [fw.sbuf(st, "stg", [128, 2608], F32) for _ in range(2)]
                tstg = [T(), T()]
                for k in range(8):
                    load_cast(st, w_in[:, k, :], t_w, nsa_w_in[k * 128:(k + 1) * 128, :], 128, NSA_IN,
                              gain=g0[:, k:k + 1], tgain=tg0, stg=stg, tstg=tstg)
                bg = fw.sbuf(st, "bg", [128, 48], F32)
                t_bg = T()
                fw.dma(sp, bg[:], nsa_b_gate.partition_broadcast(128), w=[t_bg])
                xt = [fw.sbuf(st, "xt", [128, D], F32) for _ in range(2)]
                txt = [T(), T()]
                hn = [fw.sbuf(st, "hn", [128, D], BF16) for _ in range(2)]
                thn = [T(), T()]
                junk = fw.sbuf(st, "junk", [128, D], BF16)
                tjunk = T()
                sm = [[fw.sbuf(st, "sm", [128, 1], F32) for _ in range(3)] for _ in range(2)]
                tsm = [T(), T()]
                hnT = [fw.sbuf(st, "hnT", [128, 8, 512], BF16) for _ in range(2)]
                thnT = [T(), T()]
                fst = [fw.sbuf(st, "fst", [128, 16, 512], BF16) for _ in range(2)]
                tfst = [T(), T()]
                tst = [fw.sbuf(st, "tst", [128, 4, 512], BF16) for _ in range(2)]
                ttst = [T(), T()]
                gst_ = [fw.sbuf(st, "gst", [128, 4, 48], F32) for _ in range(2)]
                tgst = [T(), T()]
                fcols = [c * 128 for c in range(8)] + [1024, 1152, 1280, 1408, 1536, 1664, 2048, 2176]
                it = 0
                for R in range(8):
                    b = R % 2
                    for u in range(4):
                        ti = 4 * R + u
                        s2 = it % 2
                        it += 1
                        fw.dma(sp, xt[s2][:], x_in[ti * 128:(ti + 1) * 128, :], w=[txt[s2]])
                        rmsnorm_bf(xt[s2][:], txt[s2], hn[s2][:], thn[s2], junk[:], tjunk, [z[:] for z in sm[s2]], tsm[s2])
                        transpose_to(hn[s2], thn[s2], hnT[b], thnT[b], 8, ps[6 + s2], pst[6 + s2], u * 128)
                    for ci, c0 in enumerate(fcols):
                        pb = ci % 4
                        for k in range(8):
                            fw.op(pe, lambda e: e.matmul(out=ps[pb][:], lhsT=w_in[:, k, c0:c0 + 128], rhs=hnT[b][:, k, :],
                                                         start=(k == 0), stop=(k == 7)), r=[t_w, thnT[b]], w=[pst[pb]])
                        evac(fst[b][:, ci, :], ps[pb][:], r=[pst[pb]], w=[tfst[b]], scale=(0.125 if ci < 8 else None))
                    cs = slice(R * 512, (R + 1) * 512)
                    fw.dma(sp, QT.rearrange("(c p) t -> p c t", p=128)[:, :, cs], fst[b][:, 0:8, :], r=[tfst[b]], w=[tQT])
                    fw.dma(pool, KcT.rearrange("(c p) t -> p c t", p=128)[:, :, cs], fst[b][:, 8:10, :], r=[tfst[b]], w=[tKcT])
                    fw.dma(pool, VcT.rearrange("(c p) t -> p c t", p=128)[:, :, cs], fst[b][:, 10:12, :], r=[tfst[b]], w=[tVcT])
                    fw.dma(sp, KsT.rearrange("(c p) t -> p c t", p=128)[:, :, cs], fst[b][:, 12:14, :], r=[tfst[b]], w=[tKsT])
                    fw.dma(pool, KwT.rearrange("(c p) t -> p c t", p=128)[:, :, cs], fst[b][:, 14:16, :], r=[tfst[b]], w=[tKwT])
                    for u in range(4):
                        pb = 4 + (u % 2)
                        for (c0, cw, o0) in ((1792, 256, 0), (2304, 256, 256)):
                            for k in range(8):
                                fw.op(pe, lambda e: e.matmul(out=ps[pb][:, o0:o0 + cw], lhsT=hnT[b][:, k, u * 128:(u + 1) * 128],
                                                             rhs=w_in[:, k, c0:c0 + cw], start=(k == 0), stop=(k == 7)),
                                      r=[t_w, thnT[b]], w=[pst[pb]])
                        evac(tst[b][:, u, :], ps[pb][:], r=[pst[pb]], w=[ttst[b]])
                        for k in range(8):
                            fw.op(pe, lambda e: e.matmul(out=ps[pb][:, 0:48], lhsT=hnT[b][:, k, u * 128:(u + 1) * 128],
                                                         rhs=w_in[:, k, 2560:2608], start=(k == 0), stop=(k == 7)),
                                  r=[t_w, thnT[b]], w=[pst[pb]])
                        fw.op(dve, lambda e: e.tensor_tensor(out=gst_[b][:, u, :], in0=ps[pb][:, 0:48], in1=bg[:], op=ALU.add),
                              r=[pst[pb], t_bg], w=[tgst[b]])
                    fw.op(act, lambda e: e.activation(out=gst_[b][:], in_=gst_[b][:], func=AF.Sigmoid), r=[tgst[b]], w=[tgst[b]])
                    rs_ = slice(R * 512, (R + 1) * 512)
                    fw.dma(sp, Vs[rs_, :].rearrange("(u p) c -> p u c", p=128), tst[b][:, :, 0:256], r=[ttst[b]], w=[tVs])
                    fw.dma(pool, Vw[rs_, :].rearrange("(u p) c -> p u c", p=128), tst[b][:, :, 256:512], r=[ttst[b]], w=[tVw])
                    fw.dma(sp, G[rs_, :].rearrange("(u p) c -> p u c", p=128), gst_[b][:], r=[tgst[b]], w=[tG])

        fw.barrier()
        if stage >= 2:
            with contextlib.ExitStack() as st:
                kcT_all = fw.sbuf(st, "kcT", [128, 4, 256], BF16)
                t_kc = T()
                vc_all = fw.sbuf(st, "vc", [128, 4, 2, 64], BF16)
                t_vc = T()
                fw.op(pool, lambda e: e.memset(vc_all[:], 0.0), w=[t_vc])
                fw.op(pool, lambda e: e.memset(kcT_all[:], 0.0), w=[t_kc])
                with contextlib.ExitStack() as sb:
                    stgB = fw.sbuf(sb, "stgB", [64, 32, 256], F32)
                    t_stgB = T()
                    w1 = fw.sbuf(sb, "w1", [64, 32, 256], BF16)
                    t_w1 = T()
                    w2f = fw.sbuf(sb, "w2f", [128, 2, 64], F32)
                    w2 = fw.sbuf(sb, "w2", [128, 2, 64], BF16)
                    t_w2f, t_w2 = T(), T()
                    posf = fw.sbuf(sb, "posf", [64, 32], F32)
                    posT = fw.sbuf(sb, "posT", [64, 32], BF16)
                    t_posf, t_posT = T(), T()
                    b1t = fw.sbuf(sb, "b1t", [128, 2], F32)
                    c1b = fw.sbuf(sb, "c1b", [128, 2], F32)
                    b2col = fw.sbuf(sb, "b2col", [64, 1], F32)
                    b2row = fw.sbuf(sb, "b2row", [128, 64], F32)
                    t_b1, t_c1b, t_b2c, t_b2r = T(), T(), T(), T()
                    rawT = [fw.sbuf(sb, "rawT", [64, S], BF16) for _ in range(2)]
                    t_raw = [T(), T()]
                    hidT = fw.sbuf(sb, "hidT", [128, 2, 256], BF16)
                    t_hid = T()
                    for kv in range(2):
                        fw.dma(sp, stgB[:], nsa_cmp_w1[kv].rearrange("(l d) h -> d l h", d=64), w=[t_stgB])
                        for q4 in range(4):
                            E = (dve, pool, act, dve)[q4]
                            if E is act:
                                fw.op(E, lambda e: e.activation(out=w1[:, q4 * 8:(q4 + 1) * 8, :], in_=stgB[:, q4 * 8:(q4 + 1) * 8, :], func=AF.Copy),
                                      r=[t_stgB], w=[t_w1])
                            else:
                                fw.op(E, lambda e: e.tensor_copy(out=w1[:, q4 * 8:(q4 + 1) * 8, :], in_=stgB[:, q4 * 8:(q4 + 1) * 8, :]),
                                      r=[t_stgB], w=[t_w1])
                        fw.dma(sp, w2f[:], nsa_cmp_w2[kv].rearrange("(c p) d -> p c d", p=128), w=[t_w2f])
                        fw.op(dve, lambda e: e.tensor_copy(out=w2[:], in_=w2f[:]), r=[t_w2f], w=[t_w2])
                        fw.dma(sp, posf[:], nsa_cmp_pos[kv].rearrange("l d -> d l"), w=[t_posf], allow_slow_non_contiguous=True)
                        fw.op(dve, lambda e: e.tensor_copy(out=posT[:], in_=posf[:]), r=[t_posf], w=[t_posT])
                        fw.dma(sp, b1t[:], nsa_cmp_b1[kv].rearrange("(c p) -> p c", p=128), w=[t_b1], allow_slow_non_contiguous=True)
                        fw.dma(sp, b2col[:], nsa_cmp_b2[kv].rearrange("(d o) -> d o", o=1), w=[t_b2c], allow_slow_non_contiguous=True)
                        fw.dma(sp, b2row[:], nsa_cmp_b2[kv].partition_broadcast(128), w=[t_b2r])
                        for c in range(2):
                            for l in range(32):
                                fw.op(pe, lambda e: e.matmul(out=ps[0][:, c:c + 1], lhsT=w1[:, l, c * 128:(c + 1) * 128], rhs=posT[:, l:l + 1],
                                                             start=(l == 0), stop=(l == 31)), r=[t_w1, t_posT], w=[pst[0]])
                        fw.op(dve, lambda e: e.tensor_tensor(out=c1b[:], in0=ps[0][:, 0:2], in1=b1t[:], op=ALU.add), r=[pst[0], t_b1], w=[t_c1b])
                        for g in range(4):
                            rt, trt = rawT[g % 2], t_raw[g % 2]
                            src = (KcT, VcT)[kv]
                            fw.dma(sp, rt[:], src[g * 64:(g + 1) * 64, :], r=[(tKcT, tVcT)[kv]], w=[trt])
                            for c in range(2):
                                for l in range(32):
                                    fw.op(pe, lambda e: e.matmul(out=ps[1 + c][:, 0:255], lhsT=w1[:, l, c * 128:(c + 1) * 128],
                                                                 rhs=rt[:, l:l + 16 * 254 + 1:16], start=(l == 0), stop=(l == 31)),
                                          r=[t_w1, trt], w=[pst[1 + c]])
                                fw.op(act, lambda e: e.activation(out=hidT[:, c, 0:255], in_=ps[1 + c][:, 0:255], func=AF.Gelu_apprx_tanh,
                                                                  bias=c1b[:, c:c + 1]), r=[pst[1 + c], t_c1b], w=[t_hid])
                            if kv == 0:
                                for c in range(2):
                                    fw.op(pe, lambda e: e.matmul(out=ps[3][0:64, 0:255], lhsT=w2[:, c, :], rhs=hidT[:, c, 0:255],
                                                                 start=(c == 0), stop=(c == 1)), r=[t_w2, t_hid], w=[pst[3]])
                                fw.op(dve, lambda e: e.tensor_scalar(out=kcT_all[0:64, g, 0:255], in0=ps[3][0:64, 0:255], scalar1=b2col[:],
                                                                     scalar2=None, op0=ALU.add), r=[pst[3], t_b2c], w=[t_kc])
                            else:
                                for nch, (n0, nn) in enumerate(((0, 128), (128, 127))):
                                    for c in range(2):
                                        fw.op(pe, lambda e: e.matmul(out=ps[3][0:nn, nch * 64:(nch + 1) * 64], lhsT=hidT[:, c, n0:n0 + nn],
                                                                     rhs=w2[:, c, :], start=(c == 0), stop=(c == 1)), r=[t_w2, t_hid], w=[pst[3]])
                                    fw.op(dve, lambda e: e.tensor_tensor(out=vc_all[0:nn, g, nch, :], in0=ps[3][0:nn, nch * 64:(nch + 1) * 64],
                                                                         in1=b2row[0:nn, :], op=ALU.add), r=[pst[3], t_b2r], w=[t_vc])
                fw.barrier()
                if debug:
                    dbg_kc = nc.dram_tensor("dbg_kc", [128, 4, 256], BF16, kind="ExternalOutput").ap()
                    dbg_vc = nc.dram_tensor("dbg_vc", [128, 4, 2, 64], BF16, kind="ExternalOutput").ap()
                    fw.dma(sp, dbg_kc, kcT_all[:], r=[t_kc])
                    fw.dma(sp, dbg_vc, vc_all[:], r=[t_vc])

                if stage >= 3:
                    d0 = fw.sbuf(st, "d0", [128, 3, 128], F32)
                    distc = fw.sbuf(st, "distc", [128, 504], F32)
                    fbias = fw.sbuf(st, "fbias", [128, 126], F32)
                    t_cc = T()
                    fw.dma(sp, d0[:], c_d0.rearrange("k p t -> p k t"), w=[t_cc])
                    fw.dma(sp, distc[:], c_distc, w=[t_cc])
                    fw.dma(sp, fbias[:], c_fbias, w=[t_cc])
                    ovl = fw.sbuf(st, "ovl", [128, 2, 64], BF16)
                    fw.dma(sp, ovl[:], c_ovl.rearrange("k p j -> p k j"), w=[t_cc])
                    bcb = fw.sbuf(st, "bcb", [128, 4, 504], BF16)
                    ks_sel = fw.sbuf(st, "ks_sel", [128, S], BF16)
                    t_ks = T()
                    fw.dma(sp, ks_sel[64:128, :], c_e256, w=[t_ks])
                    kw = fw.sbuf(st, "kw", [128, S], BF16)
                    t_kw = T()
                    fw.op(pool, lambda e: e.memset(kw[64:128, :], 0.0), w=[t_kw])
                    vs_aug = fw.sbuf(st, "vs_aug", [128, 32, 65], BF16)
                    vw_aug = fw.sbuf(st, "vw_aug", [128, 32, 65], BF16)
                    t_vs, t_vw = T(), T()
                    fw.op(pool, lambda e: e.memset(vs_aug[:], 1.0), w=[t_vs])
                    fw.op(pool, lambda e: e.memset(vw_aug[:], 1.0), w=[t_vw])
                    qstack = fw.sbuf(st, "qstack", [128, 4, S], BF16)
                    t_q = T()
                    fw.op(pool, lambda e: e.memset(qstack[:], 0.0), w=[t_q])
                    t_qm = [T() for _ in range(32)]
                    A = fw.sbuf(st, "A", [128, 33, 512], BF16)
                    t_A = T()
                    bc = fw.sbuf(st, "bc", [128, 4, 504], F32)
                    t_bc = T()
                    gt = fw.sbuf(st, "gt", [128, 32, 12], F32)
                    t_gt = T()
                    sc = fw.sbuf(st, "sc", [128, 4, 256], F32)
                    pc = fw.sbuf(st, "pc", [128, 4, 256], F32)
                    t_sc, t_pc = T(), T()
                    rowsum = fw.sbuf(st, "rowsum", [128, 4], F32)
                    rinv = fw.sbuf(st, "rinv", [128, 4], F32)
                    t_rs, t_ri = T(), T()
                    pn = fw.sbuf(st, "pn", [128, 4, 256], BF16)
                    t_pn = T()
                    fw.op(pool, lambda e: e.memset(pn[:], 0.0), w=[t_pn])
                    psumh = fw.sbuf(st, "psumh", [128, 256], F32)
                    t_ph = T()
                    fw.op(pool, lambda e: e.memset(psumh[:], 0.0), w=[t_ph])
                    pnT = fw.sbuf(st, "pnT", [128, 8, 128], BF16)
                    t_pnT = T()
                    imp = fw.sbuf(st, "imp", [128, 64], F32)
                    score = fw.sbuf(st, "score", [128, 64], F32)
                    score2 = fw.sbuf(st, "score2", [128, 64], F32)
                    m8a = fw.sbuf(st, "m8a", [128, 8], F32)
                    m8b = fw.sbuf(st, "m8b", [128, 8], F32)
                    t_imp, t_score, t_score2, t_m8a, t_m8b = T(), T(), T(), T(), T()
                    msel = fw.sbuf(st, "msel", [128, 128], BF16)
                    t_msel = T()
                    fw.op(pool, lambda e: e.memset(msel[:], 0.0), w=[t_msel])
                    NPT = 3
                    pt = [fw.sbuf(st, "pt", [128, 512], BF16) for _ in range(NPT)]
                    pa = [fw.sbuf(st, "pa", [128, 512], BF16) for _ in range(NPT)]
                    t_pt = [T() for _ in range(NPT)]
                    t_pa = [T() for _ in range(NPT)]
                    osb = fw.sbuf(st, "osb", [65, 512], F32)
                    t_osb = T()
                    f4 = fw.sbuf(st, "f4", [128, 4], F32)
                    t_f4 = T()
                    yacc = fw.sbuf(st, "yacc", [128, 256], F32)
                    t_yacc = T()
                    ybf = [fw.sbuf(st, "ybf", [128, 256], BF16) for _ in range(2)]
                    t_ybf = [T(), T()]
                    pv6 = ps[6][:].bitcast(BF16)
                    pst7b = pst[7]
                    yacc2 = [yacc, fw.sbuf(st, "yacc2", [128, 256], F32)]
                    t_yacc2 = [t_yacc, T()]
                    osb2 = [osb, fw.sbuf(st, "osb2", [65, 512], F32)]
                    t_osb2 = [t_osb, T()]
                    f2 = [fw.sbuf(st, "f2", [128, 2], F32) for _ in range(2)]
                    t_f2 = [T(), T()]
                    NPT2 = 6
                    pt = pt + [fw.sbuf(st, "pt", [128, 512], BF16) for _ in range(3)]
                    pa = pa + [fw.sbuf(st, "pa", [128, 512], BF16) for _ in range(3)]
                    t_pt = t_pt + [T(), T(), T()]
                    t_pa = t_pa + [T(), T(), T()]
                    kctr = [0]
                    import os as _os
                    _NG = int(_os.environ.get('NSA_G', '4')); _NA = int(_os.environ.get('NSA_A', '32'))

                    CS = []
                    for ci in range(4):
                        if ci == 0:
                            tiles = [sc, pc, pn, psumh, pnT, rowsum, rinv, imp, score, score2, m8a, m8b, msel]
                            trk = [t_sc, t_pc, t_pn, t_ph, t_pnT, t_rs, t_ri, t_imp, t_score, t_score2, t_m8a, t_m8b, t_msel]
                        else:
                            tiles = [fw.sbuf(st, "sc", [128, 4], F32), fw.sbuf(st, "pc", [128, 4], F32),
                                     fw.sbuf(st, "pn", [128, 4, 256], BF16), fw.sbuf(st, "psumh", [128, 4], F32),
                                     fw.sbuf(st, "pnT", [128, 8, 128], BF16), fw.sbuf(st, "rowsum", [128, 4], F32),
                                     fw.sbuf(st, "rinv", [128, 4], F32), fw.sbuf(st, "imp", [128, 64], F32),
                                     fw.sbuf(st, "score", [128, 64], F32), fw.sbuf(st, "score2", [128, 64], F32),
                                     fw.sbuf(st, "m8a", [128, 8], F32), fw.sbuf(st, "m8b", [128, 8], F32),
                                     fw.sbuf(st, "msel", [128, 128], BF16)]
                            trk = [T() for _ in range(13)]
                            fw.op(pool, lambda e: e.memset(tiles[2][:], 0.0), w=[trk[2]])
                            fw.op(pool, lambda e: e.memset(tiles[12][:], 0.0), w=[trk[12]])
                        tiles = tiles + [fw.sbuf(st, "gf", [128, 4], F32), fw.sbuf(st, "imp4", [128, 4, 64], F32)]
                        trk = trk + [T(), T()]
                        CS.append(tuple(tiles + trk))
                    ocbuf = fw.sbuf(st, "ocbuf", [128, 32, 256], BF16)
                    t_oc = [T() for _ in range(32)]

                    def cmp_thunks(g, a, ci):
                        th = []
                        (sc_, pc_, pcb, psumh_, pnT, rowsum, rinv, imp, score, score2, m8a, m8b, msel, gf, imp4,
                         t_sc_, t_pc_, t_pcb, t_ph_, t_pnT, t_rs, t_ri, t_imp, t_score, t_score2, t_m8a, t_m8b, t_msel, t_gf, t_imp4) = CS[ci]
                        bS, bT = 2 * ci, 2 * ci + 1
                        pvT = ps[bT][:].bitcast(BF16)
                        ta = slice(a * 128, (a + 1) * 128)
                        boff = 248 - 8 * a
                        for hp in range(2):
                            def f_mm(hp=hp):
                                for hh in range(2):
                                    h = 2 * hp + hh
                                    fw.op(pe, lambda e: e.matmul(out=ps[bS][:, hh * 256:hh * 256 + 255], lhsT=qstack[:, h, ta],
                                                                 rhs=kcT_all[:, g, 0:255], start=True, stop=False), r=[t_q, t_qm[a], t_kc], w=[pst[bS]])
                                    fw.op(pe, lambda e: e.matmul(out=ps[bS][:, hh * 256:hh * 256 + 255], lhsT=ident_bf[:],
                                                                 rhs=bcb[:, h, boff:boff + 255], start=False, stop=True), r=[t_bc, t_ident], w=[pst[bS]])
                            th.append(f_mm)
                            for hh in range(2):
                                def f_exp(hp=hp, hh=hh):
                                    h = 2 * hp + hh
                                    fw.op(act, lambda e: e.activation(out=pcb[:, h, 0:255], in_=ps[bS][:, hh * 256:hh * 256 + 255], func=AF.Exp,
                                                                      accum_out=rowsum[:, h:h + 1]), r=[pst[bS]], w=[t_pcb, t_rs])
                                th.append(f_exp)

                        def f_rinv():
                            fw.op(dve, lambda e: e.tensor_scalar(out=rinv[:], in0=rowsum[:], scalar1=1e-30, scalar2=None, op0=ALU.max),
                                  r=[t_rs], w=[t_ri])
                            fw.op(dve, lambda e: e.reciprocal(out=rinv[:], in_=rinv[:]), r=[t_ri], w=[t_ri])
                        th.append(f_rinv)

                        def f_pnT():
                            for h in range(4):
                                for nch in range(2):
                                    fw.op(pe, lambda e: e.transpose(out=pvT[:, (h * 2 + nch) * 128:(h * 2 + nch + 1) * 128],
                                                                    in_=pcb[:, h, nch * 128:(nch + 1) * 128], identity=ident_bf[:]),
                                          r=[t_pcb, t_ident], w=[pst[bT]])
                            fw.op(dve, lambda e: e.tensor_copy(out=pnT[:], in_=pvT.rearrange("p (k t) -> p k t", k=8)),
                                  r=[pst[bT]], w=[t_pnT])
                        th.append(f_pnT)
                        th.append(lambda: fw.op(dve, lambda e: e.tensor_tensor(out=gf[:], in0=rinv[:], in1=gt[:, a, 0:12:3], op=ALU.mult),
                                                r=[t_ri, t_gt], w=[t_gf]))

                        def f_oc():
                            for h in range(4):
                                for nch in range(2):
                                    fw.op(pe, lambda e: e.matmul(out=ps[bS][:, h * 64:(h + 1) * 64], lhsT=pnT[:, h * 2 + nch, :],
                                                                 rhs=vc_all[:, g, nch, :], start=(nch == 0), stop=(nch == 1)),
                                          r=[t_pnT, t_vc], w=[pst[bS]])
                                for nch in range(2):
                                    fw.op(pe, lambda e: e.matmul(out=ps[bS][:, 256 + h * 64:256 + (h + 1) * 64], lhsT=pnT[:, h * 2 + nch, :],
                                                                 rhs=ovl[:, nch, :], start=(nch == 0), stop=(nch == 1)),
                                          r=[t_pnT, t_cc], w=[pst[bS]])
                        th.append(f_oc)
                        th.append(lambda: fw.op(dve, lambda e: e.tensor_tensor(out=ocbuf[:, a, :].rearrange("p (h c) -> p h c", h=4),
                                                                               in0=ps[bS][:, 0:256].rearrange("p (h c) -> p h c", h=4),
                                                                               in1=gf[:].unsqueeze(2).to_broadcast([128, 4, 64]), op=ALU.mult),
                                                r=[pst[bS], t_gf], w=[t_oc[a]]))
                        th.append(lambda: fw.op(dve, lambda e: e.tensor_tensor(out=imp4[:], in0=ps[bS][:, 256:512].rearrange("p (h c) -> p h c", h=4),
                                                                               in1=rinv[:].unsqueeze(2).to_broadcast([128, 4, 64]), op=ALU.mult),
                                                r=[pst[bS], t_ri], w=[t_imp4]))

                        def f_imp():
                            fw.op(dve, lambda e: e.tensor_reduce(out=imp[:], in_=imp4[:].rearrange("p h j -> p j h"), axis=AX.X, op=ALU.add),
                                  r=[t_imp4], w=[t_imp])
                            fw.op(dve, lambda e: e.tensor_tensor(out=score[:], in0=imp[:], in1=fbias[:, 62 - 2 * a:62 - 2 * a + 64], op=ALU.add),
                                  r=[t_imp, t_cc], w=[t_score])
                            fw.op(dve, lambda e: e.memset(score[:, 0:1], 100.0), w=[t_score])
                        th.append(f_imp)

                        def f_topk():
                            fw.op(dve, lambda e: e.max(out=m8a[:], in_=score[:]), r=[t_score], w=[t_m8a])
                            fw.op(dve, lambda e: e.match_replace(out=score2[:], in_to_replace=m8a[:], in_values=score[:], imm_value=-1.0e9),
                                  r=[t_score, t_m8a], w=[t_score2])
                            fw.op(dve, lambda e: e.max(out=m8b[:], in_=score2[:]), r=[t_score2], w=[t_m8b])
                            fw.op(dve, lambda e: e.tensor_scalar(out=msel[:, 64:128], in0=score[:], scalar1=m8b[:, 7:8], scalar2=-1.0,
                                                                 op0=ALU.is_ge, op1=ALU.add), r=[t_score, t_m8b], w=[t_msel])
                        th.append(f_topk)

                        def f_mask():
                            fw.op(pe, lambda e: e.transpose(out=pvT[:, 0:128], in_=msel[:], identity=ident_bf[:]), r=[t_msel, t_ident], w=[pst[bT]])
                            for h in range(4):
                                if h % 2 == 0:
                                    fw.op(act, lambda e: e.activation(out=qstack[64:128, h, ta], in_=pvT[64:128, 0:128], func=AF.Copy),
                                          r=[pst[bT]], w=[t_qm[a]])
                                else:
                                    fw.op(dve, lambda e: e.tensor_copy(out=qstack[64:128, h, ta], in_=pvT[64:128, 0:128]), r=[pst[bT]], w=[t_qm[a]])
                        th.append(f_mask)
                        return th

                    def obank(a, kind):
                        return ((3, 3), (4, 4))[kind][a % 2]

                    f4k = [fw.sbuf(st, "f4k", [128, 4], F32) for _ in range(2)]
                    t_f4k = [T(), T()]
                    otmp = [fw.sbuf(st, "otmp", [128, 4, 64], F32) for _ in range(2)]
                    t_otmp = [T(), T()]

                    def finish_thunks(g, a, kind):
                        ya, tya = yacc2[a % 2], t_yacc2[a % 2]
                        ob, tob = osb2[kind], t_osb2[kind]
                        bank = obank(a, kind)
                        gcol = 1 + kind
                        ff, tff = f4k[kind], t_f4k[kind]
                        tmp, ttmp = otmp[kind], t_otmp[kind]
                        ta = slice(a * 128, (a + 1) * 128)
                        th = []
                        th.append(lambda: fw.op(dve, lambda e: e.tensor_copy(out=ob[:], in_=ps[bank][0:65, :]), r=[pst[bank]], w=[tob]))

                        def f_tr():
                            for h in range(4):
                                fw.op(pe, lambda e: e.transpose(out=ps[7][:, h * 65:(h + 1) * 65], in_=ob[:, h * 128:(h + 1) * 128],
                                                                identity=ident_f[0:65, 0:65]), r=[tob, t_ident], w=[pst[7]])
                        th.append(f_tr)
                        th.append(lambda: fw.op(dve, lambda e: e.tensor_scalar(out=ff[:], in0=ps[7][:, 64:260:65], scalar1=1e-30, scalar2=None, op0=ALU.max),
                                                r=[pst[7]], w=[tff]))
                        th.append(lambda: fw.op(dve, lambda e: e.reciprocal(out=ff[:], in_=ff[:]), r=[tff], w=[tff]))
                        th.append(lambda: fw.op(dve, lambda e: e.tensor_tensor(out=ff[:], in0=ff[:], in1=gt[:, a, gcol:12:3], op=ALU.mult),
                                                r=[tff, t_gt], w=[tff]))
                        th.append(lambda: fw.op(dve, lambda e: e.tensor_tensor(out=tmp[:], in0=ps[7][:, 0:260].rearrange("p (h c) -> p h c", c=65)[:, :, 0:64],
                                                                               in1=ff[:].unsqueeze(2).to_broadcast([128, 4, 64]), op=ALU.mult),
                                                r=[pst[7], tff], w=[ttmp]))
                        if kind == 0:
                            th.append(lambda: fw.op(pool, lambda e: e.tensor_tensor(out=ya[:], in0=tmp[:].rearrange("p h c -> p (h c)"), in1=ocbuf[:, a, :], op=ALU.add),
                                                    r=[ttmp, t_oc[a]], w=[tya]))
                        else:
                            th.append(lambda: fw.op(pool, lambda e: e.tensor_tensor(out=ya[:], in0=tmp[:].rearrange("p h c -> p (h c)"), in1=ya[:], op=ALU.add),
                                                    r=[ttmp, tya], w=[tya]))

                            def f_out():
                                yb, tyb = ybf[a % 2], t_ybf[a % 2]
                                fw.op(pool, lambda e: e.tensor_copy(out=yb[:], in_=ya[:]), r=[tya], w=[tyb])
                                fw.dma(sp, Y[ta, g * 256:(g + 1) * 256], yb[:], r=[tyb], w=[tY])
                            th.append(f_out)
                        return th

                    for g in range(_NG):
                        for h in range(4):
                            fw.dma(sp, qstack[0:64, h, :], QT[(4 * g + h) * 64:(4 * g + h + 1) * 64, :], r=[tQT], w=[t_q])
                        fw.dma(sp, ks_sel[0:64, :], KsT[g * 64:(g + 1) * 64, :], r=[tKsT], w=[t_ks])
                        fw.dma(sp, kw[0:64, :], KwT[g * 64:(g + 1) * 64, :], r=[tKwT], w=[t_kw])
                        fw.dma(pool, vs_aug[:, :, 0:64], Vs[:, g * 64:(g + 1) * 64].rearrange("(a p) d -> p a d", p=128), r=[tVs], w=[t_vs])
                        fw.dma(pool, vw_aug[:, :, 0:64], Vw[:, g * 64:(g + 1) * 64].rearrange("(a p) d -> p a d", p=128), r=[tVw], w=[t_vw])
                        fw.dma(sp, gt[:], G[:, g * 12:(g + 1) * 12].rearrange("(a p) c -> p a c", p=128), r=[tG], w=[t_gt])
                        for h in range(4):
                            sl = SLOPES[4 * g + h]
                            for dl in range(33):
                                src = d0[:, 0, :] if dl == 0 else (d0[:, 2, :] if dl == 32 else d0[:, 1, :])
                                off = 512.0 if dl == 32 else 128.0 * dl
                                fw.op(act, lambda e: e.activation(out=A[:, dl, h * 128:(h + 1) * 128], in_=src, func=AF.Exp,
                                                                  scale=-sl, bias=-sl * off), r=[t_cc], w=[t_A])
                            fw.op(dve, lambda e: e.tensor_scalar(out=bcb[:, h, :], in0=distc[:], scalar1=-sl, scalar2=None, op0=ALU.mult),
                                  r=[t_cc], w=[t_bc])
                        for a0 in range(0, _NA, 4):
                            chains = [cmp_thunks(g, a0 + ci, ci) for ci in range(4) if a0 + ci < _NA]
                            while chains:
                                for c_ in chains:
                                    c_.pop(0)()
                                chains = [c_ for c_ in chains if c_]
                        LA = 4
                        SB = (0, 1, 2, 6, 5)
                        pend = []
                        for a in range(_NA):
                            ta = slice(a * 128, (a + 1) * 128)
                            steps = [(0, b) for b in range(a + 1)] + [(1, b) for b in range(max(0, a - 4), a + 1)]
                            n = len(steps)
                            per = -(-len(pend) // max(1, n - 1))
                            slots = {}
                            newp = []
                            for i in range(n + LA):
                                if i < n:
                                    kind, b = steps[i]
                                    sbk = SB[kctr[0] % 5]
                                    k3 = kctr[0] % NPT2
                                    kctr[0] += 1
                                    slots[i] = k3
                                    krows = 128
                                    kT, tk = (ks_sel, t_ks) if kind == 0 else (kw, t_kw)
                                    dl = a - b
                                    ai = dl if (kind == 0 or dl < 4) else 32
                                    fw.op(pe, lambda e: e.matmul(out=ps[sbk][:], lhsT=kT[0:krows, b * 128:(b + 1) * 128],
                                                                 rhs=qstack[0:krows, :, ta], start=True, stop=True),
                                          r=[tk, t_q, t_qm[a]], w=[pst[sbk]])
                                    fw.op(act, lambda e: e.activation(out=pt[k3][:], in_=ps[sbk][:], func=AF.Exp), r=[pst[sbk]], w=[t_pt[k3]])
                                    fw.op(dve, lambda e: e.tensor_tensor(out=pa[k3][:], in0=pt[k3][:], in1=A[:, ai, :], op=ALU.mult),
                                          r=[t_pt[k3], t_A], w=[t_pa[k3]])
                                    if i >= 1:
                                        for _ in range(per):
                                            if pend:
                                                pend.pop(0)()
                                if i >= LA:
                                    j = i - LA
                                    kind, b = steps[j]
                                    k3 = slots[j]
                                    vaug, tv = (vs_aug, t_vs) if kind == 0 else (vw_aug, t_vw)
                                    first = (j == 0) or (steps[j - 1][0] != kind)
                                    last = (j == n - 1) or (steps[j + 1][0] != kind)
                                    bank = obank(a, kind)
                                    fw.op(pe, lambda e: e.matmul(out=ps[bank][0:65, :], lhsT=vaug[:, b, :], rhs=pa[k3][:],
                                                                 start=first, stop=last), r=[tv, t_pa[k3]], w=[pst[bank]])
                                    if last:
                                        newp += finish_thunks(g, a, kind)
                            while pend:
                                pend.pop(0)()
                            pend = newp
                        while pend:
                            pend.pop(0)()

        fw.barrier()
        tXB = [T() for _ in range(16)]

        def phase_ffn(layer, outproj, final):
            with contextlib.ExitStack() as st:
                wi = fw.sbuf(st, "wi", [128, 8, 2 * D_FF], BF16)
                wo2 = fw.sbuf(st, "wo2", [128, 22, D], BF16)
                t_wi, t_wo2 = T(), T()
                gf, tgf = load_gain(st, norm_ffn[layer], "gf")
                stg = [fw.sbuf(st, "stgF", [128, 704], F32) for _ in range(4)]
                tstg = [T() for _ in range(4)]
                if outproj:
                    wo = fw.sbuf(st, "wo", [128, 8, D], BF16)
                    t_wo = T()
                    for k in range(8):
                        load_cast(st, wo[:, k, :], t_wo, nsa_w_out[k * 128:(k + 1) * 128, :], 128, D, stg=stg, tstg=tstg)
                for k in range(8):
                    load_cast(st, wi[:, k, :], t_wi, ffn_w_in[layer, k * 128:(k + 1) * 128, :], 128, 2 * D_FF,
                              gain=gf[:, k:k + 1], tgain=tgf, stg=stg, tstg=tstg)
                for f in range(22):
                    load_cast(st, wo2[:, f, :], t_wo2, ffn_w_out[layer, f * 128:(f + 1) * 128, :], 128, D, stg=stg, tstg=tstg)
                if final:
                    gfin = fw.sbuf(st, "gfin", [128, D], F32)
                    t_gfin = T()
                    fw.dma(sp, gfin[:], norm_final.partition_broadcast(128), w=[t_gfin])
                xb = [fw.sbuf(st, "xb", [128, 2, D], F32) for _ in range(2)]
                t_xb = [T(), T()]
                if outproj:
                    ybt = [fw.sbuf(st, "ybt", [128, 2, D], BF16)] * 2
                    t_ybt = [T()] * 2
                    yT = fw.sbuf(st, "yT", [128, 8, 256], BF16)
                    t_yT = T()
                hn = [fw.sbuf(st, "hnF", [128, D], BF16) for _ in range(2)]
                thn = [T(), T()]
                junk = fw.sbuf(st, "junkF", [128, D], BF16)
                tjunk = T()
                sm = [[fw.sbuf(st, "smF", [128, 1], F32) for _ in range(3)] for _ in range(2)]
                tsm = [T(), T()]
                hnT = fw.sbuf(st, "hnTF", [128, 8, 256], BF16)
                thnT = T()
                actT = fw.sbuf(st, "actT", [128, 22, 256], BF16)
                t_actT = T()
                sg = [fw.sbuf(st, "sg", [128, 256], F32) for _ in range(2)]
                t_sg = [T(), T()]
                if final:
                    ob = [fw.sbuf(st, "ob", [128, D], F32) for _ in range(2)]
                    t_ob = [T(), T()]
                tout = T()
                for blk in range(16):
                    s2 = blk % 2
                    rows = slice(blk * 256, (blk + 1) * 256)
                    X, tX = xb[s2], t_xb[s2]
                    if outproj:
                        fw.dma(sp, X[:], x_in[rows, :].rearrange("(u p) d -> p u d", p=128), w=[tX])
                        fw.dma(pool, ybt[s2][:], Y[rows, :].rearrange("(u p) d -> p u d", p=128), r=[tY], w=[t_ybt[s2]])
                        for u in range(2):
                            transpose_to(ybt[s2][:, u, :], t_ybt[s2], yT, t_yT, 8, ps[6 + u], pst[6 + u], u * 128)
                        for u in range(2):
                            for nh in range(2):
                                pb = 4 + nh
                                for k in range(8):
                                    fw.op(pe, lambda e: e.matmul(out=ps[pb][:], lhsT=yT[:, k, u * 128:(u + 1) * 128],
                                                                 rhs=wo[:, k, nh * 512:(nh + 1) * 512], start=(k == 0), stop=(k == 7)),
                                          r=[t_yT, t_wo], w=[pst[pb]])
                                fw.op(dve, lambda e: e.tensor_tensor(out=X[:, u, nh * 512:(nh + 1) * 512], in0=X[:, u, nh * 512:(nh + 1) * 512],
                                                                     in1=ps[pb][:], op=ALU.add), r=[pst[pb], tX], w=[tX])
                    else:
                        fw.dma(sp, X[:], X1[rows, :].rearrange("(u p) d -> p u d", p=128), r=[tXB[blk]], w=[tX])
                    for u in range(2):
                        rmsnorm_bf(X[:, u, :], tX, hn[u][:], thn[u], junk[:], tjunk, [z[:] for z in sm[u]], tsm[u])
                        transpose_to(hn[u], thn[u], hnT, thnT, 8, ps[6 + u], pst[6 + u], u * 128)
                    for f in range(22):
                        pb = f % 4
                        for half in range(2):
                            c0 = half * D_FF + f * 128
                            for k in range(8):
                                fw.op(pe, lambda e: e.matmul(out=ps[pb][:, half * 256:(half + 1) * 256], lhsT=wi[:, k, c0:c0 + 128],
                                                             rhs=hnT[:, k, :], start=(k == 0), stop=(k == 7)), r=[t_wi, thnT], w=[pst[pb]])
                        fw.op(act, lambda e: e.activation(out=sg[f % 2][:], in_=ps[pb][:, 0:256], func=AF.Silu), r=[pst[pb]], w=[t_sg[f % 2]])
                        fw.op(dve, lambda e: e.tensor_tensor(out=actT[:, f, :], in0=sg[f % 2][:], in1=ps[pb][:, 256:512], op=ALU.mult),
                              r=[pst[pb], t_sg[f % 2]], w=[t_actT])
                    for u in range(2):
                        for nh in range(2):
                            pb = 4 + nh
                            for f in range(22):
                                fw.op(pe, lambda e: e.matmul(out=ps[pb][:], lhsT=actT[:, f, u * 128:(u + 1) * 128],
                                                             rhs=wo2[:, f, nh * 512:(nh + 1) * 512], start=(f == 0), stop=(f == 21)),
                                      r=[t_actT, t_wo2], w=[pst[pb]])
                            fw.op(dve, lambda e: e.tensor_tensor(out=X[:, u, nh * 512:(nh + 1) * 512], in0=X[:, u, nh * 512:(nh + 1) * 512],
                                                                 in1=ps[pb][:], op=ALU.add), r=[pst[pb], tX], w=[tX])
                    if final:
                        for u in range(2):
                            ss, sd, rs = [z[:] for z in sm[u]]
                            fw.op(act, lambda e: e.activation(out=junk[:], in_=X[:, u, :], func=AF.Square, accum_out=ss), r=[tX], w=[tjunk, tsm[u]])
                            fw.op(act, lambda e: e.activation(out=sd, in_=ss, func=AF.Sqrt, bias=epsc[:], scale=1.0 / D), r=[tsm[u], t_eps], w=[tsm[u]])
                            fw.op(dve, lambda e: e.reciprocal(out=rs, in_=sd), r=[tsm[u]], w=[tsm[u]])
                            fw.op(dve, lambda e: e.scalar_tensor_tensor(out=ob[u][:], in0=X[:, u, :], scalar=rs, in1=gfin[:], op0=ALU.mult, op1=ALU.mult),
                                  r=[tX, tsm[u], t_gfin], w=[t_ob[u]])
                            fw.dma(sp, out_ap[blk * 256 + u * 128:blk * 256 + (u + 1) * 128, :], ob[u][:], r=[t_ob[u]], w=[tout])
                    else:
                        fw.dma(sp, X1[rows, :].rearrange("(u p) d -> p u d", p=128), X[:], r=[tX], w=[tXB[blk]])
                fw.barrier()

        if stage >= 4:
            phase_ffn(0, True, False)

        if stage >= 5:
            with contextlib.ExitStack() as st:
                NB = 512
                NU = NB // 128
                wl = fw.sbuf(st, "wl", [128, 8, 2 * D_RNN], BF16)
                t_wl = T()
                gl, tgl = load_gain(st, norm_mix[1], "gl")
                stg = [fw.sbuf(st, "stgL", [128, 704], F32) for _ in range(4)]
                tstg = [T() for _ in range(4)]
                for k in range(8):
                    load_cast(st, wl[:, k, :], t_wl, lru_w_in[k * 128:(k + 1) * 128, :], 128, 2 * D_RNN,
                              gain=gl[:, k:k + 1], tgain=tgl, stg=stg, tstg=tstg)
                wa = fw.sbuf(st, "wa", [88, 16, 176], BF16)
                wx = fw.sbuf(st, "wx", [88, 16, 176], BF16)
                wlo = fw.sbuf(st, "wlo", [88, 16, D], BF16)
                t_wa, t_wx, t_wlo = T(), T(), T()
                for (dst, tdst, src) in ((wa, t_wa, lru_w_a), (wx, t_wx, lru_w_x)):
                    for n in range(8):
                        for ih in range(2):
                            load_cast(st, dst[:, 2 * n + ih, :], tdst, src[n, ih * 88:(ih + 1) * 88, :], 88, 176, stg=stg, tstg=tstg)
                for c in range(16):
                    load_cast(st, wlo[:, c, :], t_wlo, lru_w_out[c * 88:(c + 1) * 88, :], 88, D, stg=stg, tstg=tstg)
                cw = fw.sbuf(st, "cw", [88, 16, 4], F32)
                cb = fw.sbuf(st, "cb", [88, 16], F32)
                nba = fw.sbuf(st, "nba", [88, 16], F32)
                nbx = fw.sbuf(st, "nbx", [88, 16], F32)
                nsp = fw.sbuf(st, "nsp", [88, 16], F32)
                t_small = T()
                for j in range(4):
                    fw.dma(sp, cw[:, :, j], lru_conv_w[j].rearrange("(c p) -> p c", p=88), w=[t_small], allow_slow_non_contiguous=True)
                fw.dma(sp, cb[:], lru_conv_b.rearrange("(c p) -> p c", p=88), w=[t_small], allow_slow_non_contiguous=True)
                fw.dma(sp, nba[:], lru_b_a.rearrange("(c p) -> p c", p=88), w=[t_small], allow_slow_non_contiguous=True)
                fw.dma(sp, nbx[:], lru_b_x.rearrange("(c p) -> p c", p=88), w=[t_small], allow_slow_non_contiguous=True)
                fw.dma(sp, nsp[:], lru_lambda.rearrange("(c p) -> p c", p=88), w=[t_small], allow_slow_non_contiguous=True)
                fw.op(dve, lambda e: e.tensor_scalar(out=nba[:], in0=nba[:], scalar1=-1.0, scalar2=None, op0=ALU.mult), r=[t_small], w=[t_small])
                fw.op(dve, lambda e: e.tensor_scalar(out=nbx[:], in0=nbx[:], scalar1=-1.0, scalar2=None, op0=ALU.mult), r=[t_small], w=[t_small])
                fw.op(act, lambda e: e.activation(out=nsp[:], in_=nsp[:], func=AF.Exp, scale=-1.0), r=[t_small], w=[t_small])
                fw.op(act, lambda e: e.activation(out=nsp[:], in_=nsp[:], func=AF.Ln, bias=1.0), r=[t_small], w=[t_small])
                fw.op(dve, lambda e: e.tensor_scalar(out=nsp[:], in0=nsp[:], scalar1=-8.0, scalar2=None, op0=ALU.mult), r=[t_small], w=[t_small])
                halo = fw.sbuf(st, "halo", [88, 16, 3], F32)
                hlast = fw.sbuf(st, "hlast", [88, 16], F32)
                t_halo, t_hl = T(), T()
                fw.op(pool, lambda e: e.memset(halo[:], 0.0), w=[t_halo])
                fw.op(pool, lambda e: e.memset(hlast[:], 0.0), w=[t_hl])
                X = fw.sbuf(st, "xbL", [128, NU, D], F32)
                tX = T()
                hn = [fw.sbuf(st, "hnL", [128, D], BF16) for _ in range(2)]
                thn = [T(), T()]
                junk = fw.sbuf(st, "junkL", [128, D], BF16)
                tjunk = T()
                sm = [[fw.sbuf(st, "smL", [128, 1], F32) for _ in range(3)] for _ in range(2)]
                tsm = [T(), T()]
                hnT = fw.sbuf(st, "hnTL", [128, 8, NB], BF16)
                thnT = T()
                gate16 = fw.sbuf(st, "gate16", [88, 16, NB], BF16)
                t_g16 = T()
                hg = fw.sbuf(st, "hg", [88, 16, NB], BF16)
                t_hg = T()

                def mk(name, dt=F32, w=NB, nbuf=1):
                    return [fw.sbuf(st, name, [88, 2, w], dt) for _ in range(nbuf)], [[T(), T()] for _ in range(nbuf)]
                recb, t_recb = mk("recb", F32, NB + 3, 2)
                xr, t_xr = mk("xr", F32, NB, 2)
                xrb, t_xrb = mk("xrb", BF16, NB, 2)
                rr_, t_rr = mk("rr")
                ii_, t_ii = mk("ii")
                aa, t_aa = mk("aa")
                uu, t_uu = mk("uu")
                hh, t_hh = mk("hh")
                rr_, ii_, aa, uu, hh = rr_[0], ii_[0], aa[0], uu[0], hh[0]
                t_rr, t_ii, t_aa, t_uu, t_hh = t_rr[0], t_ii[0], t_aa[0], t_uu[0], t_hh[0]

                def conv_chain(n, half):
                    q = n % 2
                    c = 2 * n + half
                    pb = half
                    RB, XR, XB = recb[q], xr[q], xrb[q]
                    tRB, tXR, tXB_ = t_recb[q][half], t_xr[q][half], t_xrb[q][half]
                    th = []

                    def f0():
                        for k in range(8):
                            fw.op(pe, lambda e: e.matmul(out=ps[pb][0:88, :], lhsT=wl[:, k, D_RNN + c * 88:D_RNN + (c + 1) * 88], rhs=hnT[:, k, :],
                                                         start=(k == 0), stop=(k == 7)), r=[t_wl, thnT], w=[pst[pb]])
                        fw.op(pool, lambda e: e.tensor_copy(out=RB[:, half, 0:3], in_=halo[:, c, :]), r=[t_halo], w=[tRB])
                    th.append(f0)

                    def f1():
                        fw.op(act, lambda e: e.activation(out=RB[:, half, 3:3 + NB], in_=ps[pb][0:88, :], func=AF.Copy), r=[pst[pb]], w=[tRB])
                        fw.op(pool, lambda e: e.tensor_copy(out=halo[:, c, :], in_=RB[:, half, NB:NB + 3]), r=[tRB], w=[t_halo])
                    th.append(f1)
                    th.append(lambda: fw.op(dve, lambda e: e.tensor_scalar(out=XR[:, half, :], in0=RB[:, half, 0:NB], scalar1=cw[:, c, 0:1],
                                                                           scalar2=cb[:, c:c + 1], op0=ALU.mult, op1=ALU.add),
                                            r=[tRB, t_small], w=[tXR]))
                    for j in range(1, 4):
                        th.append(lambda j=j: fw.op(dve, lambda e: e.scalar_tensor_tensor(out=XR[:, half, :], in0=RB[:, half, j:j + NB], scalar=cw[:, c, j:j + 1],
                                                                                           in1=XR[:, half, :], op0=ALU.mult, op1=ALU.add),
                                                    r=[tRB, t_small, tXR], w=[tXR]))
                    th.append(lambda: fw.op(pool, lambda e: e.tensor_copy(out=XB[:, half, :], in_=XR[:, half, :]), r=[tXR], w=[tXB_]))
                    return th

                def gate_chain(n, oh):
                    q = n % 2
                    XR, XB = xr[q], xrb[q]
                    c = 2 * n + oh
                    pr, pi = 2 + 2 * oh, 3 + 2 * oh
                    R_, I_, A_, U_, H_ = rr_[:, oh, :], ii_[:, oh, :], aa[:, oh, :], uu[:, oh, :], hh[:, oh, :]
                    tr, ti, ta_, tu, th_ = [t_rr[oh]], [t_ii[oh]], [t_aa[oh]], [t_uu[oh]], [t_hh[oh]]
                    th = []

                    def f0():
                        for (wgt, twg, pbG) in ((wa, t_wa, pr), (wx, t_wx, pi)):
                            for ih in range(2):
                                fw.op(pe, lambda e: e.matmul(out=ps[pbG][0:88, :], lhsT=wgt[:, 2 * n + ih, oh * 88:(oh + 1) * 88],
                                                             rhs=XB[:, ih, :], start=(ih == 0), stop=(ih == 1)),
                                      r=[twg, t_xrb[q][0], t_xrb[q][1]], w=[pst[pbG]])
                    th.append(f0)
                    th.append(lambda: fw.op(act, lambda e: e.activation(out=R_, in_=ps[pr][0:88, :], func=AF.Exp, bias=nba[:, c:c + 1], scale=-1.0),
                                            r=[pst[pr], t_small], w=tr))
                    th.append(lambda: fw.op(act, lambda e: e.activation(out=I_, in_=ps[pi][0:88, :], func=AF.Exp, bias=nbx[:, c:c + 1], scale=-1.0),
                                            r=[pst[pi], t_small], w=ti))
                    th.append(lambda: fw.op(act, lambda e: e.activation(out=R_, in_=R_, func=AF.Ln, bias=1.0), r=tr, w=tr))
                    th.append(lambda: fw.op(act, lambda e: e.activation(out=I_, in_=I_, func=AF.Ln, bias=1.0), r=ti, w=ti))
                    th.append(lambda: fw.op(act, lambda e: e.activation(out=R_, in_=R_, func=AF.Exp, scale=-1.0), r=tr, w=tr))
                    th.append(lambda: fw.op(act, lambda e: e.activation(out=I_, in_=I_, func=AF.Exp, scale=-1.0), r=ti, w=ti))
                    th.append(lambda: fw.op(act, lambda e: e.activation(out=A_, in_=R_, func=AF.Exp, scale=nsp[:, c:c + 1]), r=tr + [t_small], w=ta_))
                    th.append(lambda: fw.op(pool, lambda e: e.tensor_tensor(out=I_, in0=I_, in1=XR[:, oh, :], op=ALU.mult), r=ti + [t_xr[q][oh]], w=ti))
                    th.append(lambda: fw.op(dve, lambda e: e.scalar_tensor_tensor(out=U_, in0=A_, scalar=-1.0, in1=A_, op0=ALU.mult, op1=ALU.mult),
                                            r=ta_, w=tu))
                    th.append(lambda: fw.op(dve, lambda e: e.tensor_scalar(out=U_, in0=U_, scalar1=1.0, scalar2=1e-30, op0=ALU.add, op1=ALU.max),
                                            r=tu, w=tu))
                    th.append(lambda: fw.op(act, lambda e: e.activation(out=U_, in_=U_, func=AF.Ln), r=tu, w=tu))
                    th.append(lambda: fw.op(act, lambda e: e.activation(out=U_, in_=U_, func=AF.Exp, scale=0.5), r=tu, w=tu))
                    th.append(lambda: fw.op(dve, lambda e: e.tensor_tensor(out=U_, in0=U_, in1=I_, op=ALU.mult), r=tu + ti, w=tu))

                    def f_scan():
                        fw.op(dve, lambda e: e.tensor_tensor_scan(out=H_, data0=A_, data1=U_, initial=hlast[:, c:c + 1], op0=ALU.mult, op1=ALU.add),
                              r=ta_ + tu + [t_hl], w=th_)
                        fw.op(dve, lambda e: e.tensor_copy(out=hlast[:, c:c + 1], in_=hh[:, oh, NB - 1:NB]), r=th_, w=[t_hl])
                    th.append(f_scan)
                    th.append(lambda: fw.op(pool, lambda e: e.tensor_tensor(out=hg[:, c, :], in0=H_, in1=gate16[:, c, :], op=ALU.mult),
                                            r=th_ + [t_g16], w=[t_hg]))
                    return th

                def zip_emit(chains):
                    chains = [c for c in chains if c]
                    while chains:
                        for c in chains:
                            c.pop(0)()
                        chains = [c for c in chains if c]

                for blk in range(S // NB):
                    rows = slice(blk * NB, (blk + 1) * NB)
                    tblks = [tXB[blk * (NB // 256) + q] for q in range(NB // 256)]
                    fw.dma(sp, X[:], X1[rows, :].rearrange("(u p) d -> p u d", p=128), r=tblks, w=[tX])
                    for u in range(NU):
                        rmsnorm_bf(X[:, u, :], tX, hn[u % 2][:], thn[u % 2], junk[:], tjunk, [z[:] for z in sm[u % 2]], tsm[u % 2])
                        transpose_to(hn[u % 2], thn[u % 2], hnT, thnT, 8, ps[6 + u % 2], pst[6 + u % 2], u * 128)
                    for c in range(16):
                        pb = c % 2
                        for k in range(8):
                            fw.op(pe, lambda e: e.matmul(out=ps[pb][0:88, :], lhsT=wl[:, k, c * 88:(c + 1) * 88], rhs=hnT[:, k, :],
                                                         start=(k == 0), stop=(k == 7)), r=[t_wl, thnT], w=[pst[pb]])
                        fw.op(act, lambda e: e.activation(out=gate16[:, c, :], in_=ps[pb][0:88, :], func=AF.Gelu_apprx_tanh),
                              r=[pst[pb]], w=[t_g16])
                    zip_emit([conv_chain(0, 0), conv_chain(0, 1)])
                    for n in range(8):
                        chains = [gate_chain(n, 0), gate_chain(n, 1)]
                        if n + 1 < 8:
                            chains += [conv_chain(n + 1, 0), conv_chain(n + 1, 1)]
                        zip_emit(chains)
                    for u in range(NU):
                        for nh in range(2):
                            pb = 6 + nh
                            for c in range(16):
                                fw.op(pe, lambda e: e.matmul(out=ps[pb][:], lhsT=hg[:, c, u * 128:(u + 1) * 128],
                                                             rhs=wlo[:, c, nh * 512:(nh + 1) * 512], start=(c == 0), stop=(c == 15)),
                                      r=[t_hg, t_wlo], w=[pst[pb]])
                            fw.op(dve, lambda e: e.tensor_tensor(out=X[:, u, nh * 512:(nh + 1) * 512], in0=X[:, u, nh * 512:(nh + 1) * 512],
                                                                 in1=ps[pb][:], op=ALU.add), r=[pst[pb], tX], w=[tX])
                    fw.dma(sp, X1[rows, :].rearrange("(u p) d -> p u d", p=128), X[:], r=[tX], w=tblks)
                fw.barrier()

        if stage >= 6:
            phase_ffn(1, False, True)

        fw.finish()
    return nc


_CONSTS = None


def make_in_map(inp, b):
    global _CONSTS
    if _CONSTS is None:
        _CONSTS = host_consts()
    f = lambda a: np.ascontiguousarray(np.asarray(a, dtype=np.float32))
    m = {
        "x": f(inp["x"][b]),
        "norm_mix": f(inp["norm_mix"]), "norm_ffn": f(inp["norm_ffn"]), "norm_final": f(inp["norm_final"]),
        "nsa_w_in": f(inp["nsa_w_in"][0]), "nsa_b_gate": f(inp["nsa_b_gate"][0]),
        "nsa_cmp_pos": f(inp["nsa_cmp_pos"][0]), "nsa_cmp_w1": f(inp["nsa_cmp_w1"][0]),
        "nsa_cmp_b1": f(inp["nsa_cmp_b1"][0]), "nsa_cmp_w2": f(inp["nsa_cmp_w2"][0]),
        "nsa_cmp_b2": f(inp["nsa_cmp_b2"][0]), "nsa_w_out": f(inp["nsa_w_out"][0]),
        "lru_w_in": f(inp["lru_w_in"][0]), "lru_conv_w": f(inp["lru_conv_w"][0]),
        "lru_conv_b": f(inp["lru_conv_b"][0]), "lru_w_a": f(inp["lru_w_a"][0]), "lru_b_a": f(inp["lru_b_a"][0]),
        "lru_w_x": f(inp["lru_w_x"][0]), "lru_b_x": f(inp["lru_b_x"][0]), "lru_lambda": f(inp["lru_lambda"][0]),
        "lru_w_out": f(inp["lru_w_out"][0]), "ffn_w_in": f(inp["ffn_w_in"]), "ffn_w_out": f(inp["ffn_w_out"]),
    }
    m.update(_CONSTS)
    return m


def kernel(**inputs):
    nc = build(debug=False)
    n = 4
    maps = [make_in_map(inputs, b) for b in range(n)]
    res = run_bass_kernel_spmd(nc, maps, core_ids=list(range(n)))
    return np.stack([np.asarray(res.results[b]["out"], dtype=np.float32) for b in range(n)], axis=0)
```

```python
import contextlib
import numpy as np
import ml_dtypes
import concourse.bass as bass
import concourse.mybir as mybir
from concourse.bass_utils import run_bass_kernel_spmd

F32 = mybir.dt.float32
BF16 = mybir.dt.bfloat16
AF = mybir.ActivationFunctionType
ALU = mybir.AluOpType
AX = mybir.AxisListType

S = 4096
D = 1024
NT = S // 128
NSA_IN = 2608
D_RNN = 1408
D_FF = 2816
EPS = 1e-6
SLOPES = [2.0 ** (-8.0 * (h + 1) / 16) for h in range(16)]
BIGD = 1.0e6


class Dom:
    def __init__(self, fw, name, unit):
        self.sem = fw.es.enter_context(fw.nc.semaphore(name))
        self.unit = unit
        self.count = 0


class T:
    __slots__ = ("w", "r", "dd")

    def __init__(self):
        self.w = None
        self.r = {}
        self.dd = None


class Eng:
    def __init__(self, fw, name, eng, is_pe=False, has_dom=True):
        self.name = name
        self.eng = eng
        self.is_pe = is_pe
        self.dom = Dom(fw, "c_" + name, 1) if has_dom else None
        self.known = {}


class FW:
    def __init__(self, nc):
        self.nc = nc
        self.es = contextlib.ExitStack()
        self.pe = Eng(self, "pe", nc.tensor, is_pe=True)
        self.act = Eng(self, "act", nc.scalar)
        self.dve = Eng(self, "dve", nc.vector)
        self.pool = Eng(self, "pool", nc.gpsimd)
        self.sp = Eng(self, "sp", nc.sync, has_dom=False)
        self.dma_doms = []
        self.free_doms = []
        self.uid = 0

    def sbuf(self, st, name, shape, dt):
        self.uid += 1
        return st.enter_context(self.nc.sbuf_tensor("%s_%d" % (name, self.uid), list(shape), dt))

    def _waits(self, E, r, w):
        deps = {}
        for t in r:
            if t.w is not None and deps.get(t.w[0], 0) < t.w[1]:
                deps[t.w[0]] = t.w[1]
        for t in w:
            if t.w is not None and deps.get(t.w[0], 0) < t.w[1]:
                deps[t.w[0]] = t.w[1]
            for d, s in t.r.items():
                if deps.get(d, 0) < s:
                    deps[d] = s
        for d, s in deps.items():
            if E.is_pe and d is E.dom:
                continue
            if E.known.get(d, 0) >= s:
                continue
            E.eng.wait_ge(d.sem, s * d.unit)
            E.known[d] = s

    def op(self, E, fn, r=(), w=()):
        self._waits(E, r, w)
        ins = fn(E.eng)
        d = E.dom
        d.count += 1
        ins.then_inc(d.sem, 1)
        for t in r:
            t.r[d] = d.count
        for t in w:
            t.w = (d, d.count)
            t.r = {}
        return ins

    def dma(self, E, out, in_, r=(), w=(), **kw):
        self._waits(E, r, w)
        t0 = w[0] if len(w) else r[0]
        if t0.dd is None:
            t0.dd = Dom(self, "d%d" % len(self.dma_doms), 16)
            self.dma_doms.append(t0.dd)
        d = t0.dd
        ins = E.eng.dma_start(out=out, in_=in_, **kw)
        d.count += 1
        ins.then_inc(d.sem, 16)
        for t in r:
            t.r[d] = d.count
        for t in w:
            t.w = (d, d.count)
            t.r = {}
        return ins

    def barrier(self):
        doms = [d for d in self.dma_doms if d.count] + [X.dom for X in (self.pe, self.act, self.dve, self.pool) if X.dom.count]
        for E in (self.pe, self.act, self.dve, self.pool, self.sp):
            for d in doms:
                if E.known.get(d, 0) < d.count:
                    E.eng.wait_ge(d.sem, d.count * d.unit)
                    E.known[d] = d.count

    def finish(self):
        E = self.sp
        for d in self.dma_doms:
            if d.count:
                E.eng.wait_ge(d.sem, d.count * d.unit)
        for X in (self.pe, self.act, self.dve, self.pool):
            if X.dom.count:
                E.eng.wait_ge(X.dom.sem, X.dom.count)


def host_consts():
    c = {}
    c["ident_bf"] = np.eye(128, dtype=np.float32).astype(ml_dtypes.bfloat16)
    c["ident_f"] = np.eye(128, dtype=np.float32)
    e = np.zeros((64, S), np.float32)
    for j in range(64):
        e[j, j * 64:(j + 1) * 64] = 256.0
    c["e256"] = e.astype(ml_dtypes.bfloat16)
    sr = np.arange(128)[:, None].astype(np.float32)
    tr = np.arange(128)[None, :].astype(np.float32)
    d0 = tr - sr
    c["d0"] = np.stack([np.where(d0 >= 0, d0, BIGD), d0, np.where(d0 < 0, d0, BIGD)]).astype(np.float32)
    i = np.arange(128)[:, None]
    m = np.arange(-248, 256)[None, :]
    dc = (i - 16 * m - 31).astype(np.float32)
    c["distc"] = np.where(dc >= 0, dc, BIGD).astype(np.float32)
    jp = np.arange(-62, 64)[None, :]
    ci = (np.arange(128)[:, None] // 64)
    fb = np.zeros((128, 126), np.float32)
    fb = np.where((jp == ci) | (jp == ci - 1), 100.0, fb)
    fb = np.where(jp > ci, -100.0, fb)
    c["fbias"] = fb.astype(np.float32)
    ov = np.zeros((256, 64), np.float32)
    for n in range(255):
        for j in range(64):
            if 16 * n < 64 * j + 64 and 16 * n + 32 > 64 * j:
                ov[n, j] = 1.0
    c["ovl"] = ov.reshape(2, 128, 64).astype(ml_dtypes.bfloat16)
    return c


def build(debug=False, stage=99):
    nc = bass.Bass("TRN2", target_bir_lowering=False)
    fw = FW(nc)
    pe, act, dve, pool, sp = fw.pe, fw.act, fw.dve, fw.pool, fw.sp

    def din(name, shape, dt=F32):
        return nc.dram_tensor(name, list(shape), dt, kind="ExternalInput").ap()

    def dscr(name, shape, dt):
        return nc.dram_tensor(name, list(shape), dt, kind="ExternalOutput" if debug else "Internal").ap()

    x_in = din("x", [S, D])
    norm_mix = din("norm_mix", [2, D])
    norm_ffn = din("norm_ffn", [2, D])
    norm_final = din("norm_final", [D])
    nsa_w_in = din("nsa_w_in", [D, NSA_IN])
    nsa_b_gate = din("nsa_b_gate", [48])
    nsa_cmp_pos = din("nsa_cmp_pos", [2, 32, 64])
    nsa_cmp_w1 = din("nsa_cmp_w1", [2, 2048, 256])
    nsa_cmp_b1 = din("nsa_cmp_b1", [2, 256])
    nsa_cmp_w2 = din("nsa_cmp_w2", [2, 256, 64])
    nsa_cmp_b2 = din("nsa_cmp_b2", [2, 64])
    nsa_w_out = din("nsa_w_out", [D, D])
    lru_w_in = din("lru_w_in", [D, 2 * D_RNN])
    lru_conv_w = din("lru_conv_w", [4, D_RNN])
    lru_conv_b = din("lru_conv_b", [D_RNN])
    lru_w_a = din("lru_w_a", [8, 176, 176])
    lru_b_a = din("lru_b_a", [D_RNN])
    lru_w_x = din("lru_w_x", [8, 176, 176])
    lru_b_x = din("lru_b_x", [D_RNN])
    lru_lambda = din("lru_lambda", [D_RNN])
    lru_w_out = din("lru_w_out", [D_RNN, D])
    ffn_w_in = din("ffn_w_in", [2, D, 2 * D_FF])
    ffn_w_out = din("ffn_w_out", [2, D_FF, D])
    c_ident_bf = din("ident_bf", [128, 128], BF16)
    c_ident_f = din("ident_f", [128, 128])
    c_e256 = din("e256", [64, S], BF16)
    c_d0 = din("d0", [3, 128, 128])
    c_distc = din("distc", [128, 504])
    c_fbias = din("fbias", [128, 126])
    c_ovl = din("ovl", [2, 128, 64], BF16)

    out_ap = nc.dram_tensor("out", [S, D], F32, kind="ExternalOutput").ap()

    QT = dscr("QT", [1024, S], BF16)
    KcT = dscr("KcT", [256, S], BF16)
    VcT = dscr("VcT", [256, S], BF16)
    KsT = dscr("KsT", [256, S], BF16)
    KwT = dscr("KwT", [256, S], BF16)
    Vs = dscr("Vs", [S, 256], BF16)
    Vw = dscr("Vw", [S, 256], BF16)
    G = dscr("G", [S, 48], F32)
    Y = dscr("Y", [S, D], BF16)
    X1 = dscr("X1", [S, D], F32)
    tQT, tKcT, tVcT, tKsT, tKwT, tVs, tVw, tG, tY, tX1 = [T() for _ in range(10)]

    with fw.es:
        gst = fw.es
        ps = [gst.enter_context(nc.psum_tensor("ps%d" % i, [128, 512], F32)) for i in range(8)]
        pst = [T() for _ in range(8)]
        ident_bf = fw.sbuf(gst, "identbf", [128, 128], BF16)
        ident_f = fw.sbuf(gst, "identf", [128, 128], F32)
        t_ident = T()
        fw.dma(sp, ident_bf[:], c_ident_bf, w=[t_ident])
        fw.dma(sp, ident_f[:], c_ident_f, w=[t_ident])
        epsc = fw.sbuf(gst, "epsc", [128, 1], F32)
        t_eps = T()
        fw.op(dve, lambda e: e.memset(epsc[:], EPS), w=[t_eps])

        rr = [0]

        def evac(out, in_, r, w, scale=None):
            rr[0] += 1
            if rr[0] % 2 == 0:
                if scale is None:
                    fw.op(act, lambda e: e.activation(out=out, in_=in_, func=AF.Copy), r=r, w=w)
                else:
                    fw.op(act, lambda e: e.activation(out=out, in_=in_, func=AF.Copy, scale=scale), r=r, w=w)
            else:
                if scale is None:
                    fw.op(dve, lambda e: e.tensor_copy(out=out, in_=in_), r=r, w=w)
                else:
                    fw.op(dve, lambda e: e.tensor_scalar(out=out, in0=in_, scalar1=scale, scalar2=None, op0=ALU.mult), r=r, w=w)

        def load_gain(st, g_ap, name):
            gt = fw.sbuf(st, name, [128, 8], F32)
            tg = T()
            fw.dma(sp, gt[:], g_ap.rearrange("(k p) -> p k", p=128), w=[tg], allow_slow_non_contiguous=True)
            return gt, tg

        cast_rr = [0]

        def load_cast(st, dst, tdst, src, nrows, ncols, gain=None, tgain=None, stg=None, tstg=None):
            CH = stg[0].shape[1]
            for c0 in range(0, ncols, CH):
                cw = min(CH, ncols - c0)
                k = cast_rr[0] % len(stg)
                cast_rr[0] += 1
                s_, ts_ = stg[k], tstg[k]
                fw.dma((sp, pool, act, sp)[k % 4], s_[0:nrows, 0:cw], src[:, c0:c0 + cw], w=[ts_])
                if k % 2 == 0:
                    if gain is None:
                        fw.op(dve, lambda e: e.tensor_copy(out=dst[:, c0:c0 + cw], in_=s_[0:nrows, 0:cw]), r=[ts_], w=[tdst])
                    else:
                        fw.op(dve, lambda e: e.tensor_scalar(out=dst[:, c0:c0 + cw], in0=s_[0:nrows, 0:cw], scalar1=gain,
                                                             scalar2=None, op0=ALU.mult), r=[ts_, tgain], w=[tdst])
                else:
                    if gain is None:
                        fw.op(act, lambda e: e.activation(out=dst[:, c0:c0 + cw], in_=s_[0:nrows, 0:cw], func=AF.Copy), r=[ts_], w=[tdst])
                    else:
                        fw.op(act, lambda e: e.activation(out=dst[:, c0:c0 + cw], in_=s_[0:nrows, 0:cw], func=AF.Copy, scale=gain),
                              r=[ts_, tgain], w=[tdst])

        def rmsnorm_bf(xt, tx, hn, thn, junk, tjunk, st_small, tsm):
            ss, sd, rs = st_small
            fw.op(act, lambda e: e.activation(out=junk, in_=xt, func=AF.Square, accum_out=ss), r=[tx], w=[tjunk, tsm])
            fw.op(act, lambda e: e.activation(out=sd, in_=ss, func=AF.Sqrt, bias=epsc[:], scale=1.0 / D), r=[tsm, t_eps], w=[tsm])
            fw.op(dve, lambda e: e.reciprocal(out=rs, in_=sd), r=[tsm], w=[tsm])
            fw.op(dve, lambda e: e.tensor_scalar(out=hn, in0=xt, scalar1=rs, scalar2=None, op0=ALU.mult), r=[tx, tsm], w=[thn])

        def transpose_to(hn, thn, dstT, tdst, nchunk, pbank, tpbank, col0, rows=128):
            pv = pbank[:].bitcast(BF16)
            for k0 in range(0, nchunk, 8):
                kn = min(8, nchunk - k0)
                for k in range(kn):
                    fw.op(pe, lambda e: e.transpose(out=pv[:, k * 128:(k + 1) * 128], in_=hn[:, (k0 + k) * 128:(k0 + k + 1) * 128],
                                                    identity=ident_bf[:]), r=[thn, t_ident], w=[tpbank])
                evac(dstT[:, k0:k0 + kn, col0:col0 + 128], pv[:, 0:kn * 128].rearrange("p (k t) -> p k t", k=kn), r=[tpbank], w=[tdst])

        if stage >= 1:
            with contextlib.ExitStack() as st:
                w_in = fw.sbuf(st, "w_in", [128, 8, NSA_IN], BF16)
                t_w = T()
                g0, tg0 = load_gain(st, norm_mix[0], "g0")
                stg = [fw.sbuf(st, "stg", [128, 2608], F32) for _ in range(2)]
                tstg = [T(), T()]
                for k in range(8):
                    load_cast(st, w_in[:, k, :], t_w, nsa_w_in[k * 128:(k + 1) * 128, :], 128, NSA_IN,
                              gain=g0[:, k:k + 1], tgain=tg0, stg=stg, tstg=tstg)
                bg = fw.sbuf(st, "bg", [128, 48], F32)
                t_bg = T()
                fw.dma(sp, bg[:], nsa_b_gate.partition_broadcast(128), w=[t_bg])
                xt = [fw.sbuf(st, "xt", [128, D], F32) for _ in range(2)]
                txt = [T(), T()]
                hn = [fw.sbuf(st, "hn", [128, D], BF16) for _ in range(2)]
                thn = [T(), T()]
                junk = fw.sbuf(st, "junk", [128, D], BF16)
                tjunk = T()
                sm = [[fw.sbuf(st, "sm", [128, 1], F32) for _ in range(3)] for _ in range(2)]
                tsm = [T(), T()]
                hnT = [fw.sbuf(st, "hnT", [128, 8, 512], BF16) for _ in range(2)]
                thnT = [T(), T()]
                fst = [fw.sbuf(st, "fst", [128, 16, 512], BF16) for _ in range(2)]
                tfst = [T(), T()]
                tst = [fw.sbuf(st, "tst", [128, 4, 512], BF16) for _ in range(2)]
                ttst = [T(), T()]
                gst_ = [fw.sbuf(st, "gst", [128, 4, 48], F32) for _ in range(2)]
                tgst = [T(), T()]
                fcols = [c * 128 for c in range(8)] + [1024, 1152, 1280, 1408, 1536, 1664, 2048, 2176]
                it = 0
                for R in range(8):
                    b = R % 2
                    for u in range(4):
                        ti = 4 * R + u
                        s2 = it % 2
                        it += 1
                        fw.dma(sp, xt[s2][:], x_in[ti * 128:(ti + 1) * 128, :], w=[txt[s2]])
                        rmsnorm_bf(xt[s2][:], txt[s2], hn[s2][:], thn[s2], junk[:], tjunk, [z[:] for z in sm[s2]], tsm[s2])
                        transpose_to(hn[s2], thn[s2], hnT[b], thnT[b], 8, ps[6 + s2], pst[6 + s2], u * 128)
                    for ci, c0 in enumerate(fcols):
                        pb = ci % 4
                        for k in range(8):
                            fw.op(pe, lambda e: e.matmul(out=ps[pb][:], lhsT=w_in[:, k, c0:c0 + 128], rhs=hnT[b][:, k, :],
                                                         start=(k == 0), stop=(k == 7)), r=[t_w, thnT[b]], w=[pst[pb]])
                        evac(fst[b][:, ci, :], ps[pb][:], r=[pst[pb]], w=[tfst[b]], scale=(0.125 if ci < 8 else None))
                    cs = slice(R * 512, (R + 1) * 512)
                    fw.dma(sp, QT.rearrange("(c p) t -> p c t", p=128)[:, :, cs], fst[b][:, 0:8, :], r=[tfst[b]], w=[tQT])
                    fw.dma(pool, KcT.rearrange("(c p) t -> p c t", p=128)[:, :, cs], fst[b][:, 8:10, :], r=[tfst[b]], w=[tKcT])
                    fw.dma(pool, VcT.rearrange("(c p) t -> p c t", p=128)[:, :, cs], fst[b][:, 10:12, :], r=[tfst[b]], w=[tVcT])
                    fw.dma(sp, KsT.rearrange("(c p) t -> p c t", p=128)[:, :, cs], fst[b][:, 12:14, :], r=[tfst[b]], w=[tKsT])
                    fw.dma(pool, KwT.rearrange("(c p) t -> p c t", p=128)[:, :, cs], fst[b][:, 14:16, :], r=[tfst[b]], w=[tKwT])
                    for u in range(4):
                        pb = 4 + (u % 2)
                        for (c0, cw, o0) in ((1792, 256, 0), (2304, 256, 256)):
                            for k in range(8):
                                fw.op(pe, lambda e: e.matmul(out=ps[pb][:, o0:o0 + cw], lhsT=hnT[b][:, k, u * 128:(u + 1) * 128],
                                                             rhs=w_in[:, k, c0:c0 + cw], start=(k == 0), stop=(k == 7)),
                                      r=[t_w, thnT[b]], w=[pst[pb]])
                        evac(tst[b][:, u, :], ps[pb][:], r=[pst[pb]], w=[ttst[b]])
                        for k in range(8):
                            fw.op(pe, lambda e: e.matmul(out=ps[pb][:, 0:48], lhsT=hnT[b][:, k, u * 128:(u + 1) * 128],
                                                         rhs=w_in[:, k, 2560:2608], start=(k == 0), stop=(k == 7)),
                                  r=[t_w, thnT[b]], w=[pst[pb]])
                        fw.op(dve, lambda e: e.tensor_tensor(out=gst_[b][:, u, :], in0=ps[pb][:, 0:48], in1=bg[:], op=ALU.add),
                              r=[pst[pb], t_bg], w=[tgst[b]])
                    fw.op(act, lambda e: e.activation(out=gst_[b][:], in_=gst_[b][:], func=AF.Sigmoid), r=[tgst[b]], w=[tgst[b]])
                    rs_ = slice(R * 512, (R + 1) * 512)
                    fw.dma(sp, Vs[rs_, :].rearrange("(u p) c -> p u c", p=128), tst[b][:, :, 0:256], r=[ttst[b]], w=[tVs])
                    fw.dma(pool, Vw[rs_, :].rearrange("(u p) c -> p u c", p=128), tst[b][:, :, 256:512], r=[ttst[b]], w=[tVw])
                    fw.dma(sp, G[rs_, :].rearrange("(u p) c -> p u c", p=128), gst_[b][:], r=[tgst[b]], w=[tG])

        fw.barrier()
        if stage >= 2:
            with contextlib.ExitStack() as st:
                kcT_all = fw.sbuf(st, "kcT", [128, 4, 256], BF16)
                t_kc = T()
                vc_all = fw.sbuf(st, "vc", [128, 4, 2, 64], BF16)
                t_vc = T()
                fw.op(pool, lambda e: e.memset(vc_all[:], 0.0), w=[t_vc])
                fw.op(pool, lambda e: e.memset(kcT_all[:], 0.0), w=[t_kc])
                with contextlib.ExitStack() as sb:
                    stgB = fw.sbuf(sb, "stgB", [64, 32, 256], F32)
                    t_stgB = T()
                    w1 = fw.sbuf(sb, "w1", [64, 32, 256], BF16)
                    t_w1 = T()
                    w2f = fw.sbuf(sb, "w2f", [128, 2, 64], F32)
                    w2 = fw.sbuf(sb, "w2", [128, 2, 64], BF16)
                    t_w2f, t_w2 = T(), T()
                    posf = fw.sbuf(sb, "posf", [64, 32], F32)
                    posT = fw.sbuf(sb, "posT", [64, 32], BF16)
                    t_posf, t_posT = T(), T()
                    b1t = fw.sbuf(sb, "b1t", [128, 2], F32)
                    c1b = fw.sbuf(sb, "c1b", [128, 2], F32)
                    b2col = fw.sbuf(sb, "b2col", [64, 1], F32)
                    b2row = fw.sbuf(sb, "b2row", [128, 64], F32)
                    t_b1, t_c1b, t_b2c, t_b2r = T(), T(), T(), T()
                    rawT = [fw.sbuf(sb, "rawT", [64, S], BF16) for _ in range(2)]
                    t_raw = [T(), T()]
                    hidT = fw.sbuf(sb, "hidT", [128, 2, 256], BF16)
                    t_hid = T()
                    for kv in range(2):
                        fw.dma(sp, stgB[:], nsa_cmp_w1[kv].rearrange("(l d) h -> d l h", d=64), w=[t_stgB])
                        for q4 in range(4):
                            E = (dve, pool, act, dve)[q4]
                            if E is act:
                                fw.op(E, lambda e: e.activation(out=w1[:, q4 * 8:(q4 + 1) * 8, :], in_=stgB[:, q4 * 8:(q4 + 1) * 8, :], func=AF.Copy),
                                      r=[t_stgB], w=[t_w1])
                            else:
                                fw.op(E, lambda e: e.tensor_copy(out=w1[:, q4 * 8:(q4 + 1) * 8, :], in_=stgB[:, q4 * 8:(q4 + 1) * 8, :]),
                                      r=[t_stgB], w=[t_w1])
                        fw.dma(sp, w2f[:], nsa_cmp_w2[kv].rearrange("(c p) d -> p c d", p=128), w=[t_w2f])
                        fw.op(dve, lambda e: e.tensor_copy(out=w2[:], in_=w2f[:]), r=[t_w2f], w=[t_w2])
                        fw.dma(sp, posf[:], nsa_cmp_pos[kv].rearrange("l d -> d l"), w=[t_posf], allow_slow_non_contiguous=True)
                        fw.op(dve, lambda e: e.tensor_copy(out=posT[:], in_=posf[:]), r=[t_posf], w=[t_posT])
                        fw.dma(sp, b1t[:], nsa_cmp_b1[kv].rearrange("(c p) -> p c", p=128), w=[t_b1], allow_slow_non_contiguous=True)
                        fw.dma(sp, b2col[:], nsa_cmp_b2[kv].rearrange("(d o) -> d o", o=1), w=[t_b2c], allow_slow_non_contiguous=True)
                        fw.dma(sp, b2row[:], nsa_cmp_b2[kv].partition_broadcast(128), w=[t_b2r])
                        for c in range(2):
                            for l in range(32):
                                fw.op(pe, lambda e: e.matmul(out=ps[0][:, c:c + 1], lhsT=w1[:, l, c * 128:(c + 1) * 128], rhs=posT[:, l:l + 1],
                                                             start=(l == 0), stop=(l == 31)), r=[t_w1, t_posT], w=[pst[0]])
                        fw.op(dve, lambda e: e.tensor_tensor(out=c1b[:], in0=ps[0][:, 0:2], in1=b1t[:], op=ALU.add), r=[pst[0], t_b1], w=[t_c1b])
                        for g in range(4):
                            rt, trt = rawT[g % 2], t_raw[g % 2]
                            src = (KcT, VcT)[kv]
                            fw.dma(sp, rt[:], src[g * 64:(g + 1) * 64, :], r=[(tKcT, tVcT)[kv]], w=[trt])
                            for c in range(2):
                                for l in range(32):
                                    fw.op(pe, lambda e: e.matmul(out=ps[1 + c][:, 0:255], lhsT=w1[:, l, c * 128:(c + 1) * 128],
                                                                 rhs=rt[:, l:l + 16 * 254 + 1:16], start=(l == 0), stop=(l == 31)),
                                          r=[t_w1, trt], w=[pst[1 + c]])
                                fw.op(act, lambda e: e.activation(out=hidT[:, c, 0:255], in_=ps[1 + c][:, 0:255], func=AF.Gelu_apprx_tanh,
                                                                  bias=c1b[:, c:c + 1]), r=[pst[1 + c], t_c1b], w=[t_hid])
                            if kv == 0:
                                for c in range(2):
                                    fw.op(pe, lambda e: e.matmul(out=ps[3][0:64, 0:255], lhsT=w2[:, c, :], rhs=hidT[:, c, 0:255],
                                                                 start=(c == 0), stop=(c == 1)), r=[t_w2, t_hid], w=[pst[3]])
                                fw.op(dve, lambda e: e.tensor_scalar(out=kcT_all[0:64, g, 0:255], in0=ps[3][0:64, 0:255], scalar1=b2col[:],
                                                                     scalar2=None, op0=ALU.add), r=[pst[3], t_b2c], w=[t_kc])
                            else:
                                for nch, (n0, nn) in enumerate(((0, 128), (128, 127))):
                                    for c in range(2):
                                        fw.op(pe, lambda e: e.matmul(out=ps[3][0:nn, nch * 64:(nch + 1) * 64], lhsT=hidT[:, c, n0:n0 + nn],
                                                                     rhs=w2[:, c, :], start=(c == 0), stop=(c == 1)), r=[t_w2, t_hid], w=[pst[3]])
                                    fw.op(dve, lambda e: e.tensor_tensor(out=vc_all[0:nn, g, nch, :], in0=ps[3][0:nn, nch * 64:(nch + 1) * 64],
                                                                         in1=b2row[0:nn, :], op=ALU.add), r=[pst[3], t_b2r], w=[t_vc])
                fw.barrier()
                if debug:
                    dbg_kc = nc.dram_tensor("dbg_kc", [128, 4, 256], BF16, kind="ExternalOutput").ap()
                    dbg_vc = nc.dram_tensor("dbg_vc", [128, 4, 2, 64], BF16, kind="ExternalOutput").ap()
                    fw.dma(sp, dbg_kc, kcT_all[:], r=[t_kc])
                    fw.dma(sp, dbg_vc, vc_all[:], r=[t_vc])

                if stage >= 3:
                    d0 = fw.sbuf(st, "d0", [128, 3, 128], F32)
                    distc = fw.sbuf(st, "distc", [128, 504], F32)
                    fbias = fw.sbuf(st, "fbias", [128, 126], F32)
                    t_cc = T()
                    fw.dma(sp, d0[:], c_d0.rearrange("k p t -> p k t"), w=[t_cc])
                    fw.dma(sp, distc[:], c_distc, w=[t_cc])
                    fw.dma(sp, fbias[:], c_fbias, w=[t_cc])
                    ovl = fw.sbuf(st, "ovl", [128, 2, 64], BF16)
                    fw.dma(sp, ovl[:], c_ovl.rearrange("k p j -> p k j"), w=[t_cc])
                    bcb = fw.sbuf(st, "bcb", [128, 4, 504], BF16)
                    ks_sel = fw.sbuf(st, "ks_sel", [128, S], BF16)
                    t_ks = T()
                    fw.dma(sp, ks_sel[64:128, :], c_e256, w=[t_ks])
                    kw = fw.sbuf(st, "kw", [128, S], BF16)
                    t_kw = T()
                    fw.op(pool, lambda e: e.memset(kw[64:128, :], 0.0), w=[t_kw])
                    vs_aug = fw.sbuf(st, "vs_aug", [128, 32, 65], BF16)
                    vw_aug = fw.sbuf(st, "vw_aug", [128, 32, 65], BF16)
                    t_vs, t_vw = T(), T()
                    fw.op(pool, lambda e: e.memset(vs_aug[:], 1.0), w=[t_vs])
                    fw.op(pool, lambda e: e.memset(vw_aug[:], 1.0), w=[t_vw])
                    qstack = fw.sbuf(st, "qstack", [128, 4, S], BF16)
                    t_q = T()
                    fw.op(pool, lambda e: e.memset(qstack[:], 0.0), w=[t_q])
                    t_qm = [T() for _ in range(32)]
                    A = fw.sbuf(st, "A", [128, 33, 512], BF16)
                    t_A = T()
                    bc = fw.sbuf(st, "bc", [128, 4, 504], F32)
                    t_bc = T()
                    gt = fw.sbuf(st, "gt", [128, 32, 12], F32)
                    t_gt = T()
                    sc = fw.sbuf(st, "sc", [128, 4, 256], F32)
                    pc = fw.sbuf(st, "pc", [128, 4, 256], F32)
                    t_sc, t_pc = T(), T()
                    rowsum = fw.sbuf(st, "rowsum", [128, 4], F32)
                    rinv = fw.sbuf(st, "rinv", [128, 4], F32)
                    t_rs, t_ri = T(), T()
                    pn = fw.sbuf(st, "pn", [128, 4, 256], BF16)
                    t_pn = T()
                    fw.op(pool, lambda e: e.memset(pn[:], 0.0), w=[t_pn])
                    psumh = fw.sbuf(st, "psumh", [128, 256], F32)
                    t_ph = T()
                    fw.op(pool, lambda e: e.memset(psumh[:], 0.0), w=[t_ph])
                    pnT = fw.sbuf(st, "pnT", [128, 8, 128], BF16)
                    t_pnT = T()
                    imp = fw.sbuf(st, "imp", [128, 64], F32)
                    score = fw.sbuf(st, "score", [128, 64], F32)
                    score2 = fw.sbuf(st, "score2", [128, 64], F32)
                    m8a = fw.sbuf(st, "m8a", [128, 8], F32)
                    m8b = fw.sbuf(st, "m8b", [128, 8], F32)
                    t_imp, t_score, t_score2, t_m8a, t_m8b = T(), T(), T(), T(), T()
                    msel = fw.sbuf(st, "msel", [128, 128], BF16)
                    t_msel = T()
                    fw.op(pool, lambda e: e.memset(msel[:], 0.0), w=[t_msel])
                    NPT = 3
                    pt = [fw.sbuf(st, "pt", [128, 512], BF16) for _ in range(NPT)]
                    pa = [fw.sbuf(st, "pa", [128, 512], BF16) for _ in range(NPT)]
                    t_pt = [T() for _ in range(NPT)]
                    t_pa = [T() for _ in range(NPT)]
                    osb = fw.sbuf(st, "osb", [65, 512], F32)
                    t_osb = T()
                    f4 = fw.sbuf(st, "f4", [128, 4], F32)
                    t_f4 = T()
                    yacc = fw.sbuf(st, "yacc", [128, 256], F32)
                    t_yacc = T()
                    ybf = [fw.sbuf(st, "ybf", [128, 256], BF16) for _ in range(2)]
                    t_ybf = [T(), T()]
                    pv6 = ps[6][:].bitcast(BF16)
                    pst7b = pst[7]
                    yacc2 = [yacc, fw.sbuf(st, "yacc2", [128, 256], F32)]
                    t_yacc2 = [t_yacc, T()]
                    osb2 = [osb, fw.sbuf(st, "osb2", [65, 512], F32)]
                    t_osb2 = [t_osb, T()]
                    f2 = [fw.sbuf(st, "f2", [128, 2], F32) for _ in range(2)]
                    t_f2 = [T(), T()]
                    NPT2 = 6
                    pt = pt + [fw.sbuf(st, "pt", [128, 512], BF16) for _ in range(3)]
                    pa = pa + [fw.sbuf(st, "pa", [128, 512], BF16) for _ in range(3)]
                    t_pt = t_pt + [T(), T(), T()]
                    t_pa = t_pa + [T(), T(), T()]
                    kctr = [0]
                    import os as _os
                    _NG = int(_os.environ.get('NSA_G', '4')); _NA = int(_os.environ.get('NSA_A', '32'))

                    CS = []
                    for ci in range(4):
                        if ci == 0:
                            tiles = [sc, pc, pn, psumh, pnT, rowsum, rinv, imp, score, score2, m8a, m8b, msel]
                            trk = [t_sc, t_pc, t_pn, t_ph, t_pnT, t_rs, t_ri, t_imp, t_score, t_score2, t_m8a, t_m8b, t_msel]
                        else:
                            tiles = [fw.sbuf(st, "sc", [128, 4], F32), fw.sbuf(st, "pc", [128, 4], F32),
                                     fw.sbuf(st, "pn", [128, 4, 256], BF16), fw.sbuf(st, "psumh", [128, 4], F32),
                                     fw.sbuf(st, "pnT", [128, 8, 128], BF16), fw.sbuf(st, "rowsum", [128, 4], F32),
                                     fw.sbuf(st, "rinv", [128, 4], F32), fw.sbuf(st, "imp", [128, 64], F32),
                                     fw.sbuf(st, "score", [128, 64], F32), fw.sbuf(st, "score2", [128, 64], F32),
                                     fw.sbuf(st, "m8a", [128, 8], F32), fw.sbuf(st, "m8b", [128, 8], F32),
                                     fw.sbuf(st, "msel", [128, 128], BF16)]
                            trk = [T() for _ in range(13)]
                            fw.op(pool, lambda e: e.memset(tiles[2][:], 0.0), w=[trk[2]])
                            fw.op(pool, lambda e: e.memset(tiles[12][:], 0.0), w=[trk[12]])
                        tiles = tiles + [fw.sbuf(st, "gf", [128, 4], F32), fw.sbuf(st, "imp4", [128, 4, 64], F32)]
                        trk = trk + [T(), T()]
                        CS.append(tuple(tiles + trk))
                    ocbuf = fw.sbuf(st, "ocbuf", [128, 32, 256], BF16)
                    t_oc = [T() for _ in range(32)]

                    def cmp_thunks(g, a, ci):
                        th = []
                        (sc_, pc_, pcb, psumh_, pnT, rowsum, rinv, imp, score, score2, m8a, m8b, msel, gf, imp4,
                         t_sc_, t_pc_, t_pcb, t_ph_, t_pnT, t_rs, t_ri, t_imp, t_score, t_score2, t_m8a, t_m8b, t_msel, t_gf, t_imp4) = CS[ci]
                        bS, bT = 2 * ci, 2 * ci + 1
                        pvT = ps[bT][:].bitcast(BF16)
                        ta = slice(a * 128, (a + 1) * 128)
                        boff = 248 - 8 * a
                        for hp in range(2):
                            def f_mm(hp=hp):
                                for hh in range(2):
                                    h = 2 * hp + hh
                                    fw.op(pe, lambda e: e.matmul(out=ps[bS][:, hh * 256:hh * 256 + 255], lhsT=qstack[:, h, ta],
                                                                 rhs=kcT_all[:, g, 0:255], start=True, stop=False), r=[t_q, t_qm[a], t_kc], w=[pst[bS]])
                                    fw.op(pe, lambda e: e.matmul(out=ps[bS][:, hh * 256:hh * 256 + 255], lhsT=ident_bf[:],
                                                                 rhs=bcb[:, h, boff:boff + 255], start=False, stop=True), r=[t_bc, t_ident], w=[pst[bS]])
                            th.append(f_mm)
                            for hh in range(2):
                                def f_exp(hp=hp, hh=hh):
                                    h = 2 * hp + hh
                                    fw.op(act, lambda e: e.activation(out=pcb[:, h, 0:255], in_=ps[bS][:, hh * 256:hh * 256 + 255], func=AF.Exp,
                                                                      accum_out=rowsum[:, h:h + 1]), r=[pst[bS]], w=[t_pcb, t_rs])
                                th.append(f_exp)

                        def f_rinv():
                            fw.op(dve, lambda e: e.tensor_scalar(out=rinv[:], in0=rowsum[:], scalar1=1e-30, scalar2=None, op0=ALU.max),
                                  r=[t_rs], w=[t_ri])
                            fw.op(dve, lambda e: e.reciprocal(out=rinv[:], in_=rinv[:]), r=[t_ri], w=[t_ri])
                        th.append(f_rinv)

                        def f_pnT():
                            for h in range(4):
                                for nch in range(2):
                                    fw.op(pe, lambda e: e.transpose(out=pvT[:, (h * 2 + nch) * 128:(h * 2 + nch + 1) * 128],
                                                                    in_=pcb[:, h, nch * 128:(nch + 1) * 128], identity=ident_bf[:]),
                                          r=[t_pcb, t_ident], w=[pst[bT]])
                            fw.op(act, lambda e: e.activation(out=pnT[:], in_=pvT.rearrange("p (k t) -> p k t", k=8), func=AF.Copy),
                                  r=[pst[bT]], w=[t_pnT])
                        th.append(f_pnT)
                        th.append(lambda: fw.op(dve, lambda e: e.tensor_tensor(out=gf[:], in0=rinv[:], in1=gt[:, a, 0:12:3], op=ALU.mult),
                                                r=[t_ri, t_gt], w=[t_gf]))

                        def f_oc():
                            for h in range(4):
                                for nch in range(2):
                                    fw.op(pe, lambda e: e.matmul(out=ps[bS][:, h * 64:(h + 1) * 64], lhsT=pnT[:, h * 2 + nch, :],
                                                                 rhs=vc_all[:, g, nch, :], start=(nch == 0), stop=(nch == 1)),
                                          r=[t_pnT, t_vc], w=[pst[bS]])
                                for nch in range(2):
                                    fw.op(pe, lambda e: e.matmul(out=ps[bS][:, 256 + h * 64:256 + (h + 1) * 64], lhsT=pnT[:, h * 2 + nch, :],
                                                                 rhs=ovl[:, nch, :], start=(nch == 0), stop=(nch == 1)),
                                          r=[t_pnT, t_cc], w=[pst[bS]])
                        th.append(f_oc)
                        th.append(lambda: fw.op(dve, lambda e: e.tensor_tensor(out=ocbuf[:, a, :].rearrange("p (h c) -> p h c", h=4),
                                                                               in0=ps[bS][:, 0:256].rearrange("p (h c) -> p h c", h=4),
                                                                               in1=gf[:].unsqueeze(2).to_broadcast([128, 4, 64]), op=ALU.mult),
                                                r=[pst[bS], t_gf], w=[t_oc[a]]))
                        th.append(lambda: fw.op(dve, lambda e: e.tensor_tensor(out=imp4[:], in0=ps[bS][:, 256:512].rearrange("p (h c) -> p h c", h=4),
                                                                               in1=rinv[:].unsqueeze(2).to_broadcast([128, 4, 64]), op=ALU.mult),
                                                r=[pst[bS], t_ri], w=[t_imp4]))

                        def f_imp():
                            fw.op(dve, lambda e: e.tensor_reduce(out=imp[:], in_=imp4[:].rearrange("p h j -> p j h"), axis=AX.X, op=ALU.add),
                                  r=[t_imp4], w=[t_imp])
                            fw.op(dve, lambda e: e.tensor_tensor(out=score[:], in0=imp[:], in1=fbias[:, 62 - 2 * a:62 - 2 * a + 64], op=ALU.add),
                                  r=[t_imp, t_cc], w=[t_score])
                            fw.op(dve, lambda e: e.memset(score[:, 0:1], 100.0), w=[t_score])
                        th.append(f_imp)

                        def f_topk():
                            fw.op(dve, lambda e: e.max(out=m8a[:], in_=score[:]), r=[t_score], w=[t_m8a])
                            fw.op(dve, lambda e: e.match_replace(out=score2[:], in_to_replace=m8a[:], in_values=score[:], imm_value=-1.0e9),
                                  r=[t_score, t_m8a], w=[t_score2])
                            fw.op(dve, lambda e: e.max(out=m8b[:], in_=score2[:]), r=[t_score2], w=[t_m8b])
                            fw.op(dve, lambda e: e.tensor_scalar(out=msel[:, 64:128], in0=score[:], scalar1=m8b[:, 7:8], scalar2=-1.0,
                                                                 op0=ALU.is_ge, op1=ALU.add), r=[t_score, t_m8b], w=[t_msel])
                        th.append(f_topk)

                        def f_mask():
                            fw.op(pe, lambda e: e.transpose(out=pvT[:, 0:128], in_=msel[:], identity=ident_bf[:]), r=[t_msel, t_ident], w=[pst[bT]])
                            for h in range(4):
                                if h % 2 == 0:
                                    fw.op(act, lambda e: e.activation(out=qstack[64:128, h, ta], in_=pvT[64:128, 0:128], func=AF.Copy),
                                          r=[pst[bT]], w=[t_qm[a]])
                                else:
                                    fw.op(dve, lambda e: e.tensor_copy(out=qstack[64:128, h, ta], in_=pvT[64:128, 0:128]), r=[pst[bT]], w=[t_qm[a]])
                        th.append(f_mask)
                        return th

                    def obank(a, kind):
                        return ((3, 3), (4, 4))[kind][a % 2]

                    f4k = [fw.sbuf(st, "f4k", [128, 4], F32) for _ in range(2)]
                    t_f4k = [T(), T()]
                    otmp = [fw.sbuf(st, "otmp", [128, 4, 64], F32) for _ in range(2)]
                    t_otmp = [T(), T()]

                    def finish_thunks(g, a, kind):
                        ya, tya = yacc2[a % 2], t_yacc2[a % 2]
                        ob, tob = osb2[kind], t_osb2[kind]
                        bank = obank(a, kind)
                        gcol = 1 + kind
                        ff, tff = f4k[kind], t_f4k[kind]
                        tmp, ttmp = otmp[kind], t_otmp[kind]
                        ta = slice(a * 128, (a + 1) * 128)
                        th = []
                        th.append(lambda: fw.op(act, lambda e: e.activation(out=ob[:], in_=ps[bank][0:65, :], func=AF.Copy), r=[pst[bank]], w=[tob]))

                        def f_tr():
                            for h in range(4):
                                fw.op(pe, lambda e: e.transpose(out=ps[7][:, h * 65:(h + 1) * 65], in_=ob[:, h * 128:(h + 1) * 128],
                                                                identity=ident_f[0:65, 0:65]), r=[tob, t_ident], w=[pst[7]])
                        th.append(f_tr)
                        th.append(lambda: fw.op(dve, lambda e: e.tensor_scalar(out=ff[:], in0=ps[7][:, 64:260:65], scalar1=1e-30, scalar2=None, op0=ALU.max),
                                                r=[pst[7]], w=[tff]))
                        th.append(lambda: fw.op(dve, lambda e: e.reciprocal(out=ff[:], in_=ff[:]), r=[tff], w=[tff]))
                        th.append(lambda: fw.op(dve, lambda e: e.tensor_tensor(out=ff[:], in0=ff[:], in1=gt[:, a, gcol:12:3], op=ALU.mult),
                                                r=[tff, t_gt], w=[tff]))
                        th.append(lambda: fw.op(dve, lambda e: e.tensor_tensor(out=tmp[:], in0=ps[7][:, 0:260].rearrange("p (h c) -> p h c", c=65)[:, :, 0:64],
                                                                               in1=ff[:].unsqueeze(2).to_broadcast([128, 4, 64]), op=ALU.mult),
                                                r=[pst[7], tff], w=[ttmp]))
                        if kind == 0:
                            th.append(lambda: fw.op(pool, lambda e: e.tensor_tensor(out=ya[:], in0=tmp[:].rearrange("p h c -> p (h c)"), in1=ocbuf[:, a, :], op=ALU.add),
                                                    r=[ttmp, t_oc[a]], w=[tya]))
                        else:
                            th.append(lambda: fw.op(pool, lambda e: e.tensor_tensor(out=ya[:], in0=tmp[:].rearrange("p h c -> p (h c)"), in1=ya[:], op=ALU.add),
                                                    r=[ttmp, tya], w=[tya]))

                            def f_out():
                                yb, tyb = ybf[a % 2], t_ybf[a % 2]
                                fw.op(act, lambda e: e.activation(out=yb[:], in_=ya[:], func=AF.Copy), r=[tya], w=[tyb])
                                fw.dma(sp, Y[ta, g * 256:(g + 1) * 256], yb[:], r=[tyb], w=[tY])
                            th.append(f_out)
                        return th

                    for g in range(_NG):
                        for h in range(4):
                            fw.dma(sp, qstack[0:64, h, :], QT[(4 * g + h) * 64:(4 * g + h + 1) * 64, :], r=[tQT], w=[t_q])
                        fw.dma(sp, ks_sel[0:64, :], KsT[g * 64:(g + 1) * 64, :], r=[tKsT], w=[t_ks])
                        fw.dma(sp, kw[0:64, :], KwT[g * 64:(g + 1) * 64, :], r=[tKwT], w=[t_kw])
                        fw.dma(pool, vs_aug[:, :, 0:64], Vs[:, g * 64:(g + 1) * 64].rearrange("(a p) d -> p a d", p=128), r=[tVs], w=[t_vs])
                        fw.dma(pool, vw_aug[:, :, 0:64], Vw[:, g * 64:(g + 1) * 64].rearrange("(a p) d -> p a d", p=128), r=[tVw], w=[t_vw])
                        fw.dma(sp, gt[:], G[:, g * 12:(g + 1) * 12].rearrange("(a p) c -> p a c", p=128), r=[tG], w=[t_gt])
                        for h in range(4):
                            sl = SLOPES[4 * g + h]
                            for dl in range(33):
                                src = d0[:, 0, :] if dl == 0 else (d0[:, 2, :] if dl == 32 else d0[:, 1, :])
                                off = 512.0 if dl == 32 else 128.0 * dl
                                fw.op(act, lambda e: e.activation(out=A[:, dl, h * 128:(h + 1) * 128], in_=src, func=AF.Exp,
                                                                  scale=-sl, bias=-sl * off), r=[t_cc], w=[t_A])
                            fw.op(dve, lambda e: e.tensor_scalar(out=bcb[:, h, :], in0=distc[:], scalar1=-sl, scalar2=None, op0=ALU.mult),
                                  r=[t_cc], w=[t_bc])
                        for a0 in range(0, _NA, 4):
                            chains = [cmp_thunks(g, a0 + ci, ci) for ci in range(4) if a0 + ci < _NA]
                            while chains:
                                for c_ in chains:
                                    c_.pop(0)()
                                chains = [c_ for c_ in chains if c_]
                        LA = 4
                        SB = (0, 1, 2, 6, 5)
                        pend = []
                        for a in range(_NA):
                            ta = slice(a * 128, (a + 1) * 128)
                            steps = [(0, b) for b in range(a + 1)] + [(1, b) for b in range(max(0, a - 4), a + 1)]
                            n = len(steps)
                            per = -(-len(pend) // max(1, n - 1))
                            slots = {}
                            newp = []
                            for i in range(n + LA):
                                if i < n:
                                    kind, b = steps[i]
                                    sbk = SB[kctr[0] % 5]
                                    k3 = kctr[0] % NPT2
                                    kctr[0] += 1
                                    slots[i] = k3
                                    krows = 128
                                    kT, tk = (ks_sel, t_ks) if kind == 0 else (kw, t_kw)
                                    dl = a - b
                                    ai = dl if (kind == 0 or dl < 4) else 32
                                    fw.op(pe, lambda e: e.matmul(out=ps[sbk][:], lhsT=kT[0:krows, b * 128:(b + 1) * 128],
                                                                 rhs=qstack[0:krows, :, ta], start=True, stop=True),
                                          r=[tk, t_q, t_qm[a]], w=[pst[sbk]])
                                    fw.op(act, lambda e: e.activation(out=pt[k3][:], in_=ps[sbk][:], func=AF.Exp), r=[pst[sbk]], w=[t_pt[k3]])
                                    fw.op(dve, lambda e: e.tensor_tensor(out=pa[k3][:], in0=pt[k3][:], in1=A[:, ai, :], op=ALU.mult),
                                          r=[t_pt[k3], t_A], w=[t_pa[k3]])
                                    if i >= 1:
                                        for _ in range(per):
                                            if pend:
                                                pend.pop(0)()
                                if i >= LA:
                                    j = i - LA
                                    kind, b = steps[j]
                                    k3 = slots[j]
                                    vaug, tv = (vs_aug, t_vs) if kind == 0 else (vw_aug, t_vw)
                                    first = (j == 0) or (steps[j - 1][0] != kind)
                                    last = (j == n - 1) or (steps[j + 1][0] != kind)
                                    bank = obank(a, kind)
                                    fw.op(pe, lambda e: e.matmul(out=ps[bank][0:65, :], lhsT=vaug[:, b, :], rhs=pa[k3][:],
                                                                 start=first, stop=last), r=[tv, t_pa[k3]], w=[pst[bank]])
                                    if last:
                                        newp += finish_thunks(g, a, kind)
                            while pend:
                                pend.pop(0)()
                            pend = newp
                        while pend:
                            pend.pop(0)()

        fw.barrier()
        tXB = [T() for _ in range(16)]

        def phase_ffn(layer, outproj, final):
            with contextlib.ExitStack() as st:
                wi = fw.sbuf(st, "wi", [128, 8, 2 * D_FF], BF16)
                wo2 = fw.sbuf(st, "wo2", [128, 22, D], BF16)
                t_wi, t_wo2 = T(), T()
                gf, tgf = load_gain(st, norm_ffn[layer], "gf")
                stg = [fw.sbuf(st, "stgF", [128, 704], F32) for _ in range(4)]
                tstg = [T() for _ in range(4)]
                if outproj:
                    wo = fw.sbuf(st, "wo", [128, 8, D], BF16)
                    t_wo = T()
                    for k in range(8):
                        load_cast(st, wo[:, k, :], t_wo, nsa_w_out[k * 128:(k + 1) * 128, :], 128, D, stg=stg, tstg=tstg)
                for k in range(8):
                    load_cast(st, wi[:, k, :], t_wi, ffn_w_in[layer, k * 128:(k + 1) * 128, :], 128, 2 * D_FF,
                              gain=gf[:, k:k + 1], tgain=tgf, stg=stg, tstg=tstg)
                for f in range(22):
                    load_cast(st, wo2[:, f, :], t_wo2, ffn_w_out[layer, f * 128:(f + 1) * 128, :], 128, D, stg=stg, tstg=tstg)
                if final:
                    gfin = fw.sbuf(st, "gfin", [128, D], F32)
                    t_gfin = T()
                    fw.dma(sp, gfin[:], norm_final.partition_broadcast(128), w=[t_gfin])
                xb = [fw.sbuf(st, "xb", [128, 2, D], F32) for _ in range(2)]
                t_xb = [T(), T()]
                if outproj:
                    ybt = [fw.sbuf(st, "ybt", [128, 2, D], BF16)] * 2
                    t_ybt = [T()] * 2
                    yT = fw.sbuf(st, "yT", [128, 8, 256], BF16)
                    t_yT = T()
                hn = [fw.sbuf(st, "hnF", [128, D], BF16) for _ in range(2)]
                thn = [T(), T()]
                junk = fw.sbuf(st, "junkF", [128, D], BF16)
                tjunk = T()
                sm = [[fw.sbuf(st, "smF", [128, 1], F32) for _ in range(3)] for _ in range(2)]
                tsm = [T(), T()]
                hnT = fw.sbuf(st, "hnTF", [128, 8, 256], BF16)
                thnT = T()
                actT = fw.sbuf(st, "actT", [128, 22, 256], BF16)
                t_actT = T()
                sg = [fw.sbuf(st, "sg", [128, 256], F32) for _ in range(2)]
                t_sg = [T(), T()]
                if final:
                    ob = [fw.sbuf(st, "ob", [128, D], F32) for _ in range(2)]
                    t_ob = [T(), T()]
                tout = T()
                for blk in range(16):
                    s2 = blk % 2
                    rows = slice(blk * 256, (blk + 1) * 256)
                    X, tX = xb[s2], t_xb[s2]
                    if outproj:
                        fw.dma(sp, X[:], x_in[rows, :].rearrange("(u p) d -> p u d", p=128), w=[tX])
                        fw.dma(pool, ybt[s2][:], Y[rows, :].rearrange("(u p) d -> p u d", p=128), r=[tY], w=[t_ybt[s2]])
                        for u in range(2):
                            transpose_to(ybt[s2][:, u, :], t_ybt[s2], yT, t_yT, 8, ps[6 + u], pst[6 + u], u * 128)
                        for u in range(2):
                            for nh in range(2):
                                pb = 4 + nh
                                for k in range(8):
                                    fw.op(pe, lambda e: e.matmul(out=ps[pb][:], lhsT=yT[:, k, u * 128:(u + 1) * 128],
                                                                 rhs=wo[:, k, nh * 512:(nh + 1) * 512], start=(k == 0), stop=(k == 7)),
                                          r=[t_yT, t_wo], w=[pst[pb]])
                                fw.op(dve, lambda e: e.tensor_tensor(out=X[:, u, nh * 512:(nh + 1) * 512], in0=X[:, u, nh * 512:(nh + 1) * 512],
                                                                     in1=ps[pb][:], op=ALU.add), r=[pst[pb], tX], w=[tX])
                    else:
                        fw.dma(sp, X[:], X1[rows, :].rearrange("(u p) d -> p u d", p=128), r=[tXB[blk]], w=[tX])
                    for u in range(2):
                        rmsnorm_bf(X[:, u, :], tX, hn[u][:], thn[u], junk[:], tjunk, [z[:] for z in sm[u]], tsm[u])
                        transpose_to(hn[u], thn[u], hnT, thnT, 8, ps[6 + u], pst[6 + u], u * 128)
                    for f in range(22):
                        pb = f % 4
                        for half in range(2):
                            c0 = half * D_FF + f * 128
                            for k in range(8):
                                fw.op(pe, lambda e: e.matmul(out=ps[pb][:, half * 256:(half + 1) * 256], lhsT=wi[:, k, c0:c0 + 128],
                                                             rhs=hnT[:, k, :], start=(k == 0), stop=(k == 7)), r=[t_wi, thnT], w=[pst[pb]])
                        fw.op(act, lambda e: e.activation(out=sg[f % 2][:], in_=ps[pb][:, 0:256], func=AF.Silu), r=[pst[pb]], w=[t_sg[f % 2]])
                        fw.op(dve, lambda e: e.tensor_tensor(out=actT[:, f, :], in0=sg[f % 2][:], in1=ps[pb][:, 256:512], op=ALU.mult),
                              r=[pst[pb], t_sg[f % 2]], w=[t_actT])
                    for u in range(2):
                        for nh in range(2):
                            pb = 4 + nh
                            for f in range(22):
                                fw.op(pe, lambda e: e.matmul(out=ps[pb][:], lhsT=actT[:, f, u * 128:(u + 1) * 128],
                                                             rhs=wo2[:, f, nh * 512:(nh + 1) * 512], start=(f == 0), stop=(f == 21)),
                                      r=[t_actT, t_wo2], w=[pst[pb]])
                            fw.op(dve, lambda e: e.tensor_tensor(out=X[:, u, nh * 512:(nh + 1) * 512], in0=X[:, u, nh * 512:(nh + 1) * 512],
                                                                 in1=ps[pb][:], op=ALU.add), r=[pst[pb], tX], w=[tX])
                    if final:
                        for u in range(2):
                            ss, sd, rs = [z[:] for z in sm[u]]
                            fw.op(act, lambda e: e.activation(out=junk[:], in_=X[:, u, :], func=AF.Square, accum_out=ss), r=[tX], w=[tjunk, tsm[u]])
                            fw.op(act, lambda e: e.activation(out=sd, in_=ss, func=AF.Sqrt, bias=epsc[:], scale=1.0 / D), r=[tsm[u], t_eps], w=[tsm[u]])
                            fw.op(dve, lambda e: e.reciprocal(out=rs, in_=sd), r=[tsm[u]], w=[tsm[u]])
                            fw.op(dve, lambda e: e.scalar_tensor_tensor(out=ob[u][:], in0=X[:, u, :], scalar=rs, in1=gfin[:], op0=ALU.mult, op1=ALU.mult),
                                  r=[tX, tsm[u], t_gfin], w=[t_ob[u]])
                            fw.dma(sp, out_ap[blk * 256 + u * 128:blk * 256 + (u + 1) * 128, :], ob[u][:], r=[t_ob[u]], w=[tout])
                    else:
                        fw.dma(sp, X1[rows, :].rearrange("(u p) d -> p u d", p=128), X[:], r=[tX], w=[tXB[blk]])
                fw.barrier()

        if stage >= 4:
            phase_ffn(0, True, False)

        if stage >= 5:
            with contextlib.ExitStack() as st:
                NB = 512
                NU = NB // 128
                wl = fw.sbuf(st, "wl", [128, 8, 2 * D_RNN], BF16)
                t_wl = T()
                gl, tgl = load_gain(st, norm_mix[1], "gl")
                wa = fw.sbuf(st, "wa", [88, 16, 176], BF16)
                wx = fw.sbuf(st, "wx", [88, 16, 176], BF16)
                wlo = fw.sbuf(st, "wlo", [88, 16, D], BF16)
                t_wa, t_wx, t_wlo = T(), T(), T()
                cw = fw.sbuf(st, "cw", [88, 16, 4], F32)
                cb = fw.sbuf(st, "cb", [88, 16], F32)
                nba = fw.sbuf(st, "nba", [88, 16], F32)
                nbx = fw.sbuf(st, "nbx", [88, 16], F32)
                nsp = fw.sbuf(st, "nsp", [88, 16], F32)
                t_small = T()
                for j in range(4):
                    fw.dma(sp, cw[:, :, j], lru_conv_w[j].rearrange("(c p) -> p c", p=88), w=[t_small], allow_slow_non_contiguous=True)
                fw.dma(sp, cb[:], lru_conv_b.rearrange("(c p) -> p c", p=88), w=[t_small], allow_slow_non_contiguous=True)
                fw.dma(sp, nba[:], lru_b_a.rearrange("(c p) -> p c", p=88), w=[t_small], allow_slow_non_contiguous=True)
                fw.dma(sp, nbx[:], lru_b_x.rearrange("(c p) -> p c", p=88), w=[t_small], allow_slow_non_contiguous=True)
                fw.dma(sp, nsp[:], lru_lambda.rearrange("(c p) -> p c", p=88), w=[t_small], allow_slow_non_contiguous=True)
                fw.op(dve, lambda e: e.tensor_scalar(out=nba[:], in0=nba[:], scalar1=-1.0, scalar2=None, op0=ALU.mult), r=[t_small], w=[t_small])
                fw.op(dve, lambda e: e.tensor_scalar(out=nbx[:], in0=nbx[:], scalar1=-1.0, scalar2=None, op0=ALU.mult), r=[t_small], w=[t_small])
                fw.op(act, lambda e: e.activation(out=nsp[:], in_=nsp[:], func=AF.Exp, scale=-1.0), r=[t_small], w=[t_small])
                fw.op(act, lambda e: e.activation(out=nsp[:], in_=nsp[:], func=AF.Ln, bias=1.0), r=[t_small], w=[t_small])
                fw.op(dve, lambda e: e.tensor_scalar(out=nsp[:], in0=nsp[:], scalar1=-8.0, scalar2=None, op0=ALU.mult), r=[t_small], w=[t_small])
                halo = fw.sbuf(st, "halo", [88, 16, 3], F32)
                hlast = fw.sbuf(st, "hlast", [88, 16], F32)
                t_halo, t_hl = T(), T()
                fw.op(pool, lambda e: e.memset(halo[:], 0.0), w=[t_halo])
                fw.op(pool, lambda e: e.memset(hlast[:], 0.0), w=[t_hl])
                X2 = [fw.sbuf(st, "xbL", [128, NU, D], F32) for _ in range(2)]
                tX2 = [T(), T()]
                cur = [0]
                hn = [fw.sbuf(st, "hnL", [128, D], BF16) for _ in range(2)]
                thn = [T(), T()]
                junk = fw.sbuf(st, "junkL", [128, D], BF16)
                tjunk = T()
                sm = [[fw.sbuf(st, "smL", [128, 1], F32) for _ in range(3)] for _ in range(2)]
                tsm = [T(), T()]
                hnT_1 = fw.sbuf(st, "hnTL", [128, 8, NB], BF16)
                thnT_1 = T()
                hnT2 = [hnT_1, hnT_1]
                thnT2 = [thnT_1, thnT_1]
                g16 = [fw.sbuf(st, "gate16", [88, 16, NB], BF16) for _ in range(2)]
                t_g16_2 = [T(), T()]

                with contextlib.ExitStack() as st_w:
                    stg = [fw.sbuf(st_w, "stgL", [128, 704], F32) for _ in range(4)]
                    tstg = [T() for _ in range(4)]
                    for k in range(8):
                        load_cast(st_w, wl[:, k, :], t_wl, lru_w_in[k * 128:(k + 1) * 128, :], 128, 2 * D_RNN,
                                  gain=gl[:, k:k + 1], tgain=tgl, stg=stg, tstg=tstg)
                    for (dst, tdst, src) in ((wa, t_wa, lru_w_a), (wx, t_wx, lru_w_x)):
                        for n in range(8):
                            for ih in range(2):
                                load_cast(st_w, dst[:, 2 * n + ih, :], tdst, src[n, ih * 88:(ih + 1) * 88, :], 88, 176, stg=stg, tstg=tstg)
                    for c in range(16):
                        load_cast(st_w, wlo[:, c, :], t_wlo, lru_w_out[c * 88:(c + 1) * 88, :], 88, D, stg=stg, tstg=tstg)
                    fw.barrier()

                def mk(name, dt=F32, w=NB, nbuf=1):
                    return [fw.sbuf(st, name, [88, 2, w], dt) for _ in range(nbuf)], [[T(), T()] for _ in range(nbuf)]
                recb, t_recb = mk("recb", F32, NB + 3, 2)
                xr, t_xr = mk("xr", F32, NB, 2)
                xrb, t_xrb = mk("xrb", BF16, NB, 2)
                rr_, t_rr = mk("rr")
                ii_, t_ii = mk("ii")
                aa, t_aa = mk("aa")
                uu, t_uu = mk("uu")
                hh, t_hh = mk("hh")
                rr_, ii_, aa, uu, hh = rr_[0], ii_[0], aa[0], uu[0], hh[0]
                t_rr, t_ii, t_aa, t_uu, t_hh = t_rr[0], t_ii[0], t_aa[0], t_uu[0], t_hh[0]

                def conv_chain(n, half):
                    q = n % 2
                    c = 2 * n + half
                    pb = half
                    RB, XR, XB = recb[q], xr[q], xrb[q]
                    tRB, tXR, tXB_ = t_recb[q][half], t_xr[q][half], t_xrb[q][half]
                    th = []

                    def f0():
                        for k in range(8):
                            fw.op(pe, lambda e: e.matmul(out=ps[pb][0:88, :], lhsT=wl[:, k, D_RNN + c * 88:D_RNN + (c + 1) * 88], rhs=hnT2[cur[0] % 2][:, k, :],
                                                         start=(k == 0), stop=(k == 7)), r=[t_wl, thnT2[cur[0] % 2]], w=[pst[pb]])
                        fw.op(pool, lambda e: e.tensor_copy(out=RB[:, half, 0:3], in_=halo[:, c, :]), r=[t_halo], w=[tRB])
                    th.append(f0)

                    def f1():
                        fw.op(act, lambda e: e.activation(out=RB[:, half, 3:3 + NB], in_=ps[pb][0:88, :], func=AF.Copy), r=[pst[pb]], w=[tRB])
                        fw.op(pool, lambda e: e.tensor_copy(out=halo[:, c, :], in_=RB[:, half, NB:NB + 3]), r=[tRB], w=[t_halo])
                    th.append(f1)
                    th.append(lambda: fw.op(dve, lambda e: e.tensor_scalar(out=XR[:, half, :], in0=RB[:, half, 0:NB], scalar1=cw[:, c, 0:1],
                                                                           scalar2=cb[:, c:c + 1], op0=ALU.mult, op1=ALU.add),
                                            r=[tRB, t_small], w=[tXR]))
                    for j in range(1, 4):
                        th.append(lambda j=j: fw.op(dve, lambda e: e.scalar_tensor_tensor(out=XR[:, half, :], in0=RB[:, half, j:j + NB], scalar=cw[:, c, j:j + 1],
                                                                                           in1=XR[:, half, :], op0=ALU.mult, op1=ALU.add),
                                                    r=[tRB, t_small, tXR], w=[tXR]))
                    th.append(lambda: fw.op(pool, lambda e: e.tensor_copy(out=XB[:, half, :], in_=XR[:, half, :]), r=[tXR], w=[tXB_]))
                    return th

                def gate_chain(n, oh):
                    q = n % 2
                    XR, XB = xr[q], xrb[q]
                    c = 2 * n + oh
                    pr, pi = 2 + 2 * oh, 3 + 2 * oh
                    R_, I_, A_, U_, H_ = rr_[:, oh, :], ii_[:, oh, :], aa[:, oh, :], uu[:, oh, :], hh[:, oh, :]
                    tr, ti, ta_, tu, th_ = [t_rr[oh]], [t_ii[oh]], [t_aa[oh]], [t_uu[oh]], [t_hh[oh]]
                    th = []

                    def f0():
                        for (wgt, twg, pbG) in ((wa, t_wa, pr), (wx, t_wx, pi)):
                            for ih in range(2):
                                fw.op(pe, lambda e: e.matmul(out=ps[pbG][0:88, :], lhsT=wgt[:, 2 * n + ih, oh * 88:(oh + 1) * 88],
                                                             rhs=XB[:, ih, :], start=(ih == 0), stop=(ih == 1)),
                                      r=[twg, t_xrb[q][0], t_xrb[q][1]], w=[pst[pbG]])
                    th.append(f0)
                    th.append(lambda: fw.op(act, lambda e: e.activation(out=R_, in_=ps[pr][0:88, :], func=AF.Exp, bias=nba[:, c:c + 1], scale=-1.0),
                                            r=[pst[pr], t_small], w=tr))
                    th.append(lambda: fw.op(act, lambda e: e.activation(out=I_, in_=ps[pi][0:88, :], func=AF.Exp, bias=nbx[:, c:c + 1], scale=-1.0),
                                            r=[pst[pi], t_small], w=ti))
                    th.append(lambda: fw.op(act, lambda e: e.activation(out=R_, in_=R_, func=AF.Ln, bias=1.0), r=tr, w=tr))
                    th.append(lambda: fw.op(act, lambda e: e.activation(out=I_, in_=I_, func=AF.Ln, bias=1.0), r=ti, w=ti))
                    th.append(lambda: fw.op(act, lambda e: e.activation(out=R_, in_=R_, func=AF.Exp, scale=-1.0), r=tr, w=tr))
                    th.append(lambda: fw.op(act, lambda e: e.activation(out=I_, in_=I_, func=AF.Exp, scale=-1.0), r=ti, w=ti))
                    th.append(lambda: fw.op(act, lambda e: e.activation(out=A_, in_=R_, func=AF.Exp, scale=nsp[:, c:c + 1]), r=tr + [t_small], w=ta_))
                    th.append(lambda: fw.op(pool, lambda e: e.tensor_tensor(out=I_, in0=I_, in1=XR[:, oh, :], op=ALU.mult), r=ti + [t_xr[q][oh]], w=ti))
                    th.append(lambda: fw.op(dve, lambda e: e.scalar_tensor_tensor(out=U_, in0=A_, scalar=-1.0, in1=A_, op0=ALU.mult, op1=ALU.mult),
                                            r=ta_, w=tu))
                    th.append(lambda: fw.op(dve, lambda e: e.tensor_scalar(out=U_, in0=U_, scalar1=1.0, scalar2=1e-30, op0=ALU.add, op1=ALU.max),
                                            r=tu, w=tu))
                    th.append(lambda: fw.op(act, lambda e: e.activation(out=U_, in_=U_, func=AF.Ln), r=tu, w=tu))
                    th.append(lambda: fw.op(act, lambda e: e.activation(out=U_, in_=U_, func=AF.Exp, scale=0.5), r=tu, w=tu))
                    th.append(lambda: fw.op(dve, lambda e: e.tensor_tensor(out=U_, in0=U_, in1=I_, op=ALU.mult), r=tu + ti, w=tu))

                    def f_scan():
                        fw.op(dve, lambda e: e.tensor_tensor_scan(out=H_, data0=A_, data1=U_, initial=hlast[:, c:c + 1], op0=ALU.mult, op1=ALU.add),
                              r=ta_ + tu + [t_hl], w=th_)
                        fw.op(dve, lambda e: e.tensor_copy(out=hlast[:, c:c + 1], in_=hh[:, oh, NB - 1:NB]), r=th_, w=[t_hl])
                    th.append(f_scan)
                    th.append(lambda: fw.op(pool, lambda e: e.tensor_tensor(out=g16[cur[0] % 2][:, c, :], in0=H_, in1=g16[cur[0] % 2][:, c, :], op=ALU.mult),
                                            r=th_, w=[t_g16_2[cur[0] % 2]]))
                    return th

                def zip_emit(chains):
                    chains = [c for c in chains if c]
                    while chains:
                        for c in chains:
                            c.pop(0)()
                        chains = [c for c in chains if c]

                def blk_rows(b):
                    return slice(b * NB, (b + 1) * NB), [tXB[b * (NB // 256) + q] for q in range(NB // 256)]

                def prep_load(b):
                    rows, tblks = blk_rows(b)
                    fw.dma(sp, X2[b % 2][:], X1[rows, :].rearrange("(u p) d -> p u d", p=128), r=tblks, w=[tX2[b % 2]])

                def prep_norm(b, u):
                    rmsnorm_bf(X2[b % 2][:, u, :], tX2[b % 2], hn[u % 2][:], thn[u % 2], junk[:], tjunk, [z[:] for z in sm[u % 2]], tsm[u % 2])
                    transpose_to(hn[u % 2], thn[u % 2], hnT2[b % 2], thnT2[b % 2], 8, ps[6 + u % 2], pst[6 + u % 2], u * 128)

                def gelu_stage(b):
                    for c in range(16):
                        pb = c % 2
                        for k in range(8):
                            fw.op(pe, lambda e: e.matmul(out=ps[pb][0:88, :], lhsT=wl[:, k, c * 88:(c + 1) * 88], rhs=hnT2[b % 2][:, k, :],
                                                         start=(k == 0), stop=(k == 7)), r=[t_wl, thnT2[b % 2]], w=[pst[pb]])
                        fw.op(act, lambda e: e.activation(out=g16[b % 2][:, c, :], in_=ps[pb][0:88, :], func=AF.Gelu_apprx_tanh),
                              r=[pst[pb]], w=[t_g16_2[b % 2]])

                def outproj_unit(b, u, nh):
                    pb = 6 + nh
                    Xb, tXb = X2[b % 2], tX2[b % 2]
                    for c in range(16):
                        fw.op(pe, lambda e: e.matmul(out=ps[pb][:], lhsT=g16[b % 2][:, c, u * 128:(u + 1) * 128],
                                                     rhs=wlo[:, c, nh * 512:(nh + 1) * 512], start=(c == 0), stop=(c == 15)),
                              r=[t_g16_2[b % 2], t_wlo], w=[pst[pb]])
                    fw.op(dve, lambda e: e.tensor_tensor(out=Xb[:, u, nh * 512:(nh + 1) * 512], in0=Xb[:, u, nh * 512:(nh + 1) * 512],
                                                         in1=ps[pb][:], op=ALU.add), r=[pst[pb], tXb], w=[tXb])

                def store(b):
                    rows, tblks = blk_rows(b)
                    fw.dma(sp, X1[rows, :].rearrange("(u p) d -> p u d", p=128), X2[b % 2][:], r=[tX2[b % 2]], w=tblks)

                NBLK = S // NB
                prep_load(0)
                for u in range(NU):
                    prep_norm(0, u)
                gelu_stage(0)
                for blk in range(NBLK):
                    cur[0] = blk
                    zip_emit([conv_chain(0, 0), conv_chain(0, 1)])
                    for n in range(8):
                        chains = [gate_chain(n, 0), gate_chain(n, 1)]
                        if n + 1 < 8:
                            chains += [conv_chain(n + 1, 0), conv_chain(n + 1, 1)]
                        zip_emit(chains)
                        if blk >= 1 and n < NU:
                            outproj_unit(blk - 1, n, 0)
                            outproj_unit(blk - 1, n, 1)
                            if n == NU - 1:
                                store(blk - 1)
                        if blk + 1 < NBLK:
                            if n == NU - 1:
                                prep_load(blk + 1)
                            if n in (6, 7):
                                prep_norm(blk + 1, 2 * (n - 6))
                                prep_norm(blk + 1, 2 * (n - 6) + 1)
                    if blk + 1 < NBLK:
                        gelu_stage(blk + 1)
                for u in range(NU):
                    outproj_unit(NBLK - 1, u, 0)
                    outproj_unit(NBLK - 1, u, 1)
                store(NBLK - 1)
                fw.barrier()

        if stage >= 6:
            phase_ffn(1, False, True)

        fw.finish()
    return nc


_CONSTS = None


def make_in_map(inp, b):
    global _CONSTS
    if _CONSTS is None:
        _CONSTS = host_consts()
    f = lambda a: np.ascontiguousarray(np.asarray(a, dtype=np.float32))
    m = {
        "x": f(inp["x"][b]),
        "norm_mix": f(inp["norm_mix"]), "norm_ffn": f(inp["norm_ffn"]), "norm_final": f(inp["norm_final"]),
        "nsa_w_in": f(inp["nsa_w_in"][0]), "nsa_b_gate": f(inp["nsa_b_gate"][0]),
        "nsa_cmp_pos": f(inp["nsa_cmp_pos"][0]), "nsa_cmp_w1": f(inp["nsa_cmp_w1"][0]),
        "nsa_cmp_b1": f(inp["nsa_cmp_b1"][0]), "nsa_cmp_w2": f(inp["nsa_cmp_w2"][0]),
        "nsa_cmp_b2": f(inp["nsa_cmp_b2"][0]), "nsa_w_out": f(inp["nsa_w_out"][0]),
        "lru_w_in": f(inp["lru_w_in"][0]), "lru_conv_w": f(inp["lru_conv_w"][0]),
        "lru_conv_b": f(inp["lru_conv_b"][0]), "lru_w_a": f(inp["lru_w_a"][0]), "lru_b_a": f(inp["lru_b_a"][0]),
        "lru_w_x": f(inp["lru_w_x"][0]), "lru_b_x": f(inp["lru_b_x"][0]), "lru_lambda": f(inp["lru_lambda"][0]),
        "lru_w_out": f(inp["lru_w_out"][0]), "ffn_w_in": f(inp["ffn_w_in"]), "ffn_w_out": f(inp["ffn_w_out"]),
    }
    m.update(_CONSTS)
    return m


def kernel(**inputs):
    nc = build(debug=False)
    n = 4
    maps = [make_in_map(inputs, b) for b in range(n)]
    res = run_bass_kernel_spmd(nc, maps, core_ids=list(range(n)))
    return np.stack([np.asarray(res.results[b]["out"], dtype=np.float32) for b in range(n)], axis=0)
```

```python
import contextlib
import numpy as np
import ml_dtypes
import concourse.bass as bass
import concourse.mybir as mybir
from concourse.bass_utils import run_bass_kernel_spmd

F32 = mybir.dt.float32
BF16 = mybir.dt.bfloat16
AF = mybir.ActivationFunctionType
ALU = mybir.AluOpType
AX = mybir.AxisListType

S = 4096
D = 1024
NT = S // 128
NSA_IN = 2608
D_RNN = 1408
D_FF = 2816
EPS = 1e-6
SLOPES = [2.0 ** (-8.0 * (h + 1) / 16) for h in range(16)]
BIGD = 1.0e6


class Dom:
    def __init__(self, fw, name, unit):
        self.sem = fw.es.enter_context(fw.nc.semaphore(name))
        self.unit = unit
        self.count = 0


class T:
    __slots__ = ("w", "r", "dd")

    def __init__(self):
        self.w = None
        self.r = {}
        self.dd = None


class Eng:
    def __init__(self, fw, name, eng, is_pe=False, has_dom=True):
        self.name = name
        self.eng = eng
        self.is_pe = is_pe
        self.dom = Dom(fw, "c_" + name, 1) if has_dom else None
        self.known = {}


class FW:
    def __init__(self, nc):
        self.nc = nc
        self.es = contextlib.ExitStack()
        self.pe = Eng(self, "pe", nc.tensor, is_pe=True)
        self.act = Eng(self, "act", nc.scalar)
        self.dve = Eng(self, "dve", nc.vector)
        self.pool = Eng(self, "pool", nc.gpsimd)
        self.sp = Eng(self, "sp", nc.sync, has_dom=False)
        self.dma_doms = []
        self.free_doms = []
        self.uid = 0

    def sbuf(self, st, name, shape, dt):
        self.uid += 1
        return st.enter_context(self.nc.sbuf_tensor("%s_%d" % (name, self.uid), list(shape), dt))

    def _waits(self, E, r, w):
        deps = {}
        for t in r:
            if t.w is not None and deps.get(t.w[0], 0) < t.w[1]:
                deps[t.w[0]] = t.w[1]
        for t in w:
            if t.w is not None and deps.get(t.w[0], 0) < t.w[1]:
                deps[t.w[0]] = t.w[1]
            for d, s in t.r.items():
                if deps.get(d, 0) < s:
                    deps[d] = s
        for d, s in deps.items():
            if E.is_pe and d is E.dom:
                continue
            if E.known.get(d, 0) >= s:
                continue
            E.eng.wait_ge(d.sem, s * d.unit)
            E.known[d] = s

    def op(self, E, fn, r=(), w=()):
        self._waits(E, r, w)
        ins = fn(E.eng)
        d = E.dom
        d.count += 1
        ins.then_inc(d.sem, 1)
        for t in r:
            t.r[d] = d.count
        for t in w:
            t.w = (d, d.count)
            t.r = {}
        return ins

    def dma(self, E, out, in_, r=(), w=(), **kw):
        self._waits(E, r, w)
        t0 = w[0] if len(w) else r[0]
        if t0.dd is None:
            t0.dd = Dom(self, "d%d" % len(self.dma_doms), 16)
            self.dma_doms.append(t0.dd)
        d = t0.dd
        ins = E.eng.dma_start(out=out, in_=in_, **kw)
        d.count += 1
        ins.then_inc(d.sem, 16)
        for t in r:
            t.r[d] = d.count
        for t in w:
            t.w = (d, d.count)
            t.r = {}
        return ins

    def barrier(self):
        doms = [d for d in self.dma_doms if d.count] + [X.dom for X in (self.pe, self.act, self.dve, self.pool) if X.dom.count]
        for E in (self.pe, self.act, self.dve, self.pool, self.sp):
            for d in doms:
                if E.known.get(d, 0) < d.count:
                    E.eng.wait_ge(d.sem, d.count * d.unit)
                    E.known[d] = d.count

    def finish(self):
        E = self.sp
        for d in self.dma_doms:
            if d.count:
                E.eng.wait_ge(d.sem, d.count * d.unit)
        for X in (self.pe, self.act, self.dve, self.pool):
            if X.dom.count:
                E.eng.wait_ge(X.dom.sem, X.dom.count)


def host_consts():
    c = {}
    c["ident_bf"] = np.eye(128, dtype=np.float32).astype(ml_dtypes.bfloat16)
    c["ident_f"] = np.eye(128, dtype=np.float32)
    e = np.zeros((64, S), np.float32)
    for j in range(64):
        e[j, j * 64:(j + 1) * 64] = 256.0
    c["e256"] = e.astype(ml_dtypes.bfloat16)
    sr = np.arange(128)[:, None].astype(np.float32)
    tr = np.arange(128)[None, :].astype(np.float32)
    d0 = tr - sr
    c["d0"] = np.stack([np.where(d0 >= 0, d0, BIGD), d0, np.where(d0 < 0, d0, BIGD)]).astype(np.float32)
    i = np.arange(128)[:, None]
    m = np.arange(-248, 256)[None, :]
    dc = (i - 16 * m - 31).astype(np.float32)
    c["distc"] = np.where(dc >= 0, dc, BIGD).astype(np.float32)
    jp = np.arange(-62, 64)[None, :]
    ci = (np.arange(128)[:, None] // 64)
    fb = np.zeros((128, 126), np.float32)
    fb = np.where((jp == ci) | (jp == ci - 1), 100.0, fb)
    fb = np.where(jp > ci, -100.0, fb)
    c["fbias"] = fb.astype(np.float32)
    ov = np.zeros((256, 64), np.float32)
    for n in range(255):
        for j in range(64):
            if 16 * n < 64 * j + 64 and 16 * n + 32 > 64 * j:
                ov[n, j] = 1.0
    c["ovl"] = ov.reshape(2, 128, 64).astype(ml_dtypes.bfloat16)
    return c


def build(debug=False, stage=99):
    nc = bass.Bass("TRN2", target_bir_lowering=False)
    fw = FW(nc)
    pe, act, dve, pool, sp = fw.pe, fw.act, fw.dve, fw.pool, fw.sp

    def din(name, shape, dt=F32):
        return nc.dram_tensor(name, list(shape), dt, kind="ExternalInput").ap()

    def dscr(name, shape, dt):
        return nc.dram_tensor(name, list(shape), dt, kind="ExternalOutput" if debug else "Internal").ap()

    x_in = din("x", [S, D])
    norm_mix = din("norm_mix", [2, D])
    norm_ffn = din("norm_ffn", [2, D])
    norm_final = din("norm_final", [D])
    nsa_w_in = din("nsa_w_in", [D, NSA_IN])
    nsa_b_gate = din("nsa_b_gate", [48])
    nsa_cmp_pos = din("nsa_cmp_pos", [2, 32, 64])
    nsa_cmp_w1 = din("nsa_cmp_w1", [2, 2048, 256])
    nsa_cmp_b1 = din("nsa_cmp_b1", [2, 256])
    nsa_cmp_w2 = din("nsa_cmp_w2", [2, 256, 64])
    nsa_cmp_b2 = din("nsa_cmp_b2", [2, 64])
    nsa_w_out = din("nsa_w_out", [D, D])
    lru_w_in = din("lru_w_in", [D, 2 * D_RNN])
    lru_conv_w = din("lru_conv_w", [4, D_RNN])
    lru_conv_b = din("lru_conv_b", [D_RNN])
    lru_w_a = din("lru_w_a", [8, 176, 176])
    lru_b_a = din("lru_b_a", [D_RNN])
    lru_w_x = din("lru_w_x", [8, 176, 176])
    lru_b_x = din("lru_b_x", [D_RNN])
    lru_lambda = din("lru_lambda", [D_RNN])
    lru_w_out = din("lru_w_out", [D_RNN, D])
    ffn_w_in = din("ffn_w_in", [2, D, 2 * D_FF])
    ffn_w_out = din("ffn_w_out", [2, D_FF, D])
    c_ident_bf = din("ident_bf", [128, 128], BF16)
    c_ident_f = din("ident_f", [128, 128])
    c_e256 = din("e256", [64, S], BF16)
    c_d0 = din("d0", [3, 128, 128])
    c_distc = din("distc", [128, 504])
    c_fbias = din("fbias", [128, 126])
    c_ovl = din("ovl", [2, 128, 64], BF16)

    out_ap = nc.dram_tensor("out", [S, D], F32, kind="ExternalOutput").ap()

    QT = dscr("QT", [1024, S], BF16)
    KcT = dscr("KcT", [256, S], BF16)
    VcT = dscr("VcT", [256, S], BF16)
    KsT = dscr("KsT", [256, S], BF16)
    KwT = dscr("KwT", [256, S], BF16)
    Vs = dscr("Vs", [S, 256], BF16)
    Vw = dscr("Vw", [S, 256], BF16)
    G = dscr("G", [S, 48], F32)
    Y = dscr("Y", [S, D], BF16)
    X1 = dscr("X1", [S, D], F32)
    tQT, tKcT, tVcT, tKsT, tKwT, tVs, tVw, tG, tY, tX1 = [T() for _ in range(10)]

    with fw.es:
        gst = fw.es
        ps = [gst.enter_context(nc.psum_tensor("ps%d" % i, [128, 512], F32)) for i in range(8)]
        pst = [T() for _ in range(8)]
        ident_bf = fw.sbuf(gst, "identbf", [128, 128], BF16)
        ident_f = fw.sbuf(gst, "identf", [128, 128], F32)
        t_ident = T()
        fw.dma(sp, ident_bf[:], c_ident_bf, w=[t_ident])
        fw.dma(sp, ident_f[:], c_ident_f, w=[t_ident])
        epsc = fw.sbuf(gst, "epsc", [128, 1], F32)
        t_eps = T()
        fw.op(dve, lambda e: e.memset(epsc[:], EPS), w=[t_eps])

        rr = [0]

        def evac(out, in_, r, w, scale=None):
            rr[0] += 1
            if rr[0] % 2 == 0:
                if scale is None:
                    fw.op(act, lambda e: e.activation(out=out, in_=in_, func=AF.Copy), r=r, w=w)
                else:
                    fw.op(act, lambda e: e.activation(out=out, in_=in_, func=AF.Copy, scale=scale), r=r, w=w)
            else:
                if scale is None:
                    fw.op(dve, lambda e: e.tensor_copy(out=out, in_=in_), r=r, w=w)
                else:
                    fw.op(dve, lambda e: e.tensor_scalar(out=out, in0=in_, scalar1=scale, scalar2=None, op0=ALU.mult), r=r, w=w)

        def load_gain(st, g_ap, name):
            gt = fw.sbuf(st, name, [128, 8], F32)
            tg = T()
            fw.dma(sp, gt[:], g_ap.rearrange("(k p) -> p k", p=128), w=[tg], allow_slow_non_contiguous=True)
            return gt, tg

        cast_rr = [0]

        def load_cast(st, dst, tdst, src, nrows, ncols, gain=None, tgain=None, stg=None, tstg=None):
            CH = stg[0].shape[1]
            for c0 in range(0, ncols, CH):
                cw = min(CH, ncols - c0)
                k = cast_rr[0] % len(stg)
                cast_rr[0] += 1
                s_, ts_ = stg[k], tstg[k]
                fw.dma((sp, pool, act, sp)[k % 4], s_[0:nrows, 0:cw], src[:, c0:c0 + cw], w=[ts_])
                if k % 2 == 0:
                    if gain is None:
                        fw.op(dve, lambda e: e.tensor_copy(out=dst[:, c0:c0 + cw], in_=s_[0:nrows, 0:cw]), r=[ts_], w=[tdst])
                    else:
                        fw.op(dve, lambda e: e.tensor_scalar(out=dst[:, c0:c0 + cw], in0=s_[0:nrows, 0:cw], scalar1=gain,
                                                             scalar2=None, op0=ALU.mult), r=[ts_, tgain], w=[tdst])
                else:
                    if gain is None:
                        fw.op(act, lambda e: e.activation(out=dst[:, c0:c0 + cw], in_=s_[0:nrows, 0:cw], func=AF.Copy), r=[ts_], w=[tdst])
                    else:
                        fw.op(act, lambda e: e.activation(out=dst[:, c0:c0 + cw], in_=s_[0:nrows, 0:cw], func=AF.Copy, scale=gain),
                              r=[ts_, tgain], w=[tdst])

        def rmsnorm_bf(xt, tx, hn, thn, junk, tjunk, st_small, tsm):
            ss, sd, rs = st_small
            fw.op(act, lambda e: e.activation(out=junk, in_=xt, func=AF.Square, accum_out=ss), r=[tx], w=[tjunk, tsm])
            fw.op(act, lambda e: e.activation(out=sd, in_=ss, func=AF.Sqrt, bias=epsc[:], scale=1.0 / D), r=[tsm, t_eps], w=[tsm])
            fw.op(dve, lambda e: e.reciprocal(out=rs, in_=sd), r=[tsm], w=[tsm])
            fw.op(dve, lambda e: e.tensor_scalar(out=hn, in0=xt, scalar1=rs, scalar2=None, op0=ALU.mult), r=[tx, tsm], w=[thn])

        def transpose_to(hn, thn, dstT, tdst, nchunk, pbank, tpbank, col0, rows=128):
            pv = pbank[:].bitcast(BF16)
            for k0 in range(0, nchunk, 8):
                kn = min(8, nchunk - k0)
                for k in range(kn):
                    fw.op(pe, lambda e: e.transpose(out=pv[:, k * 128:(k + 1) * 128], in_=hn[:, (k0 + k) * 128:(k0 + k + 1) * 128],
                                                    identity=ident_bf[:]), r=[thn, t_ident], w=[tpbank])
                evac(dstT[:, k0:k0 + kn, col0:col0 + 128], pv[:, 0:kn * 128].rearrange("p (k t) -> p k t", k=kn), r=[tpbank], w=[tdst])

        if stage >= 1:
            with contextlib.ExitStack() as st:
                w_in = fw.sbuf(st, "w_in", [128, 8, NSA_IN], BF16)
                t_w = T()
                g0, tg0 = load_gain(st, norm_mix[0], "g0")
                stg = [fw.sbuf(st, "stg", [128, 2608], F32) for _ in range(2)]
                tstg = [T(), T()]
                for k in range(8):
                    load_cast(st, w_in[:, k, :], t_w, nsa_w_in[k * 128:(k + 1) * 128, :], 128, NSA_IN,
                              gain=g0[:, k:k + 1], tgain=tg0, stg=stg, tstg=tstg)
                bg = fw.sbuf(st, "bg", [128, 48], F32)
                t_bg = T()
                fw.dma(sp, bg[:], nsa_b_gate.partition_broadcast(128), w=[t_bg])
                xt = [fw.sbuf(st, "xt", [128, D], F32) for _ in range(2)]
                txt = [T(), T()]
                hn = [fw.sbuf(st, "hn", [128, D], BF16) for _ in range(2)]
                thn = [T(), T()]
                junk = fw.sbuf(st, "junk", [128, D], BF16)
                tjunk = T()
                sm = [[fw.sbuf(st, "sm", [128, 1], F32) for _ in range(3)] for _ in range(2)]
                tsm = [T(), T()]
                hnT = [fw.sbuf(st, "hnT", [128, 8, 512], BF16) for _ in range(2)]
                thnT = [T(), T()]
                fst = [fw.sbuf(st, "fst", [128, 16, 512], BF16) for _ in range(2)]
                tfst = [T(), T()]
                tst = [fw.sbuf(st, "tst", [128, 4, 512], BF16) for _ in range(2)]
                ttst = [T(), T()]
                gst_ = [fw.sbuf(st, "gst", [128, 4, 48], F32) for _ in range(2)]
                tgst = [T(), T()]
                fcols = [c * 128 for c in range(8)] + [1024, 1152, 1280, 1408, 1536, 1664, 2048, 2176]
                it = 0
                for R in range(8):
                    b = R % 2
                    for u in range(4):
                        ti = 4 * R + u
                        s2 = it % 2
                        it += 1
                        fw.dma(sp, xt[s2][:], x_in[ti * 128:(ti + 1) * 128, :], w=[txt[s2]])
                        rmsnorm_bf(xt[s2][:], txt[s2], hn[s2][:], thn[s2], junk[:], tjunk, [z[:] for z in sm[s2]], tsm[s2])
                        transpose_to(hn[s2], thn[s2], hnT[b], thnT[b], 8, ps[6 + s2], pst[6 + s2], u * 128)
                    for ci, c0 in enumerate(fcols):
                        pb = ci % 4
                        for k in range(8):
                            fw.op(pe, lambda e: e.matmul(out=ps[pb][:], lhsT=w_in[:, k, c0:c0 + 128], rhs=hnT[b][:, k, :],
                                                         start=(k == 0), stop=(k == 7)), r=[t_w, thnT[b]], w=[pst[pb]])
                        evac(fst[b][:, ci, :], ps[pb][:], r=[pst[pb]], w=[tfst[b]], scale=(0.125 if ci < 8 else None))
                    cs = slice(R * 512, (R + 1) * 512)
                    fw.dma(sp, QT.rearrange("(c p) t -> p c t", p=128)[:, :, cs], fst[b][:, 0:8, :], r=[tfst[b]], w=[tQT])
                    fw.dma(pool, KcT.rearrange("(c p) t -> p c t", p=128)[:, :, cs], fst[b][:, 8:10, :], r=[tfst[b]], w=[tKcT])
                    fw.dma(pool, VcT.rearrange("(c p) t -> p c t", p=128)[:, :, cs], fst[b][:, 10:12, :], r=[tfst[b]], w=[tVcT])
                    fw.dma(sp, KsT.rearrange("(c p) t -> p c t", p=128)[:, :, cs], fst[b][:, 12:14, :], r=[tfst[b]], w=[tKsT])
                    fw.dma(pool, KwT.rearrange("(c p) t -> p c t", p=128)[:, :, cs], fst[b][:, 14:16, :], r=[tfst[b]], w=[tKwT])
                    for u in range(4):
                        pb = 4 + (u % 2)
                        for (c0, cw, o0) in ((1792, 256, 0), (2304, 256, 256)):
                            for k in range(8):
                                fw.op(pe, lambda e: e.matmul(out=ps[pb][:, o0:o0 + cw], lhsT=hnT[b][:, k, u * 128:(u + 1) * 128],
                                                             rhs=w_in[:, k, c0:c0 + cw], start=(k == 0), stop=(k == 7)),
                                      r=[t_w, thnT[b]], w=[pst[pb]])
                        evac(tst[b][:, u, :], ps[pb][:], r=[pst[pb]], w=[ttst[b]])
                        for k in range(8):
                            fw.op(pe, lambda e: e.matmul(out=ps[pb][:, 0:48], lhsT=hnT[b][:, k, u * 128:(u + 1) * 128],
                                                         rhs=w_in[:, k, 2560:2608], start=(k == 0), stop=(k == 7)),
                                  r=[t_w, thnT[b]], w=[pst[pb]])
                        fw.op(dve, lambda e: e.tensor_tensor(out=gst_[b][:, u, :], in0=ps[pb][:, 0:48], in1=bg[:], op=ALU.add),
                              r=[pst[pb], t_bg], w=[tgst[b]])
                    fw.op(act, lambda e: e.activation(out=gst_[b][:], in_=gst_[b][:], func=AF.Sigmoid), r=[tgst[b]], w=[tgst[b]])
                    rs_ = slice(R * 512, (R + 1) * 512)
                    fw.dma(sp, Vs[rs_, :].rearrange("(u p) c -> p u c", p=128), tst[b][:, :, 0:256], r=[ttst[b]], w=[tVs])
                    fw.dma(pool, Vw[rs_, :].rearrange("(u p) c -> p u c", p=128), tst[b][:, :, 256:512], r=[ttst[b]], w=[tVw])
                    fw.dma(sp, G[rs_, :].rearrange("(u p) c -> p u c", p=128), gst_[b][:], r=[tgst[b]], w=[tG])

        fw.barrier()
        if stage >= 2:
            with contextlib.ExitStack() as st:
                kcT_all = fw.sbuf(st, "kcT", [128, 4, 256], BF16)
                t_kc = T()
                vc_all = fw.sbuf(st, "vc", [128, 4, 2, 64], BF16)
                t_vc = T()
                fw.op(pool, lambda e: e.memset(vc_all[:], 0.0), w=[t_vc])
                fw.op(pool, lambda e: e.memset(kcT_all[:], 0.0), w=[t_kc])
                with contextlib.ExitStack() as sb:
                    stgB = fw.sbuf(sb, "stgB", [64, 32, 256], F32)
                    t_stgB = T()
                    w1 = fw.sbuf(sb, "w1", [64, 32, 256], BF16)
                    t_w1 = T()
                    w2f = fw.sbuf(sb, "w2f", [128, 2, 64], F32)
                    w2 = fw.sbuf(sb, "w2", [128, 2, 64], BF16)
                    t_w2f, t_w2 = T(), T()
                    posf = fw.sbuf(sb, "posf", [64, 32], F32)
                    posT = fw.sbuf(sb, "posT", [64, 32], BF16)
                    t_posf, t_posT = T(), T()
                    b1t = fw.sbuf(sb, "b1t", [128, 2], F32)
                    c1b = fw.sbuf(sb, "c1b", [128, 2], F32)
                    b2col = fw.sbuf(sb, "b2col", [64, 1], F32)
                    b2row = fw.sbuf(sb, "b2row", [128, 64], F32)
                    t_b1, t_c1b, t_b2c, t_b2r = T(), T(), T(), T()
                    rawT = [fw.sbuf(sb, "rawT", [64, S], BF16) for _ in range(2)]
                    t_raw = [T(), T()]
                    hidT = fw.sbuf(sb, "hidT", [128, 2, 256], BF16)
                    t_hid = T()
                    for kv in range(2):
                        fw.dma(sp, stgB[:], nsa_cmp_w1[kv].rearrange("(l d) h -> d l h", d=64), w=[t_stgB])
                        for q4 in range(4):
                            E = (dve, pool, act, dve)[q4]
                            if E is act:
                                fw.op(E, lambda e: e.activation(out=w1[:, q4 * 8:(q4 + 1) * 8, :], in_=stgB[:, q4 * 8:(q4 + 1) * 8, :], func=AF.Copy),
                                      r=[t_stgB], w=[t_w1])
                            else:
                                fw.op(E, lambda e: e.tensor_copy(out=w1[:, q4 * 8:(q4 + 1) * 8, :], in_=stgB[:, q4 * 8:(q4 + 1) * 8, :]),
                                      r=[t_stgB], w=[t_w1])
                        fw.dma(sp, w2f[:], nsa_cmp_w2[kv].rearrange("(c p) d -> p c d", p=128), w=[t_w2f])
                        fw.op(dve, lambda e: e.tensor_copy(out=w2[:], in_=w2f[:]), r=[t_w2f], w=[t_w2])
                        fw.dma(sp, posf[:], nsa_cmp_pos[kv].rearrange("l d -> d l"), w=[t_posf], allow_slow_non_contiguous=True)
                        fw.op(dve, lambda e: e.tensor_copy(out=posT[:], in_=posf[:]), r=[t_posf], w=[t_posT])
                        fw.dma(sp, b1t[:], nsa_cmp_b1[kv].rearrange("(c p) -> p c", p=128), w=[t_b1], allow_slow_non_contiguous=True)
                        fw.dma(sp, b2col[:], nsa_cmp_b2[kv].rearrange("(d o) -> d o", o=1), w=[t_b2c], allow_slow_non_contiguous=True)
                        fw.dma(sp, b2row[:], nsa_cmp_b2[kv].partition_broadcast(128), w=[t_b2r])
                        for c in range(2):
                            for l in range(32):
                                fw.op(pe, lambda e: e.matmul(out=ps[0][:, c:c + 1], lhsT=w1[:, l, c * 128:(c + 1) * 128], rhs=posT[:, l:l + 1],
                                                             start=(l == 0), stop=(l == 31)), r=[t_w1, t_posT], w=[pst[0]])
                        fw.op(dve, lambda e: e.tensor_tensor(out=c1b[:], in0=ps[0][:, 0:2], in1=b1t[:], op=ALU.add), r=[pst[0], t_b1], w=[t_c1b])
                        for g in range(4):
                            rt, trt = rawT[g % 2], t_raw[g % 2]
                            src = (KcT, VcT)[kv]
                            fw.dma(sp, rt[:], src[g * 64:(g + 1) * 64, :], r=[(tKcT, tVcT)[kv]], w=[trt])
                            for c in range(2):
                                for l in range(32):
                                    fw.op(pe, lambda e: e.matmul(out=ps[1 + c][:, 0:255], lhsT=w1[:, l, c * 128:(c + 1) * 128],
                                                                 rhs=rt[:, l:l + 16 * 254 + 1:16], start=(l == 0), stop=(l == 31)),
                                          r=[t_w1, trt], w=[pst[1 + c]])
                                fw.op(act, lambda e: e.activation(out=hidT[:, c, 0:255], in_=ps[1 + c][:, 0:255], func=AF.Gelu_apprx_tanh,
                                                                  bias=c1b[:, c:c + 1]), r=[pst[1 + c], t_c1b], w=[t_hid])
                            if kv == 0:
                                for c in range(2):
                                    fw.op(pe, lambda e: e.matmul(out=ps[3][0:64, 0:255], lhsT=w2[:, c, :], rhs=hidT[:, c, 0:255],
                                                                 start=(c == 0), stop=(c == 1)), r=[t_w2, t_hid], w=[pst[3]])
                                fw.op(dve, lambda e: e.tensor_scalar(out=kcT_all[0:64, g, 0:255], in0=ps[3][0:64, 0:255], scalar1=b2col[:],
                                                                     scalar2=None, op0=ALU.add), r=[pst[3], t_b2c], w=[t_kc])
                            else:
                                for nch, (n0, nn) in enumerate(((0, 128), (128, 127))):
                                    for c in range(2):
                                        fw.op(pe, lambda e: e.matmul(out=ps[3][0:nn, nch * 64:(nch + 1) * 64], lhsT=hidT[:, c, n0:n0 + nn],
                                                                     rhs=w2[:, c, :], start=(c == 0), stop=(c == 1)), r=[t_w2, t_hid], w=[pst[3]])
                                    fw.op(dve, lambda e: e.tensor_tensor(out=vc_all[0:nn, g, nch, :], in0=ps[3][0:nn, nch * 64:(nch + 1) * 64],
                                                                         in1=b2row[0:nn, :], op=ALU.add), r=[pst[3], t_b2r], w=[t_vc])
                fw.barrier()
                if debug:
                    dbg_kc = nc.dram_tensor("dbg_kc", [128, 4, 256], BF16, kind="ExternalOutput").ap()
                    dbg_vc = nc.dram_tensor("dbg_vc", [128, 4, 2, 64], BF16, kind="ExternalOutput").ap()
                    fw.dma(sp, dbg_kc, kcT_all[:], r=[t_kc])
                    fw.dma(sp, dbg_vc, vc_all[:], r=[t_vc])

                if stage >= 3:
                    d0 = fw.sbuf(st, "d0", [128, 3, 128], F32)
                    distc = fw.sbuf(st, "distc", [128, 504], F32)
                    fbias = fw.sbuf(st, "fbias", [128, 126], F32)
                    t_cc = T()
                    fw.dma(sp, d0[:], c_d0.rearrange("k p t -> p k t"), w=[t_cc])
                    fw.dma(sp, distc[:], c_distc, w=[t_cc])
                    fw.dma(sp, fbias[:], c_fbias, w=[t_cc])
                    ovl = fw.sbuf(st, "ovl", [128, 2, 64], BF16)
                    fw.dma(sp, ovl[:], c_ovl.rearrange("k p j -> p k j"), w=[t_cc])
                    bcb = fw.sbuf(st, "bcb", [128, 4, 504], BF16)
                    ks_sel = fw.sbuf(st, "ks_sel", [128, S], BF16)
                    t_ks = T()
                    fw.dma(sp, ks_sel[64:128, :], c_e256, w=[t_ks])
                    kw = fw.sbuf(st, "kw", [128, S], BF16)
                    t_kw = T()
                    fw.op(pool, lambda e: e.memset(kw[64:128, :], 0.0), w=[t_kw])
                    vs_aug = fw.sbuf(st, "vs_aug", [128, 32, 65], BF16)
                    vw_aug = fw.sbuf(st, "vw_aug", [128, 32, 65], BF16)
                    t_vs, t_vw = T(), T()
                    fw.op(pool, lambda e: e.memset(vs_aug[:], 1.0), w=[t_vs])
                    fw.op(pool, lambda e: e.memset(vw_aug[:], 1.0), w=[t_vw])
                    qstack = fw.sbuf(st, "qstack", [128, 4, S], BF16)
                    t_q = T()
                    fw.op(pool, lambda e: e.memset(qstack[:], 0.0), w=[t_q])
                    t_qm = [T() for _ in range(32)]
                    A = fw.sbuf(st, "A", [128, 33, 512], BF16)
                    t_A = T()
                    bc = fw.sbuf(st, "bc", [128, 4, 504], F32)
                    t_bc = T()
                    gt = fw.sbuf(st, "gt", [128, 32, 12], F32)
                    t_gt = T()
                    sc = fw.sbuf(st, "sc", [128, 4, 256], F32)
                    pc = fw.sbuf(st, "pc", [128, 4, 256], F32)
                    t_sc, t_pc = T(), T()
                    rowsum = fw.sbuf(st, "rowsum", [128, 4], F32)
                    rinv = fw.sbuf(st, "rinv", [128, 4], F32)
                    t_rs, t_ri = T(), T()
                    pn = fw.sbuf(st, "pn", [128, 4, 256], BF16)
                    t_pn = T()
                    fw.op(pool, lambda e: e.memset(pn[:], 0.0), w=[t_pn])
                    psumh = fw.sbuf(st, "psumh", [128, 256], F32)
                    t_ph = T()
                    fw.op(pool, lambda e: e.memset(psumh[:], 0.0), w=[t_ph])
                    pnT = fw.sbuf(st, "pnT", [128, 8, 128], BF16)
                    t_pnT = T()
                    imp = fw.sbuf(st, "imp", [128, 64], F32)
                    score = fw.sbuf(st, "score", [128, 64], F32)
                    score2 = fw.sbuf(st, "score2", [128, 64], F32)
                    m8a = fw.sbuf(st, "m8a", [128, 8], F32)
                    m8b = fw.sbuf(st, "m8b", [128, 8], F32)
                    t_imp, t_score, t_score2, t_m8a, t_m8b = T(), T(), T(), T(), T()
                    msel = fw.sbuf(st, "msel", [128, 128], BF16)
                    t_msel = T()
                    fw.op(pool, lambda e: e.memset(msel[:], 0.0), w=[t_msel])
                    NPT = 3
                    pt = [fw.sbuf(st, "pt", [128, 512], BF16) for _ in range(NPT)]
                    pa = [fw.sbuf(st, "pa", [128, 512], BF16) for _ in range(NPT)]
                    t_pt = [T() for _ in range(NPT)]
                    t_pa = [T() for _ in range(NPT)]
                    osb = fw.sbuf(st, "osb", [65, 512], F32)
                    t_osb = T()
                    f4 = fw.sbuf(st, "f4", [128, 4], F32)
                    t_f4 = T()
                    yacc = fw.sbuf(st, "yacc", [128, 256], F32)
                    t_yacc = T()
                    ybf = [fw.sbuf(st, "ybf", [128, 256], BF16) for _ in range(2)]
                    t_ybf = [T(), T()]
                    pv6 = ps[6][:].bitcast(BF16)
                    pst7b = pst[7]
                    yacc2 = [yacc, fw.sbuf(st, "yacc2", [128, 256], F32)]
                    t_yacc2 = [t_yacc, T()]
                    osb2 = [osb, fw.sbuf(st, "osb2", [65, 512], F32)]
                    t_osb2 = [t_osb, T()]
                    f2 = [fw.sbuf(st, "f2", [128, 2], F32) for _ in range(2)]
                    t_f2 = [T(), T()]
                    NPT2 = 6
                    pt = pt + [fw.sbuf(st, "pt", [128, 512], BF16) for _ in range(3)]
                    pa = pa + [fw.sbuf(st, "pa", [128, 512], BF16) for _ in range(3)]
                    t_pt = t_pt + [T(), T(), T()]
                    t_pa = t_pa + [T(), T(), T()]
                    kctr = [0]
                    import os as _os
                    _NG = int(_os.environ.get('NSA_G', '4')); _NA = int(_os.environ.get('NSA_A', '32'))

                    CS = []
                    for ci in range(4):
                        if ci == 0:
                            tiles = [sc, pc, pn, psumh, pnT, rowsum, rinv, imp, score, score2, m8a, m8b, msel]
                            trk = [t_sc, t_pc, t_pn, t_ph, t_pnT, t_rs, t_ri, t_imp, t_score, t_score2, t_m8a, t_m8b, t_msel]
                        else:
                            tiles = [fw.sbuf(st, "sc", [128, 4], F32), fw.sbuf(st, "pc", [128, 4], F32),
                                     fw.sbuf(st, "pn", [128, 4, 256], BF16), fw.sbuf(st, "psumh", [128, 4], F32),
                                     fw.sbuf(st, "pnT", [128, 8, 128], BF16), fw.sbuf(st, "rowsum", [128, 4], F32),
                                     fw.sbuf(st, "rinv", [128, 4], F32), fw.sbuf(st, "imp", [128, 64], F32),
                                     fw.sbuf(st, "score", [128, 64], F32), fw.sbuf(st, "score2", [128, 64], F32),
                                     fw.sbuf(st, "m8a", [128, 8], F32), fw.sbuf(st, "m8b", [128, 8], F32),
                                     fw.sbuf(st, "msel", [128, 128], BF16)]
                            trk = [T() for _ in range(13)]
                            fw.op(pool, lambda e: e.memset(tiles[2][:], 0.0), w=[trk[2]])
                            fw.op(pool, lambda e: e.memset(tiles[12][:], 0.0), w=[trk[12]])
                        tiles = tiles + [fw.sbuf(st, "gf", [128, 4], F32), fw.sbuf(st, "imp4", [128, 4, 64], F32)]
                        trk = trk + [T(), T()]
                        CS.append(tuple(tiles + trk))
                    ocbuf = fw.sbuf(st, "ocbuf", [128, 32, 256], BF16)
                    t_oc = [T() for _ in range(32)]

                    def cmp_thunks(g, a, ci):
                        th = []
                        (sc_, pc_, pcb, psumh_, pnT, rowsum, rinv, imp, score, score2, m8a, m8b, msel, gf, imp4,
                         t_sc_, t_pc_, t_pcb, t_ph_, t_pnT, t_rs, t_ri, t_imp, t_score, t_score2, t_m8a, t_m8b, t_msel, t_gf, t_imp4) = CS[ci]
                        bS, bT = 2 * ci, 2 * ci + 1
                        pvT = ps[bT][:].bitcast(BF16)
                        ta = slice(a * 128, (a + 1) * 128)
                        boff = 248 - 8 * a
                        for hp in range(2):
                            def f_mm(hp=hp):
                                for hh in range(2):
                                    h = 2 * hp + hh
                                    fw.op(pe, lambda e: e.matmul(out=ps[bS][:, hh * 256:hh * 256 + 255], lhsT=qstack[:, h, ta],
                                                                 rhs=kcT_all[:, g, 0:255], start=True, stop=False), r=[t_q, t_qm[a], t_kc], w=[pst[bS]])
                                    fw.op(pe, lambda e: e.matmul(out=ps[bS][:, hh * 256:hh * 256 + 255], lhsT=ident_bf[:],
                                                                 rhs=bcb[:, h, boff:boff + 255], start=False, stop=True), r=[t_bc, t_ident], w=[pst[bS]])
                            th.append(f_mm)
                            for hh in range(2):
                                def f_exp(hp=hp, hh=hh):
                                    h = 2 * hp + hh
                                    fw.op(act, lambda e: e.activation(out=pcb[:, h, 0:255], in_=ps[bS][:, hh * 256:hh * 256 + 255], func=AF.Exp,
                                                                      accum_out=rowsum[:, h:h + 1]), r=[pst[bS]], w=[t_pcb, t_rs])
                                th.append(f_exp)

                        def f_rinv():
                            fw.op(dve, lambda e: e.tensor_scalar(out=rinv[:], in0=rowsum[:], scalar1=1e-30, scalar2=None, op0=ALU.max),
                                  r=[t_rs], w=[t_ri])
                            fw.op(dve, lambda e: e.reciprocal(out=rinv[:], in_=rinv[:]), r=[t_ri], w=[t_ri])
                        th.append(f_rinv)

                        def f_pnT():
                            for h in range(4):
                                for nch in range(2):
                                    fw.op(pe, lambda e: e.transpose(out=pvT[:, (h * 2 + nch) * 128:(h * 2 + nch + 1) * 128],
                                                                    in_=pcb[:, h, nch * 128:(nch + 1) * 128], identity=ident_bf[:]),
                                          r=[t_pcb, t_ident], w=[pst[bT]])
                            fw.op(act, lambda e: e.activation(out=pnT[:], in_=pvT.rearrange("p (k t) -> p k t", k=8), func=AF.Copy),
                                  r=[pst[bT]], w=[t_pnT])
                        th.append(f_pnT)
                        th.append(lambda: fw.op(dve, lambda e: e.tensor_tensor(out=gf[:], in0=rinv[:], in1=gt[:, a, 0:12:3], op=ALU.mult),
                                                r=[t_ri, t_gt], w=[t_gf]))

                        def f_oc():
                            for h in range(4):
                                for nch in range(2):
                                    fw.op(pe, lambda e: e.matmul(out=ps[bS][:, h * 64:(h + 1) * 64], lhsT=pnT[:, h * 2 + nch, :],
                                                                 rhs=vc_all[:, g, nch, :], start=(nch == 0), stop=(nch == 1)),
                                          r=[t_pnT, t_vc], w=[pst[bS]])
                                for nch in range(2):
                                    fw.op(pe, lambda e: e.matmul(out=ps[bS][:, 256 + h * 64:256 + (h + 1) * 64], lhsT=pnT[:, h * 2 + nch, :],
                                                                 rhs=ovl[:, nch, :], start=(nch == 0), stop=(nch == 1)),
                                          r=[t_pnT, t_cc], w=[pst[bS]])
                        th.append(f_oc)
                        th.append(lambda: fw.op(dve, lambda e: e.tensor_tensor(out=ocbuf[:, a, :].rearrange("p (h c) -> p h c", h=4),
                                                                               in0=ps[bS][:, 0:256].rearrange("p (h c) -> p h c", h=4),
                                                                               in1=gf[:].unsqueeze(2).to_broadcast([128, 4, 64]), op=ALU.mult),
                                                r=[pst[bS], t_gf], w=[t_oc[a]]))
                        th.append(lambda: fw.op(dve, lambda e: e.tensor_tensor(out=imp4[:], in0=ps[bS][:, 256:512].rearrange("p (h c) -> p h c", h=4),
                                                                               in1=rinv[:].unsqueeze(2).to_broadcast([128, 4, 64]), op=ALU.mult),
                                                r=[pst[bS], t_ri], w=[t_imp4]))

                        def f_imp():
                            fw.op(dve, lambda e: e.tensor_reduce(out=imp[:], in_=imp4[:].rearrange("p h j -> p j h"), axis=AX.X, op=ALU.add),
                                  r=[t_imp4], w=[t_imp])
                            fw.op(dve, lambda e: e.tensor_tensor(out=score[:], in0=imp[:], in1=fbias[:, 62 - 2 * a:62 - 2 * a + 64], op=ALU.add),
                                  r=[t_imp, t_cc], w=[t_score])
                            fw.op(dve, lambda e: e.memset(score[:, 0:1], 100.0), w=[t_score])
                        th.append(f_imp)

                        def f_topk():
                            fw.op(dve, lambda e: e.max(out=m8a[:], in_=score[:]), r=[t_score], w=[t_m8a])
                            fw.op(dve, lambda e: e.match_replace(out=score2[:], in_to_replace=m8a[:], in_values=score[:], imm_value=-1.0e9),
                                  r=[t_score, t_m8a], w=[t_score2])
                            fw.op(dve, lambda e: e.max(out=m8b[:], in_=score2[:]), r=[t_score2], w=[t_m8b])
                            fw.op(dve, lambda e: e.tensor_scalar(out=msel[:, 64:128], in0=score[:], scalar1=m8b[:, 7:8], scalar2=-1.0,
                                                                 op0=ALU.is_ge, op1=ALU.add), r=[t_score, t_m8b], w=[t_msel])
                        th.append(f_topk)

                        def f_mask():
                            fw.op(pe, lambda e: e.transpose(out=pvT[:, 0:128], in_=msel[:], identity=ident_bf[:]), r=[t_msel, t_ident], w=[pst[bT]])
                            for h in range(4):
                                if h % 2 == 0:
                                    fw.op(act, lambda e: e.activation(out=qstack[64:128, h, ta], in_=pvT[64:128, 0:128], func=AF.Copy),
                                          r=[pst[bT]], w=[t_qm[a]])
                                else:
                                    fw.op(dve, lambda e: e.tensor_copy(out=qstack[64:128, h, ta], in_=pvT[64:128, 0:128]), r=[pst[bT]], w=[t_qm[a]])
                        th.append(f_mask)
                        return th

                    def obank(a, kind):
                        return ((3, 3), (4, 4))[kind][a % 2]

                    f4k = [fw.sbuf(st, "f4k", [128, 4], F32) for _ in range(2)]
                    t_f4k = [T(), T()]
                    otmp = [fw.sbuf(st, "otmp", [128, 4, 64], F32) for _ in range(2)]
                    t_otmp = [T(), T()]

                    def finish_thunks(g, a, kind):
                        ya, tya = yacc2[a % 2], t_yacc2[a % 2]
                        ob, tob = osb2[kind], t_osb2[kind]
                        bank = obank(a, kind)
                        gcol = 1 + kind
                        ff, tff = f4k[kind], t_f4k[kind]
                        tmp, ttmp = otmp[kind], t_otmp[kind]
                        ta = slice(a * 128, (a + 1) * 128)
                        th = []
                        th.append(lambda: fw.op(act, lambda e: e.activation(out=ob[:], in_=ps[bank][0:65, :], func=AF.Copy), r=[pst[bank]], w=[tob]))

                        def f_tr():
                            for h in range(4):
                                fw.op(pe, lambda e: e.transpose(out=ps[7][:, h * 65:(h + 1) * 65], in_=ob[:, h * 128:(h + 1) * 128],
                                                                identity=ident_f[0:65, 0:65]), r=[tob, t_ident], w=[pst[7]])
                        th.append(f_tr)
                        th.append(lambda: fw.op(dve, lambda e: e.tensor_scalar(out=ff[:], in0=ps[7][:, 64:260:65], scalar1=1e-30, scalar2=None, op0=ALU.max),
                                                r=[pst[7]], w=[tff]))
                        th.append(lambda: fw.op(dve, lambda e: e.reciprocal(out=ff[:], in_=ff[:]), r=[tff], w=[tff]))
                        th.append(lambda: fw.op(dve, lambda e: e.tensor_tensor(out=ff[:], in0=ff[:], in1=gt[:, a, gcol:12:3], op=ALU.mult),
                                                r=[tff, t_gt], w=[tff]))
                        th.append(lambda: fw.op(dve, lambda e: e.tensor_tensor(out=tmp[:], in0=ps[7][:, 0:260].rearrange("p (h c) -> p h c", c=65)[:, :, 0:64],
                                                                               in1=ff[:].unsqueeze(2).to_broadcast([128, 4, 64]), op=ALU.mult),
                                                r=[pst[7], tff], w=[ttmp]))
                        if kind == 0:
                            th.append(lambda: fw.op(pool, lambda e: e.tensor_tensor(out=ya[:], in0=tmp[:].rearrange("p h c -> p (h c)"), in1=ocbuf[:, a, :], op=ALU.add),
                                                    r=[ttmp, t_oc[a]], w=[tya]))
                        else:
                            th.append(lambda: fw.op(pool, lambda e: e.tensor_tensor(out=ya[:], in0=tmp[:].rearrange("p h c -> p (h c)"), in1=ya[:], op=ALU.add),
                                                    r=[ttmp, tya], w=[tya]))

                            def f_out():
                                yb, tyb = ybf[a % 2], t_ybf[a % 2]
                                fw.op(act, lambda e: e.activation(out=yb[:], in_=ya[:], func=AF.Copy), r=[tya], w=[tyb])
                                fw.dma(sp, Y[ta, g * 256:(g + 1) * 256], yb[:], r=[tyb], w=[tY])
                            th.append(f_out)
                        return th

                    for g in range(_NG):
                        for h in range(4):
                            fw.dma(sp, qstack[0:64, h, :], QT[(4 * g + h) * 64:(4 * g + h + 1) * 64, :], r=[tQT], w=[t_q])
                        fw.dma(sp, ks_sel[0:64, :], KsT[g * 64:(g + 1) * 64, :], r=[tKsT], w=[t_ks])
                        fw.dma(sp, kw[0:64, :], KwT[g * 64:(g + 1) * 64, :], r=[tKwT], w=[t_kw])
                        fw.dma(pool, vs_aug[:, :, 0:64], Vs[:, g * 64:(g + 1) * 64].rearrange("(a p) d -> p a d", p=128), r=[tVs], w=[t_vs])
                        fw.dma(pool, vw_aug[:, :, 0:64], Vw[:, g * 64:(g + 1) * 64].rearrange("(a p) d -> p a d", p=128), r=[tVw], w=[t_vw])
                        fw.dma(sp, gt[:], G[:, g * 12:(g + 1) * 12].rearrange("(a p) c -> p a c", p=128), r=[tG], w=[t_gt])
                        for h in range(4):
                            sl = SLOPES[4 * g + h]
                            for dl in range(33):
                                src = d0[:, 0, :] if dl == 0 else (d0[:, 2, :] if dl == 32 else d0[:, 1, :])
                                off = 512.0 if dl == 32 else 128.0 * dl
                                fw.op(act, lambda e: e.activation(out=A[:, dl, h * 128:(h + 1) * 128], in_=src, func=AF.Exp,
                                                                  scale=-sl, bias=-sl * off), r=[t_cc], w=[t_A])
                            fw.op(dve, lambda e: e.tensor_scalar(out=bcb[:, h, :], in0=distc[:], scalar1=-sl, scalar2=None, op0=ALU.mult),
                                  r=[t_cc], w=[t_bc])
                        for a0 in range(0, _NA, 4):
                            chains = [cmp_thunks(g, a0 + ci, ci) for ci in range(4) if a0 + ci < _NA]
                            while chains:
                                for c_ in chains:
                                    c_.pop(0)()
                                chains = [c_ for c_ in chains if c_]
                        LA = 4
                        SB = (0, 1, 2, 6, 5)
                        pend = []
                        for a in range(_NA):
                            ta = slice(a * 128, (a + 1) * 128)
                            steps = [(0, b) for b in range(a + 1)] + [(1, b) for b in range(max(0, a - 4), a + 1)]
                            n = len(steps)
                            per = -(-len(pend) // max(1, n - 1))
                            slots = {}
                            newp = []
                            for i in range(n + LA):
                                if i < n:
                                    kind, b = steps[i]
                                    sbk = SB[kctr[0] % 5]
                                    k3 = kctr[0] % NPT2
                                    kctr[0] += 1
                                    slots[i] = k3
                                    krows = 128
                                    kT, tk = (ks_sel, t_ks) if kind == 0 else (kw, t_kw)
                                    dl = a - b
                                    ai = dl if (kind == 0 or dl < 4) else 32
                                    fw.op(pe, lambda e: e.matmul(out=ps[sbk][:], lhsT=kT[0:krows, b * 128:(b + 1) * 128],
                                                                 rhs=qstack[0:krows, :, ta], start=True, stop=True),
                                          r=[tk, t_q, t_qm[a]], w=[pst[sbk]])
                                    fw.op(act, lambda e: e.activation(out=pt[k3][:], in_=ps[sbk][:], func=AF.Exp), r=[pst[sbk]], w=[t_pt[k3]])
                                    fw.op(dve, lambda e: e.tensor_tensor(out=pa[k3][:], in0=pt[k3][:], in1=A[:, ai, :], op=ALU.mult),
                                          r=[t_pt[k3], t_A], w=[t_pa[k3]])
                                    if i >= 1:
                                        for _ in range(per):
                                            if pend:
                                                pend.pop(0)()
                                if i >= LA:
                                    j = i - LA
                                    kind, b = steps[j]
                                    k3 = slots[j]
                                    vaug, tv = (vs_aug, t_vs) if kind == 0 else (vw_aug, t_vw)
                                    first = (j == 0) or (steps[j - 1][0] != kind)
                                    last = (j == n - 1) or (steps[j + 1][0] != kind)
                                    bank = obank(a, kind)
                                    fw.op(pe, lambda e: e.matmul(out=ps[bank][0:65, :], lhsT=vaug[:, b, :], rhs=pa[k3][:],
                                                                 start=first, stop=last), r=[tv, t_pa[k3]], w=[pst[bank]])
                                    if last:
                                        newp += finish_thunks(g, a, kind)
                            while pend:
                                pend.pop(0)()
                            pend = newp
                        while pend:
                            pend.pop(0)()

        fw.barrier()
        tXB = [T() for _ in range(16)]

        def phase_ffn(layer, outproj, final):
            with contextlib.ExitStack() as st:
                wi = fw.sbuf(st, "wi", [128, 8, 2 * D_FF], BF16)
                wo2 = fw.sbuf(st, "wo2", [128, 22, D], BF16)
                t_wi, t_wo2 = T(), T()
                gf, tgf = load_gain(st, norm_ffn[layer], "gf")
                stg = [fw.sbuf(st, "stgF", [128, 704], F32) for _ in range(4)]
                tstg = [T() for _ in range(4)]
                if outproj:
                    wo = fw.sbuf(st, "wo", [128, 8, D], BF16)
                    t_wo = T()
                    for k in range(8):
                        load_cast(st, wo[:, k, :], t_wo, nsa_w_out[k * 128:(k + 1) * 128, :], 128, D, stg=stg, tstg=tstg)
                for k in range(8):
                    load_cast(st, wi[:, k, :], t_wi, ffn_w_in[layer, k * 128:(k + 1) * 128, :], 128, 2 * D_FF,
                              gain=gf[:, k:k + 1], tgain=tgf, stg=stg, tstg=tstg)
                for f in range(22):
                    load_cast(st, wo2[:, f, :], t_wo2, ffn_w_out[layer, f * 128:(f + 1) * 128, :], 128, D, stg=stg, tstg=tstg)
                if final:
                    gfin = fw.sbuf(st, "gfin", [128, D], F32)
                    t_gfin = T()
                    fw.dma(sp, gfin[:], norm_final.partition_broadcast(128), w=[t_gfin])
                xb = [fw.sbuf(st, "xb", [128, 2, D], F32) for _ in range(2)]
                t_xb = [T(), T()]
                if outproj:
                    ybt = [fw.sbuf(st, "ybt", [128, 2, D], BF16)] * 2
                    t_ybt = [T()] * 2
                    yT = fw.sbuf(st, "yT", [128, 8, 256], BF16)
                    t_yT = T()
                hn = [fw.sbuf(st, "hnF", [128, D], BF16) for _ in range(2)]
                thn = [T(), T()]
                junk = fw.sbuf(st, "junkF", [128, D], BF16)
                tjunk = T()
                sm = [[fw.sbuf(st, "smF", [128, 1], F32) for _ in range(3)] for _ in range(2)]
                tsm = [T(), T()]
                hnT = fw.sbuf(st, "hnTF", [128, 8, 256], BF16)
                thnT = T()
                actT = fw.sbuf(st, "actT", [128, 22, 256], BF16)
                t_actT = T()
                sg = [fw.sbuf(st, "sg", [128, 256], F32) for _ in range(2)]
                t_sg = [T(), T()]
                if final:
                    ob = [fw.sbuf(st, "ob", [128, D], F32) for _ in range(2)]
                    t_ob = [T(), T()]
                tout = T()
                for blk in range(16):
                    s2 = blk % 2
                    rows = slice(blk * 256, (blk + 1) * 256)
                    X, tX = xb[s2], t_xb[s2]
                    if outproj:
                        fw.dma(sp, X[:], x_in[rows, :].rearrange("(u p) d -> p u d", p=128), w=[tX])
                        fw.dma(pool, ybt[s2][:], Y[rows, :].rearrange("(u p) d -> p u d", p=128), r=[tY], w=[t_ybt[s2]])
                        for u in range(2):
                            transpose_to(ybt[s2][:, u, :], t_ybt[s2], yT, t_yT, 8, ps[6 + u], pst[6 + u], u * 128)
                        for u in range(2):
                            for nh in range(2):
                                pb = 4 + nh
                                for k in range(8):
                                    fw.op(pe, lambda e: e.matmul(out=ps[pb][:], lhsT=yT[:, k, u * 128:(u + 1) * 128],
                                                                 rhs=wo[:, k, nh * 512:(nh + 1) * 512], start=(k == 0), stop=(k == 7)),
                                          r=[t_yT, t_wo], w=[pst[pb]])
                                fw.op(dve, lambda e: e.tensor_tensor(out=X[:, u, nh * 512:(nh + 1) * 512], in0=X[:, u, nh * 512:(nh + 1) * 512],
                                                                     in1=ps[pb][:], op=ALU.add), r=[pst[pb], tX], w=[tX])
                    else:
                        fw.dma(sp, X[:], X1[rows, :].rearrange("(u p) d -> p u d", p=128), r=[tXB[blk]], w=[tX])
                    for u in range(2):
                        rmsnorm_bf(X[:, u, :], tX, hn[u][:], thn[u], junk[:], tjunk, [z[:] for z in sm[u]], tsm[u])
                        transpose_to(hn[u], thn[u], hnT, thnT, 8, ps[6 + u], pst[6 + u], u * 128)
                    for f in range(22):
                        pb = f % 4
                        for half in range(2):
                            c0 = half * D_FF + f * 128
                            for k in range(8):
                                fw.op(pe, lambda e: e.matmul(out=ps[pb][:, half * 256:(half + 1) * 256], lhsT=wi[:, k, c0:c0 + 128],
                                                             rhs=hnT[:, k, :], start=(k == 0), stop=(k == 7)), r=[t_wi, thnT], w=[pst[pb]])
                        fw.op(act, lambda e: e.activation(out=sg[f % 2][:], in_=ps[pb][:, 0:256], func=AF.Silu), r=[pst[pb]], w=[t_sg[f % 2]])
                        fw.op(dve, lambda e: e.tensor_tensor(out=actT[:, f, :], in0=sg[f % 2][:], in1=ps[pb][:, 256:512], op=ALU.mult),
                              r=[pst[pb], t_sg[f % 2]], w=[t_actT])
                    for u in range(2):
                        for nh in range(2):
                            pb = 4 + nh
                            for f in range(22):
                                fw.op(pe, lambda e: e.matmul(out=ps[pb][:], lhsT=actT[:, f, u * 128:(u + 1) * 128],
                                                             rhs=wo2[:, f, nh * 512:(nh + 1) * 512], start=(f == 0), stop=(f == 21)),
                                      r=[t_actT, t_wo2], w=[pst[pb]])
                            fw.op(dve, lambda e: e.tensor_tensor(out=X[:, u, nh * 512:(nh + 1) * 512], in0=X[:, u, nh * 512:(nh + 1) * 512],
                                                                 in1=ps[pb][:], op=ALU.add), r=[pst[pb], tX], w=[tX])
                    if final:
                        for u in range(2):
                            ss, sd, rs = [z[:] for z in sm[u]]
                            fw.op(act, lambda e: e.activation(out=junk[:], in_=X[:, u, :], func=AF.Square, accum_out=ss), r=[tX], w=[tjunk, tsm[u]])
                            fw.op(act, lambda e: e.activation(out=sd, in_=ss, func=AF.Sqrt, bias=epsc[:], scale=1.0 / D), r=[tsm[u], t_eps], w=[tsm[u]])
                            fw.op(dve, lambda e: e.reciprocal(out=rs, in_=sd), r=[tsm[u]], w=[tsm[u]])
                            fw.op(dve, lambda e: e.scalar_tensor_tensor(out=ob[u][:], in0=X[:, u, :], scalar=rs, in1=gfin[:], op0=ALU.mult, op1=ALU.mult),
                                  r=[tX, tsm[u], t_gfin], w=[t_ob[u]])
                            fw.dma(sp, out_ap[blk * 256 + u * 128:blk * 256 + (u + 1) * 128, :], ob[u][:], r=[t_ob[u]], w=[tout])
                    else:
                        fw.dma(sp, X1[rows, :].rearrange("(u p) d -> p u d", p=128), X[:], r=[tX], w=[tXB[blk]])
                fw.barrier()

        if stage >= 4:
            phase_ffn(0, True, False)

        if stage >= 5:
            with contextlib.ExitStack() as st:
                NB = 512
                NU = NB // 128
                wl = fw.sbuf(st, "wl", [128, 8, 2 * D_RNN], BF16)
                t_wl = T()
                gl, tgl = load_gain(st, norm_mix[1], "gl")
                wa = fw.sbuf(st, "wa", [88, 16, 176], BF16)
                wx = fw.sbuf(st, "wx", [88, 16, 176], BF16)
                wlo = fw.sbuf(st, "wlo", [88, 16, D], BF16)
                t_wa, t_wx, t_wlo = T(), T(), T()
                cw = fw.sbuf(st, "cw", [88, 16, 4], F32)
                cb = fw.sbuf(st, "cb", [88, 16], F32)
                nba = fw.sbuf(st, "nba", [88, 16], F32)
                nbx = fw.sbuf(st, "nbx", [88, 16], F32)
                nsp = fw.sbuf(st, "nsp", [88, 16], F32)
                t_small = T()
                for j in range(4):
                    fw.dma(sp, cw[:, :, j], lru_conv_w[j].rearrange("(c p) -> p c", p=88), w=[t_small], allow_slow_non_contiguous=True)
                fw.dma(sp, cb[:], lru_conv_b.rearrange("(c p) -> p c", p=88), w=[t_small], allow_slow_non_contiguous=True)
                fw.dma(sp, nba[:], lru_b_a.rearrange("(c p) -> p c", p=88), w=[t_small], allow_slow_non_contiguous=True)
                fw.dma(sp, nbx[:], lru_b_x.rearrange("(c p) -> p c", p=88), w=[t_small], allow_slow_non_contiguous=True)
                fw.dma(sp, nsp[:], lru_lambda.rearrange("(c p) -> p c", p=88), w=[t_small], allow_slow_non_contiguous=True)
                fw.op(dve, lambda e: e.tensor_scalar(out=nba[:], in0=nba[:], scalar1=-1.0, scalar2=None, op0=ALU.mult), r=[t_small], w=[t_small])
                fw.op(dve, lambda e: e.tensor_scalar(out=nbx[:], in0=nbx[:], scalar1=-1.0, scalar2=None, op0=ALU.mult), r=[t_small], w=[t_small])
                fw.op(act, lambda e: e.activation(out=nsp[:], in_=nsp[:], func=AF.Exp, scale=-1.0), r=[t_small], w=[t_small])
                fw.op(act, lambda e: e.activation(out=nsp[:], in_=nsp[:], func=AF.Ln, bias=1.0), r=[t_small], w=[t_small])
                fw.op(dve, lambda e: e.tensor_scalar(out=nsp[:], in0=nsp[:], scalar1=-8.0, scalar2=None, op0=ALU.mult), r=[t_small], w=[t_small])
                halo = fw.sbuf(st, "halo", [88, 16, 3], F32)
                hlast = fw.sbuf(st, "hlast", [88, 16], F32)
                t_halo, t_hl = T(), T()
                fw.op(pool, lambda e: e.memset(halo[:], 0.0), w=[t_halo])
                fw.op(pool, lambda e: e.memset(hlast[:], 0.0), w=[t_hl])
                X2 = [fw.sbuf(st, "xbL", [128, NU, D], F32) for _ in range(2)]
                tX2 = [T(), T()]
                cur = [0]
                hn = [fw.sbuf(st, "hnL", [128, D], BF16) for _ in range(2)]
                thn = [T(), T()]
                junk = fw.sbuf(st, "junkL", [128, D], BF16)
                tjunk = T()
                sm = [[fw.sbuf(st, "smL", [128, 1], F32) for _ in range(3)] for _ in range(2)]
                tsm = [T(), T()]
                hnT_1 = fw.sbuf(st, "hnTL", [128, 8, NB], BF16)
                thnT_1 = T()
                hnT2 = [hnT_1, hnT_1]
                thnT2 = [thnT_1, thnT_1]
                g16 = [fw.sbuf(st, "gate16", [88, 16, NB], BF16) for _ in range(2)]
                t_g16_2 = [T(), T()]

                with contextlib.ExitStack() as st_w:
                    stg = [fw.sbuf(st_w, "stgL", [128, 704], F32) for _ in range(4)]
                    tstg = [T() for _ in range(4)]
                    for k in range(8):
                        load_cast(st_w, wl[:, k, :], t_wl, lru_w_in[k * 128:(k + 1) * 128, :], 128, 2 * D_RNN,
                                  gain=gl[:, k:k + 1], tgain=tgl, stg=stg, tstg=tstg)
                    for (dst, tdst, src) in ((wa, t_wa, lru_w_a), (wx, t_wx, lru_w_x)):
                        for n in range(8):
                            for ih in range(2):
                                load_cast(st_w, dst[:, 2 * n + ih, :], tdst, src[n, ih * 88:(ih + 1) * 88, :], 88, 176, stg=stg, tstg=tstg)
                    for c in range(16):
                        load_cast(st_w, wlo[:, c, :], t_wlo, lru_w_out[c * 88:(c + 1) * 88, :], 88, D, stg=stg, tstg=tstg)
                    fw.barrier()

                def mk(name, dt=F32, w=NB, nbuf=1):
                    return [fw.sbuf(st, name, [88, 2, w], dt) for _ in range(nbuf)], [[T(), T()] for _ in range(nbuf)]
                recb, t_recb = mk("recb", F32, NB + 3, 2)
                xr, t_xr = mk("xr", F32, NB, 2)
                xrb, t_xrb = mk("xrb", BF16, NB, 2)
                rr_, t_rr = mk("rr")
                ii_, t_ii = mk("ii")
                aa, t_aa = mk("aa")
                uu, t_uu = mk("uu")
                hh, t_hh = mk("hh")
                rr_, ii_, aa, uu, hh = rr_[0], ii_[0], aa[0], uu[0], hh[0]
                t_rr, t_ii, t_aa, t_uu, t_hh = t_rr[0], t_ii[0], t_aa[0], t_uu[0], t_hh[0]

                def conv_chain(n, half):
                    q = n % 2
                    c = 2 * n + half
                    pb = half
                    RB, XR, XB = recb[q], xr[q], xrb[q]
                    tRB, tXR, tXB_ = t_recb[q][half], t_xr[q][half], t_xrb[q][half]
                    th = []

                    def f0():
                        for k in range(8):
                            fw.op(pe, lambda e: e.matmul(out=ps[pb][0:88, :], lhsT=wl[:, k, D_RNN + c * 88:D_RNN + (c + 1) * 88], rhs=hnT2[cur[0] % 2][:, k, :],
                                                         start=(k == 0), stop=(k == 7)), r=[t_wl, thnT2[cur[0] % 2]], w=[pst[pb]])
                        fw.op(act, lambda e: e.activation(out=RB[:, half, 0:3], in_=halo[:, c, :], func=AF.Copy), r=[t_halo], w=[tRB])
                    th.append(f0)

                    def f1():
                        fw.op(act, lambda e: e.activation(out=RB[:, half, 3:3 + NB], in_=ps[pb][0:88, :], func=AF.Copy), r=[pst[pb]], w=[tRB])
                        fw.op(act, lambda e: e.activation(out=halo[:, c, :], in_=RB[:, half, NB:NB + 3], func=AF.Copy), r=[tRB], w=[t_halo])
                    th.append(f1)
                    th.append(lambda: fw.op(dve, lambda e: e.tensor_scalar(out=XR[:, half, :], in0=RB[:, half, 0:NB], scalar1=cw[:, c, 0:1],
                                                                           scalar2=cb[:, c:c + 1], op0=ALU.mult, op1=ALU.add),
                                            r=[tRB, t_small], w=[tXR]))
                    for j in range(1, 4):
                        th.append(lambda j=j: fw.op(dve, lambda e: e.scalar_tensor_tensor(out=XR[:, half, :], in0=RB[:, half, j:j + NB], scalar=cw[:, c, j:j + 1],
                                                                                           in1=XR[:, half, :], op0=ALU.mult, op1=ALU.add),
                                                    r=[tRB, t_small, tXR], w=[tXR]))
                    th.append(lambda: fw.op(act, lambda e: e.activation(out=XB[:, half, :], in_=XR[:, half, :], func=AF.Copy), r=[tXR], w=[tXB_]))
                    return th

                def gate_chain(n, oh):
                    q = n % 2
                    XR, XB = xr[q], xrb[q]
                    c = 2 * n + oh
                    pr, pi = 2 + 2 * oh, 3 + 2 * oh
                    R_, I_, A_, U_, H_ = rr_[:, oh, :], ii_[:, oh, :], aa[:, oh, :], uu[:, oh, :], hh[:, oh, :]
                    tr, ti, ta_, tu, th_ = [t_rr[oh]], [t_ii[oh]], [t_aa[oh]], [t_uu[oh]], [t_hh[oh]]
                    th = []

                    def f0():
                        for (wgt, twg, pbG) in ((wa, t_wa, pr), (wx, t_wx, pi)):
                            for ih in range(2):
                                fw.op(pe, lambda e: e.matmul(out=ps[pbG][0:88, :], lhsT=wgt[:, 2 * n + ih, oh * 88:(oh + 1) * 88],
                                                             rhs=XB[:, ih, :], start=(ih == 0), stop=(ih == 1)),
                                      r=[twg, t_xrb[q][0], t_xrb[q][1]], w=[pst[pbG]])
                    th.append(f0)
                    th.append(lambda: fw.op(act, lambda e: e.activation(out=R_, in_=ps[pr][0:88, :], func=AF.Exp, bias=nba[:, c:c + 1], scale=-1.0),
                                            r=[pst[pr], t_small], w=tr))
                    th.append(lambda: fw.op(act, lambda e: e.activation(out=I_, in_=ps[pi][0:88, :], func=AF.Exp, bias=nbx[:, c:c + 1], scale=-1.0),
                                            r=[pst[pi], t_small], w=ti))
                    th.append(lambda: fw.op(act, lambda e: e.activation(out=R_, in_=R_, func=AF.Ln, bias=1.0), r=tr, w=tr))
                    th.append(lambda: fw.op(act, lambda e: e.activation(out=I_, in_=I_, func=AF.Ln, bias=1.0), r=ti, w=ti))
                    th.append(lambda: fw.op(act, lambda e: e.activation(out=R_, in_=R_, func=AF.Exp, scale=-1.0), r=tr, w=tr))
                    th.append(lambda: fw.op(act, lambda e: e.activation(out=I_, in_=I_, func=AF.Exp, scale=-1.0), r=ti, w=ti))
                    th.append(lambda: fw.op(act, lambda e: e.activation(out=A_, in_=R_, func=AF.Exp, scale=nsp[:, c:c + 1]), r=tr + [t_small], w=ta_))
                    th.append(lambda: fw.op(dve, lambda e: e.tensor_tensor(out=I_, in0=I_, in1=XR[:, oh, :], op=ALU.mult), r=ti + [t_xr[q][oh]], w=ti))
                    th.append(lambda: fw.op(dve, lambda e: e.scalar_tensor_tensor(out=U_, in0=A_, scalar=-1.0, in1=A_, op0=ALU.mult, op1=ALU.mult),
                                            r=ta_, w=tu))
                    th.append(lambda: fw.op(dve, lambda e: e.tensor_scalar(out=U_, in0=U_, scalar1=1.0, scalar2=1e-30, op0=ALU.add, op1=ALU.max),
                                            r=tu, w=tu))
                    th.append(lambda: fw.op(act, lambda e: e.activation(out=U_, in_=U_, func=AF.Ln), r=tu, w=tu))
                    th.append(lambda: fw.op(act, lambda e: e.activation(out=U_, in_=U_, func=AF.Exp, scale=0.5), r=tu, w=tu))
                    th.append(lambda: fw.op(dve, lambda e: e.tensor_tensor(out=U_, in0=U_, in1=I_, op=ALU.mult), r=tu + ti, w=tu))

                    def f_scan():
                        fw.op(dve, lambda e: e.tensor_tensor_scan(out=H_, data0=A_, data1=U_, initial=hlast[:, c:c + 1], op0=ALU.mult, op1=ALU.add),
                              r=ta_ + tu + [t_hl], w=th_)
                        fw.op(dve, lambda e: e.tensor_copy(out=hlast[:, c:c + 1], in_=hh[:, oh, NB - 1:NB]), r=th_, w=[t_hl])
                    th.append(f_scan)
                    th.append(lambda: fw.op(dve, lambda e: e.tensor_tensor(out=g16[cur[0] % 2][:, c, :], in0=H_, in1=g16[cur[0] % 2][:, c, :], op=ALU.mult),
                                            r=th_, w=[t_g16_2[cur[0] % 2]]))
                    return th

                def zip_emit(chains):
                    chains = [c for c in chains if c]
                    while chains:
                        for c in chains:
                            c.pop(0)()
                        chains = [c for c in chains if c]

                def blk_rows(b):
                    return slice(b * NB, (b + 1) * NB), [tXB[b * (NB // 256) + q] for q in range(NB // 256)]

                def prep_load(b):
                    rows, tblks = blk_rows(b)
                    fw.dma(sp, X2[b % 2][:], X1[rows, :].rearrange("(u p) d -> p u d", p=128), r=tblks, w=[tX2[b % 2]])

                def prep_norm(b, u):
                    rmsnorm_bf(X2[b % 2][:, u, :], tX2[b % 2], hn[u % 2][:], thn[u % 2], junk[:], tjunk, [z[:] for z in sm[u % 2]], tsm[u % 2])
                    transpose_to(hn[u % 2], thn[u % 2], hnT2[b % 2], thnT2[b % 2], 8, ps[6 + u % 2], pst[6 + u % 2], u * 128)

                def gelu_stage(b):
                    for c in range(16):
                        pb = c % 2
                        for k in range(8):
                            fw.op(pe, lambda e: e.matmul(out=ps[pb][0:88, :], lhsT=wl[:, k, c * 88:(c + 1) * 88], rhs=hnT2[b % 2][:, k, :],
                                                         start=(k == 0), stop=(k == 7)), r=[t_wl, thnT2[b % 2]], w=[pst[pb]])
                        fw.op(act, lambda e: e.activation(out=g16[b % 2][:, c, :], in_=ps[pb][0:88, :], func=AF.Gelu_apprx_tanh),
                              r=[pst[pb]], w=[t_g16_2[b % 2]])

                def outproj_unit(b, u, nh):
                    pb = 6 + nh
                    Xb, tXb = X2[b % 2], tX2[b % 2]
                    for c in range(16):
                        fw.op(pe, lambda e: e.matmul(out=ps[pb][:], lhsT=g16[b % 2][:, c, u * 128:(u + 1) * 128],
                                                     rhs=wlo[:, c, nh * 512:(nh + 1) * 512], start=(c == 0), stop=(c == 15)),
                              r=[t_g16_2[b % 2], t_wlo], w=[pst[pb]])
                    fw.op(dve, lambda e: e.tensor_tensor(out=Xb[:, u, nh * 512:(nh + 1) * 512], in0=Xb[:, u, nh * 512:(nh + 1) * 512],
                                                         in1=ps[pb][:], op=ALU.add), r=[pst[pb], tXb], w=[tXb])

                def store(b):
                    rows, tblks = blk_rows(b)
                    fw.dma(sp, X1[rows, :].rearrange("(u p) d -> p u d", p=128), X2[b % 2][:], r=[tX2[b % 2]], w=tblks)

                NBLK = S // NB
                prep_load(0)
                for u in range(NU):
                    prep_norm(0, u)
                gelu_stage(0)
                for blk in range(NBLK):
                    cur[0] = blk
                    zip_emit([conv_chain(0, 0), conv_chain(0, 1)])
                    for n in range(8):
                        chains = [gate_chain(n, 0), gate_chain(n, 1)]
                        if n + 1 < 8:
                            chains += [conv_chain(n + 1, 0), conv_chain(n + 1, 1)]
                        zip_emit(chains)
                        if blk >= 1 and n < NU:
                            outproj_unit(blk - 1, n, 0)
                            outproj_unit(blk - 1, n, 1)
                            if n == NU - 1:
                                store(blk - 1)
                        if blk + 1 < NBLK:
                            if n == NU - 1:
                                prep_load(blk + 1)
                            if n in (6, 7):
                                prep_norm(blk + 1, 2 * (n - 6))
                                prep_norm(blk + 1, 2 * (n - 6) + 1)
                    if blk + 1 < NBLK:
                        gelu_stage(blk + 1)
                for u in range(NU):
                    outproj_unit(NBLK - 1, u, 0)
                    outproj_unit(NBLK - 1, u, 1)
                store(NBLK - 1)
                fw.barrier()

        if stage >= 6:
            phase_ffn(1, False, True)

        fw.finish()
    return nc


_CONSTS = None


def make_in_map(inp, b):
    global _CONSTS
    if _CONSTS is None:
        _CONSTS = host_consts()
    f = lambda a: np.ascontiguousarray(np.asarray(a, dtype=np.float32))
    m = {
        "x": f(inp["x"][b]),
        "norm_mix": f(inp["norm_mix"]), "norm_ffn": f(inp["norm_ffn"]), "norm_final": f(inp["norm_final"]),
        "nsa_w_in": f(inp["nsa_w_in"][0]), "nsa_b_gate": f(inp["nsa_b_gate"][0]),
        "nsa_cmp_pos": f(inp["nsa_cmp_pos"][0]), "nsa_cmp_w1": f(inp["nsa_cmp_w1"][0]),
        "nsa_cmp_b1": f(inp["nsa_cmp_b1"][0]), "nsa_cmp_w2": f(inp["nsa_cmp_w2"][0]),
        "nsa_cmp_b2": f(inp["nsa_cmp_b2"][0]), "nsa_w_out": f(inp["nsa_w_out"][0]),
        "lru_w_in": f(inp["lru_w_in"][0]), "lru_conv_w": f(inp["lru_conv_w"][0]),
        "lru_conv_b": f(inp["lru_conv_b"][0]), "lru_w_a": f(inp["lru_w_a"][0]), "lru_b_a": f(inp["lru_b_a"][0]),
        "lru_w_x": f(inp["lru_w_x"][0]), "lru_b_x": f(inp["lru_b_x"][0]), "lru_lambda": f(inp["lru_lambda"][0]),
        "lru_w_out": f(inp["lru_w_out"][0]), "ffn_w_in": f(inp["ffn_w_in"]), "ffn_w_out": f(inp["ffn_w_out"]),
    }
    m.update(_CONSTS)
    return m


def kernel(**inputs):
    nc = build(debug=False)
    n = 4
    maps = [make_in_map(inputs, b) for b in range(n)]
    res = run_bass_kernel_spmd(nc, maps, core_ids=list(range(n)))
    return np.stack([np.asarray(res.results[b]["out"], dtype=np.float32) for b in range(n)], axis=0)
```

```python
import contextlib
import numpy as np
import ml_dtypes
import concourse.bass as bass
import concourse.mybir as mybir
from concourse.bass_utils import run_bass_kernel_spmd

F32 = mybir.dt.float32
BF16 = mybir.dt.bfloat16
AF = mybir.ActivationFunctionType
ALU = mybir.AluOpType
AX = mybir.AxisListType

S = 4096
D = 1024
NT = S // 128
NSA_IN = 2608
D_RNN = 1408
D_FF = 2816
EPS = 1e-6
SLOPES = [2.0 ** (-8.0 * (h + 1) / 16) for h in range(16)]
BIGD = 1.0e6


class Dom:
    def __init__(self, fw, name, unit):
        self.sem = fw.es.enter_context(fw.nc.semaphore(name))
        self.unit = unit
        self.count = 0


class T:
    __slots__ = ("w", "r", "dd")

    def __init__(self):
        self.w = None
        self.r = {}
        self.dd = None


class Eng:
    def __init__(self, fw, name, eng, is_pe=False, has_dom=True):
        self.name = name
        self.eng = eng
        self.is_pe = is_pe
        self.dom = Dom(fw, "c_" + name, 1) if has_dom else None
        self.known = {}


class FW:
    def __init__(self, nc):
        self.nc = nc
        self.es = contextlib.ExitStack()
        self.pe = Eng(self, "pe", nc.tensor, is_pe=True)
        self.act = Eng(self, "act", nc.scalar)
        self.dve = Eng(self, "dve", nc.vector)
        self.pool = Eng(self, "pool", nc.gpsimd)
        self.sp = Eng(self, "sp", nc.sync, has_dom=False)
        self.dma_doms = []
        self.free_doms = []
        self.uid = 0

    def sbuf(self, st, name, shape, dt):
        self.uid += 1
        return st.enter_context(self.nc.sbuf_tensor("%s_%d" % (name, self.uid), list(shape), dt))

    def _waits(self, E, r, w):
        deps = {}
        for t in r:
            if t.w is not None and deps.get(t.w[0], 0) < t.w[1]:
                deps[t.w[0]] = t.w[1]
        for t in w:
            if t.w is not None and deps.get(t.w[0], 0) < t.w[1]:
                deps[t.w[0]] = t.w[1]
            for d, s in t.r.items():
                if deps.get(d, 0) < s:
                    deps[d] = s
        for d, s in deps.items():
            if E.is_pe and d is E.dom:
                continue
            if E.known.get(d, 0) >= s:
                continue
            E.eng.wait_ge(d.sem, s * d.unit)
            E.known[d] = s

    def op(self, E, fn, r=(), w=()):
        self._waits(E, r, w)
        ins = fn(E.eng)
        d = E.dom
        d.count += 1
        ins.then_inc(d.sem, 1)
        for t in r:
            t.r[d] = d.count
        for t in w:
            t.w = (d, d.count)
            t.r = {}
        return ins

    def dma(self, E, out, in_, r=(), w=(), **kw):
        self._waits(E, r, w)
        t0 = w[0] if len(w) else r[0]
        if t0.dd is None:
            t0.dd = Dom(self, "d%d" % len(self.dma_doms), 16)
            self.dma_doms.append(t0.dd)
        d = t0.dd
        ins = E.eng.dma_start(out=out, in_=in_, **kw)
        d.count += 1
        ins.then_inc(d.sem, 16)
        for t in r:
            t.r[d] = d.count
        for t in w:
            t.w = (d, d.count)
            t.r = {}
        return ins

    def barrier(self):
        doms = [d for d in self.dma_doms if d.count] + [X.dom for X in (self.pe, self.act, self.dve, self.pool) if X.dom.count]
        for E in (self.pe, self.act, self.dve, self.pool, self.sp):
            for d in doms:
                if E.known.get(d, 0) < d.count:
                    E.eng.wait_ge(d.sem, d.count * d.unit)
                    E.known[d] = d.count

    def finish(self):
        E = self.sp
        for d in self.dma_doms:
            if d.count:
                E.eng.wait_ge(d.sem, d.count * d.unit)
        for X in (self.pe, self.act, self.dve, self.pool):
            if X.dom.count:
                E.eng.wait_ge(X.dom.sem, X.dom.count)


def host_consts():
    c = {}
    c["ident_bf"] = np.eye(128, dtype=np.float32).astype(ml_dtypes.bfloat16)
    c["ident_f"] = np.eye(128, dtype=np.float32)
    e = np.zeros((64, S), np.float32)
    for j in range(64):
        e[j, j * 64:(j + 1) * 64] = 256.0
    c["e256"] = e.astype(ml_dtypes.bfloat16)
    sr = np.arange(128)[:, None].astype(np.float32)
    tr = np.arange(128)[None, :].astype(np.float32)
    d0 = tr - sr
    c["d0"] = np.stack([np.where(d0 >= 0, d0, BIGD), d0, np.where(d0 < 0, d0, BIGD)]).astype(np.float32)
    i = np.arange(128)[:, None]
    m = np.arange(-248, 256)[None, :]
    dc = (i - 16 * m - 31).astype(np.float32)
    c["distc"] = np.where(dc >= 0, dc, BIGD).astype(np.float32)
    jp = np.arange(-62, 64)[None, :]
    ci = (np.arange(128)[:, None] // 64)
    fb = np.zeros((128, 126), np.float32)
    fb = np.where((jp == ci) | (jp == ci - 1), 100.0, fb)
    fb = np.where(jp > ci, -100.0, fb)
    c["fbias"] = fb.astype(np.float32)
    ov = np.zeros((256, 64), np.float32)
    for n in range(255):
        for j in range(64):
            if 16 * n < 64 * j + 64 and 16 * n + 32 > 64 * j:
                ov[n, j] = 1.0
    c["ovl"] = ov.reshape(2, 128, 64).astype(ml_dtypes.bfloat16)
    return c


def build(debug=False, stage=99):
    nc = bass.Bass("TRN2", target_bir_lowering=False)
    fw = FW(nc)
    pe, act, dve, pool, sp = fw.pe, fw.act, fw.dve, fw.pool, fw.sp

    def din(name, shape, dt=F32):
        return nc.dram_tensor(name, list(shape), dt, kind="ExternalInput").ap()

    def dscr(name, shape, dt):
        return nc.dram_tensor(name, list(shape), dt, kind="ExternalOutput" if debug else "Internal").ap()

    x_in = din("x", [S, D])
    norm_mix = din("norm_mix", [2, D])
    norm_ffn = din("norm_ffn", [2, D])
    norm_final = din("norm_final", [D])
    nsa_w_in = din("nsa_w_in", [D, NSA_IN])
    nsa_b_gate = din("nsa_b_gate", [48])
    nsa_cmp_pos = din("nsa_cmp_pos", [2, 32, 64])
    nsa_cmp_w1 = din("nsa_cmp_w1", [2, 2048, 256])
    nsa_cmp_b1 = din("nsa_cmp_b1", [2, 256])
    nsa_cmp_w2 = din("nsa_cmp_w2", [2, 256, 64])
    nsa_cmp_b2 = din("nsa_cmp_b2", [2, 64])
    nsa_w_out = din("nsa_w_out", [D, D])
    lru_w_in = din("lru_w_in", [D, 2 * D_RNN])
    lru_conv_w = din("lru_conv_w", [4, D_RNN])
    lru_conv_b = din("lru_conv_b", [D_RNN])
    lru_w_a = din("lru_w_a", [8, 176, 176])
    lru_b_a = din("lru_b_a", [D_RNN])
    lru_w_x = din("lru_w_x", [8, 176, 176])
    lru_b_x = din("lru_b_x", [D_RNN])
    lru_lambda = din("lru_lambda", [D_RNN])
    lru_w_out = din("lru_w_out", [D_RNN, D])
    ffn_w_in = din("ffn_w_in", [2, D, 2 * D_FF])
    ffn_w_out = din("ffn_w_out", [2, D_FF, D])
    c_ident_bf = din("ident_bf", [128, 128], BF16)
    c_ident_f = din("ident_f", [128, 128])
    c_e256 = din("e256", [64, S], BF16)
    c_d0 = din("d0", [3, 128, 128])
    c_distc = din("distc", [128, 504])
    c_fbias = din("fbias", [128, 126])
    c_ovl = din("ovl", [2, 128, 64], BF16)

    out_ap = nc.dram_tensor("out", [S, D], F32, kind="ExternalOutput").ap()

    QT = dscr("QT", [1024, S], BF16)
    KcT = dscr("KcT", [256, S], BF16)
    VcT = dscr("VcT", [256, S], BF16)
    KsT = dscr("KsT", [256, S], BF16)
    KwT = dscr("KwT", [256, S], BF16)
    Vs = dscr("Vs", [S, 256], BF16)
    Vw = dscr("Vw", [S, 256], BF16)
    G = dscr("G", [S, 48], F32)
    Y = dscr("Y", [S, D], BF16)
    X1 = dscr("X1", [S, D], F32)
    tQT, tKcT, tVcT, tKsT, tKwT, tVs, tVw, tG, tY, tX1 = [T() for _ in range(10)]

    with fw.es:
        gst = fw.es
        ps = [gst.enter_context(nc.psum_tensor("ps%d" % i, [128, 512], F32)) for i in range(8)]
        pst = [T() for _ in range(8)]
        ident_bf = fw.sbuf(gst, "identbf", [128, 128], BF16)
        ident_f = fw.sbuf(gst, "identf", [128, 128], F32)
        t_ident = T()
        fw.dma(sp, ident_bf[:], c_ident_bf, w=[t_ident])
        fw.dma(sp, ident_f[:], c_ident_f, w=[t_ident])
        epsc = fw.sbuf(gst, "epsc", [128, 1], F32)
        t_eps = T()
        fw.op(dve, lambda e: e.memset(epsc[:], EPS), w=[t_eps])

        rr = [0]

        def evac(out, in_, r, w, scale=None):
            rr[0] += 1
            if rr[0] % 2 == 0:
                if scale is None:
                    fw.op(act, lambda e: e.activation(out=out, in_=in_, func=AF.Copy), r=r, w=w)
                else:
                    fw.op(act, lambda e: e.activation(out=out, in_=in_, func=AF.Copy, scale=scale), r=r, w=w)
            else:
                if scale is None:
                    fw.op(dve, lambda e: e.tensor_copy(out=out, in_=in_), r=r, w=w)
                else:
                    fw.op(dve, lambda e: e.tensor_scalar(out=out, in0=in_, scalar1=scale, scalar2=None, op0=ALU.mult), r=r, w=w)

        def load_gain(st, g_ap, name):
            gt = fw.sbuf(st, name, [128, 8], F32)
            tg = T()
            fw.dma(sp, gt[:], g_ap.rearrange("(k p) -> p k", p=128), w=[tg], allow_slow_non_contiguous=True)
            return gt, tg

        cast_rr = [0]

        def load_cast(st, dst, tdst, src, nrows, ncols, gain=None, tgain=None, stg=None, tstg=None):
            CH = stg[0].shape[1]
            for c0 in range(0, ncols, CH):
                cw = min(CH, ncols - c0)
                k = cast_rr[0] % len(stg)
                cast_rr[0] += 1
                s_, ts_ = stg[k], tstg[k]
                fw.dma((sp, pool, act, sp)[k % 4], s_[0:nrows, 0:cw], src[:, c0:c0 + cw], w=[ts_])
                if k % 2 == 0:
                    if gain is None:
                        fw.op(dve, lambda e: e.tensor_copy(out=dst[:, c0:c0 + cw], in_=s_[0:nrows, 0:cw]), r=[ts_], w=[tdst])
                    else:
                        fw.op(dve, lambda e: e.tensor_scalar(out=dst[:, c0:c0 + cw], in0=s_[0:nrows, 0:cw], scalar1=gain,
                                                             scalar2=None, op0=ALU.mult), r=[ts_, tgain], w=[tdst])
                else:
                    if gain is None:
                        fw.op(act, lambda e: e.activation(out=dst[:, c0:c0 + cw], in_=s_[0:nrows, 0:cw], func=AF.Copy), r=[ts_], w=[tdst])
                    else:
                        fw.op(act, lambda e: e.activation(out=dst[:, c0:c0 + cw], in_=s_[0:nrows, 0:cw], func=AF.Copy, scale=gain),
                              r=[ts_, tgain], w=[tdst])

        def rmsnorm_bf(xt, tx, hn, thn, junk, tjunk, st_small, tsm):
            ss, sd, rs = st_small
            fw.op(act, lambda e: e.activation(out=junk, in_=xt, func=AF.Square, accum_out=ss), r=[tx], w=[tjunk, tsm])
            fw.op(act, lambda e: e.activation(out=sd, in_=ss, func=AF.Sqrt, bias=epsc[:], scale=1.0 / D), r=[tsm, t_eps], w=[tsm])
            fw.op(dve, lambda e: e.reciprocal(out=rs, in_=sd), r=[tsm], w=[tsm])
            fw.op(dve, lambda e: e.tensor_scalar(out=hn, in0=xt, scalar1=rs, scalar2=None, op0=ALU.mult), r=[tx, tsm], w=[thn])

        def transpose_to(hn, thn, dstT, tdst, nchunk, pbank, tpbank, col0, rows=128):
            pv = pbank[:].bitcast(BF16)
            for k0 in range(0, nchunk, 8):
                kn = min(8, nchunk - k0)
                for k in range(kn):
                    fw.op(pe, lambda e: e.transpose(out=pv[:, k * 128:(k + 1) * 128], in_=hn[:, (k0 + k) * 128:(k0 + k + 1) * 128],
                                                    identity=ident_bf[:]), r=[thn, t_ident], w=[tpbank])
                evac(dstT[:, k0:k0 + kn, col0:col0 + 128], pv[:, 0:kn * 128].rearrange("p (k t) -> p k t", k=kn), r=[tpbank], w=[tdst])

        if stage >= 1:
            with contextlib.ExitStack() as st:
                w_in = fw.sbuf(st, "w_in", [128, 8, NSA_IN], BF16)
                t_w = T()
                g0, tg0 = load_gain(st, norm_mix[0], "g0")
                stg = [fw.sbuf(st, "stg", [128, 2608], F32) for _ in range(2)]
                tstg = [T(), T()]
                for k in range(8):
                    load_cast(st, w_in[:, k, :], t_w, nsa_w_in[k * 128:(k + 1) * 128, :], 128, NSA_IN,
                              gain=g0[:, k:k + 1], tgain=tg0, stg=stg, tstg=tstg)
                bg = fw.sbuf(st, "bg", [128, 48], F32)
                t_bg = T()
                fw.dma(sp, bg[:], nsa_b_gate.partition_broadcast(128), w=[t_bg])
                xt = [fw.sbuf(st, "xt", [128, D], F32) for _ in range(2)]
                txt = [T(), T()]
                hn = [fw.sbuf(st, "hn", [128, D], BF16) for _ in range(2)]
                thn = [T(), T()]
                junk = fw.sbuf(st, "junk", [128, D], BF16)
                tjunk = T()
                sm = [[fw.sbuf(st, "sm", [128, 1], F32) for _ in range(3)] for _ in range(2)]
                tsm = [T(), T()]
                hnT = [fw.sbuf(st, "hnT", [128, 8, 512], BF16) for _ in range(2)]
                thnT = [T(), T()]
                fst = [fw.sbuf(st, "fst", [128, 16, 512], BF16) for _ in range(2)]
                tfst = [T(), T()]
                tst = [fw.sbuf(st, "tst", [128, 4, 512], BF16) for _ in range(2)]
                ttst = [T(), T()]
                gst_ = [fw.sbuf(st, "gst", [128, 4, 48], F32) for _ in range(2)]
                tgst = [T(), T()]
                fcols = [c * 128 for c in range(8)] + [1024, 1152, 1280, 1408, 1536, 1664, 2048, 2176]
                it = 0
                for R in range(8):
                    b = R % 2
                    for u in range(4):
                        ti = 4 * R + u
                        s2 = it % 2
                        it += 1
                        fw.dma(sp, xt[s2][:], x_in[ti * 128:(ti + 1) * 128, :], w=[txt[s2]])
                        rmsnorm_bf(xt[s2][:], txt[s2], hn[s2][:], thn[s2], junk[:], tjunk, [z[:] for z in sm[s2]], tsm[s2])
                        transpose_to(hn[s2], thn[s2], hnT[b], thnT[b], 8, ps[6 + s2], pst[6 + s2], u * 128)
                    for ci, c0 in enumerate(fcols):
                        pb = ci % 4
                        for k in range(8):
                            fw.op(pe, lambda e: e.matmul(out=ps[pb][:], lhsT=w_in[:, k, c0:c0 + 128], rhs=hnT[b][:, k, :],
                                                         start=(k == 0), stop=(k == 7)), r=[t_w, thnT[b]], w=[pst[pb]])
                        evac(fst[b][:, ci, :], ps[pb][:], r=[pst[pb]], w=[tfst[b]], scale=(0.125 if ci < 8 else None))
                    cs = slice(R * 512, (R + 1) * 512)
                    fw.dma(sp, QT.rearrange("(c p) t -> p c t", p=128)[:, :, cs], fst[b][:, 0:8, :], r=[tfst[b]], w=[tQT])
                    fw.dma(pool, KcT.rearrange("(c p) t -> p c t", p=128)[:, :, cs], fst[b][:, 8:10, :], r=[tfst[b]], w=[tKcT])
                    fw.dma(pool, VcT.rearrange("(c p) t -> p c t", p=128)[:, :, cs], fst[b][:, 10:12, :], r=[tfst[b]], w=[tVcT])
                    fw.dma(sp, KsT.rearrange("(c p) t -> p c t", p=128)[:, :, cs], fst[b][:, 12:14, :], r=[tfst[b]], w=[tKsT])
                    fw.dma(pool, KwT.rearrange("(c p) t -> p c t", p=128)[:, :, cs], fst[b][:, 14:16, :], r=[tfst[b]], w=[tKwT])
                    for u in range(4):
                        pb = 4 + (u % 2)
                        for (c0, cw, o0) in ((1792, 256, 0), (2304, 256, 256)):
                            for k in range(8):
                                fw.op(pe, lambda e: e.matmul(out=ps[pb][:, o0:o0 + cw], lhsT=hnT[b][:, k, u * 128:(u + 1) * 128],
                                                             rhs=w_in[:, k, c0:c0 + cw], start=(k == 0), stop=(k == 7)),
                                      r=[t_w, thnT[b]], w=[pst[pb]])
                        evac(tst[b][:, u, :], ps[pb][:], r=[pst[pb]], w=[ttst[b]])
                        for k in range(8):
                            fw.op(pe, lambda e: e.matmul(out=ps[pb][:, 0:48], lhsT=hnT[b][:, k, u * 128:(u + 1) * 128],
                                                         rhs=w_in[:, k, 2560:2608], start=(k == 0), stop=(k == 7)),
                                  r=[t_w, thnT[b]], w=[pst[pb]])
                        fw.op(dve, lambda e: e.tensor_tensor(out=gst_[b][:, u, :], in0=ps[pb][:, 0:48], in1=bg[:], op=ALU.add),
                              r=[pst[pb], t_bg], w=[tgst[b]])
                    fw.op(act, lambda e: e.activation(out=gst_[b][:], in_=gst_[b][:], func=AF.Sigmoid), r=[tgst[b]], w=[tgst[b]])
                    rs_ = slice(R * 512, (R + 1) * 512)
                    fw.dma(sp, Vs[rs_, :].rearrange("(u p) c -> p u c", p=128), tst[b][:, :, 0:256], r=[ttst[b]], w=[tVs])
                    fw.dma(pool, Vw[rs_, :].rearrange("(u p) c -> p u c", p=128), tst[b][:, :, 256:512], r=[ttst[b]], w=[tVw])
                    fw.dma(sp, G[rs_, :].rearrange("(u p) c -> p u c", p=128), gst_[b][:], r=[tgst[b]], w=[tG])

        fw.barrier()
        if stage >= 2:
            with contextlib.ExitStack() as st:
                kcT_all = fw.sbuf(st, "kcT", [128, 4, 256], BF16)
                t_kc = T()
                vc_all = fw.sbuf(st, "vc", [128, 4, 2, 64], BF16)
                t_vc = T()
                fw.op(pool, lambda e: e.memset(vc_all[:], 0.0), w=[t_vc])
                fw.op(pool, lambda e: e.memset(kcT_all[:], 0.0), w=[t_kc])
                with contextlib.ExitStack() as sb:
                    stgB = fw.sbuf(sb, "stgB", [64, 32, 256], F32)
                    t_stgB = T()
                    w1 = fw.sbuf(sb, "w1", [64, 32, 256], BF16)
                    t_w1 = T()
                    w2f = fw.sbuf(sb, "w2f", [128, 2, 64], F32)
                    w2 = fw.sbuf(sb, "w2", [128, 2, 64], BF16)
                    t_w2f, t_w2 = T(), T()
                    posf = fw.sbuf(sb, "posf", [64, 32], F32)
                    posT = fw.sbuf(sb, "posT", [64, 32], BF16)
                    t_posf, t_posT = T(), T()
                    b1t = fw.sbuf(sb, "b1t", [128, 2], F32)
                    c1b = fw.sbuf(sb, "c1b", [128, 2], F32)
                    b2col = fw.sbuf(sb, "b2col", [64, 1], F32)
                    b2row = fw.sbuf(sb, "b2row", [128, 64], F32)
                    t_b1, t_c1b, t_b2c, t_b2r = T(), T(), T(), T()
                    rawT = [fw.sbuf(sb, "rawT", [64, S], BF16) for _ in range(2)]
                    t_raw = [T(), T()]
                    hidT = fw.sbuf(sb, "hidT", [128, 2, 256], BF16)
                    t_hid = T()
                    for kv in range(2):
                        fw.dma(sp, stgB[:], nsa_cmp_w1[kv].rearrange("(l d) h -> d l h", d=64), w=[t_stgB])
                        for q4 in range(4):
                            E = (dve, pool, act, dve)[q4]
                            if E is act:
                                fw.op(E, lambda e: e.activation(out=w1[:, q4 * 8:(q4 + 1) * 8, :], in_=stgB[:, q4 * 8:(q4 + 1) * 8, :], func=AF.Copy),
                                      r=[t_stgB], w=[t_w1])
                            else:
                                fw.op(E, lambda e: e.tensor_copy(out=w1[:, q4 * 8:(q4 + 1) * 8, :], in_=stgB[:, q4 * 8:(q4 + 1) * 8, :]),
                                      r=[t_stgB], w=[t_w1])
                        fw.dma(sp, w2f[:], nsa_cmp_w2[kv].rearrange("(c p) d -> p c d", p=128), w=[t_w2f])
                        fw.op(dve, lambda e: e.tensor_copy(out=w2[:], in_=w2f[:]), r=[t_w2f], w=[t_w2])
                        fw.dma(sp, posf[:], nsa_cmp_pos[kv].rearrange("l d -> d l"), w=[t_posf], allow_slow_non_contiguous=True)
                        fw.op(dve, lambda e: e.tensor_copy(out=posT[:], in_=posf[:]), r=[t_posf], w=[t_posT])
                        fw.dma(sp, b1t[:], nsa_cmp_b1[kv].rearrange("(c p) -> p c", p=128), w=[t_b1], allow_slow_non_contiguous=True)
                        fw.dma(sp, b2col[:], nsa_cmp_b2[kv].rearrange("(d o) -> d o", o=1), w=[t_b2c], allow_slow_non_contiguous=True)
                        fw.dma(sp, b2row[:], nsa_cmp_b2[kv].partition_broadcast(128), w=[t_b2r])
                        for c in range(2):
                            for l in range(32):
                                fw.op(pe, lambda e: e.matmul(out=ps[0][:, c:c + 1], lhsT=w1[:, l, c * 128:(c + 1) * 128], rhs=posT[:, l:l + 1],
                                                             start=(l == 0), stop=(l == 31)), r=[t_w1, t_posT], w=[pst[0]])
                        fw.op(dve, lambda e: e.tensor_tensor(out=c1b[:], in0=ps[0][:, 0:2], in1=b1t[:], op=ALU.add), r=[pst[0], t_b1], w=[t_c1b])
                        for g in range(4):
                            rt, trt = rawT[g % 2], t_raw[g % 2]
                            src = (KcT, VcT)[kv]
                            fw.dma(sp, rt[:], src[g * 64:(g + 1) * 64, :], r=[(tKcT, tVcT)[kv]], w=[trt])
                            for c in range(2):
                                for l in range(32):
                                    fw.op(pe, lambda e: e.matmul(out=ps[1 + c][:, 0:255], lhsT=w1[:, l, c * 128:(c + 1) * 128],
                                                                 rhs=rt[:, l:l + 16 * 254 + 1:16], start=(l == 0), stop=(l == 31)),
                                          r=[t_w1, trt], w=[pst[1 + c]])
                                fw.op(act, lambda e: e.activation(out=hidT[:, c, 0:255], in_=ps[1 + c][:, 0:255], func=AF.Gelu_apprx_tanh,
                                                                  bias=c1b[:, c:c + 1]), r=[pst[1 + c], t_c1b], w=[t_hid])
                            if kv == 0:
                                for c in range(2):
                                    fw.op(pe, lambda e: e.matmul(out=ps[3][0:64, 0:255], lhsT=w2[:, c, :], rhs=hidT[:, c, 0:255],
                                                                 start=(c == 0), stop=(c == 1)), r=[t_w2, t_hid], w=[pst[3]])
                                fw.op(dve, lambda e: e.tensor_scalar(out=kcT_all[0:64, g, 0:255], in0=ps[3][0:64, 0:255], scalar1=b2col[:],
                                                                     scalar2=None, op0=ALU.add), r=[pst[3], t_b2c], w=[t_kc])
                            else:
                                for nch, (n0, nn) in enumerate(((0, 128), (128, 127))):
                                    for c in range(2):
                                        fw.op(pe, lambda e: e.matmul(out=ps[3][0:nn, nch * 64:(nch + 1) * 64], lhsT=hidT[:, c, n0:n0 + nn],
                                                                     rhs=w2[:, c, :], start=(c == 0), stop=(c == 1)), r=[t_w2, t_hid], w=[pst[3]])
                                    fw.op(dve, lambda e: e.tensor_tensor(out=vc_all[0:nn, g, nch, :], in0=ps[3][0:nn, nch * 64:(nch + 1) * 64],
                                                                         in1=b2row[0:nn, :], op=ALU.add), r=[pst[3], t_b2r], w=[t_vc])
                fw.barrier()
                if debug:
                    dbg_kc = nc.dram_tensor("dbg_kc", [128, 4, 256], BF16, kind="ExternalOutput").ap()
                    dbg_vc = nc.dram_tensor("dbg_vc", [128, 4, 2, 64], BF16, kind="ExternalOutput").ap()
                    fw.dma(sp, dbg_kc, kcT_all[:], r=[t_kc])
                    fw.dma(sp, dbg_vc, vc_all[:], r=[t_vc])

                if stage >= 3:
                    d0 = fw.sbuf(st, "d0", [128, 3, 128], F32)
                    distc = fw.sbuf(st, "distc", [128, 504], F32)
                    fbias = fw.sbuf(st, "fbias", [128, 126], F32)
                    t_cc = T()
                    fw.dma(sp, d0[:], c_d0.rearrange("k p t -> p k t"), w=[t_cc])
                    fw.dma(sp, distc[:], c_distc, w=[t_cc])
                    fw.dma(sp, fbias[:], c_fbias, w=[t_cc])
                    ovl = fw.sbuf(st, "ovl", [128, 2, 64], BF16)
                    fw.dma(sp, ovl[:], c_ovl.rearrange("k p j -> p k j"), w=[t_cc])
                    bcb = fw.sbuf(st, "bcb", [128, 4, 504], BF16)
                    ks_sel = fw.sbuf(st, "ks_sel", [128, S], BF16)
                    t_ks = T()
                    fw.dma(sp, ks_sel[64:128, :], c_e256, w=[t_ks])
                    kw = fw.sbuf(st, "kw", [128, S], BF16)
                    t_kw = T()
                    fw.op(pool, lambda e: e.memset(kw[64:128, :], 0.0), w=[t_kw])
                    vs_aug = fw.sbuf(st, "vs_aug", [128, 32, 65], BF16)
                    vw_aug = fw.sbuf(st, "vw_aug", [128, 32, 65], BF16)
                    t_vs, t_vw = T(), T()
                    fw.op(pool, lambda e: e.memset(vs_aug[:], 1.0), w=[t_vs])
                    fw.op(pool, lambda e: e.memset(vw_aug[:], 1.0), w=[t_vw])
                    qstack = fw.sbuf(st, "qstack", [128, 4, S], BF16)
                    t_q = T()
                    fw.op(pool, lambda e: e.memset(qstack[:], 0.0), w=[t_q])
                    t_qm = [T() for _ in range(32)]
                    A = fw.sbuf(st, "A", [128, 33, 512], BF16)
                    t_A = T()
                    bc = fw.sbuf(st, "bc", [128, 4, 504], F32)
                    t_bc = T()
                    gt = fw.sbuf(st, "gt", [128, 32, 12], F32)
                    t_gt = T()
                    sc = fw.sbuf(st, "sc", [128, 4, 256], F32)
                    pc = fw.sbuf(st, "pc", [128, 4, 256], F32)
                    t_sc, t_pc = T(), T()
                    rowsum = fw.sbuf(st, "rowsum", [128, 4], F32)
                    rinv = fw.sbuf(st, "rinv", [128, 4], F32)
                    t_rs, t_ri = T(), T()
                    pn = fw.sbuf(st, "pn", [128, 4, 256], BF16)
                    t_pn = T()
                    fw.op(pool, lambda e: e.memset(pn[:], 0.0), w=[t_pn])
                    psumh = fw.sbuf(st, "psumh", [128, 256], F32)
                    t_ph = T()
                    fw.op(pool, lambda e: e.memset(psumh[:], 0.0), w=[t_ph])
                    pnT = fw.sbuf(st, "pnT", [128, 8, 128], BF16)
                    t_pnT = T()
                    imp = fw.sbuf(st, "imp", [128, 64], F32)
                    score = fw.sbuf(st, "score", [128, 64], F32)
                    score2 = fw.sbuf(st, "score2", [128, 64], F32)
                    m8a = fw.sbuf(st, "m8a", [128, 8], F32)
                    m8b = fw.sbuf(st, "m8b", [128, 8], F32)
                    t_imp, t_score, t_score2, t_m8a, t_m8b = T(), T(), T(), T(), T()
                    msel = fw.sbuf(st, "msel", [128, 128], BF16)
                    t_msel = T()
                    fw.op(pool, lambda e: e.memset(msel[:], 0.0), w=[t_msel])
                    NPT = 3
                    pt = [fw.sbuf(st, "pt", [128, 512], BF16) for _ in range(NPT)]
                    pa = [fw.sbuf(st, "pa", [128, 512], BF16) for _ in range(NPT)]
                    t_pt = [T() for _ in range(NPT)]
                    t_pa = [T() for _ in range(NPT)]
                    osb = fw.sbuf(st, "osb", [65, 512], F32)
                    t_osb = T()
                    f4 = fw.sbuf(st, "f4", [128, 4], F32)
                    t_f4 = T()
                    yacc = fw.sbuf(st, "yacc", [128, 256], F32)
                    t_yacc = T()
                    ybf = [fw.sbuf(st, "ybf", [128, 256], BF16) for _ in range(2)]
                    t_ybf = [T(), T()]
                    pv6 = ps[6][:].bitcast(BF16)
                    pst7b = pst[7]
                    yacc2 = [yacc, fw.sbuf(st, "yacc2", [128, 256], F32)]
                    t_yacc2 = [t_yacc, T()]
                    osb2 = [osb, fw.sbuf(st, "osb2", [65, 512], F32)]
                    t_osb2 = [t_osb, T()]
                    f2 = [fw.sbuf(st, "f2", [128, 2], F32) for _ in range(2)]
                    t_f2 = [T(), T()]
                    NPT2 = 6
                    pt = pt + [fw.sbuf(st, "pt", [128, 512], BF16) for _ in range(3)]
                    pa = pa + [fw.sbuf(st, "pa", [128, 512], BF16) for _ in range(3)]
                    t_pt = t_pt + [T(), T(), T()]
                    t_pa = t_pa + [T(), T(), T()]
                    kctr = [0]
                    import os as _os
                    _NG = int(_os.environ.get('NSA_G', '4')); _NA = int(_os.environ.get('NSA_A', '32'))

                    CS = []
                    for ci in range(4):
                        if ci == 0:
                            tiles = [sc, pc, pn, psumh, pnT, rowsum, rinv, imp, score, score2, m8a, m8b, msel]
                            trk = [t_sc, t_pc, t_pn, t_ph, t_pnT, t_rs, t_ri, t_imp, t_score, t_score2, t_m8a, t_m8b, t_msel]
                        else:
                            tiles = [fw.sbuf(st, "sc", [128, 4], F32), fw.sbuf(st, "pc", [128, 4], F32),
                                     fw.sbuf(st, "pn", [128, 4, 256], BF16), fw.sbuf(st, "psumh", [128, 4], F32),
                                     fw.sbuf(st, "pnT", [128, 8, 128], BF16), fw.sbuf(st, "rowsum", [128, 4], F32),
                                     fw.sbuf(st, "rinv", [128, 4], F32), fw.sbuf(st, "imp", [128, 64], F32),
                                     fw.sbuf(st, "score", [128, 64], F32), fw.sbuf(st, "score2", [128, 64], F32),
                                     fw.sbuf(st, "m8a", [128, 8], F32), fw.sbuf(st, "m8b", [128, 8], F32),
                                     fw.sbuf(st, "msel", [128, 128], BF16)]
                            trk = [T() for _ in range(13)]
                            fw.op(pool, lambda e: e.memset(tiles[2][:], 0.0), w=[trk[2]])
                            fw.op(pool, lambda e: e.memset(tiles[12][:], 0.0), w=[trk[12]])
                        tiles = tiles + [fw.sbuf(st, "gf", [128, 4], F32), fw.sbuf(st, "imp4", [128, 4, 64], F32)]
                        trk = trk + [T(), T()]
                        CS.append(tuple(tiles + trk))
                    ocbuf = fw.sbuf(st, "ocbuf", [128, 32, 256], BF16)
                    t_oc = [T() for _ in range(32)]

                    def cmp_thunks(g, a, ci):
                        th = []
                        (sc_, pc_, pcb, psumh_, pnT, rowsum, rinv, imp, score, score2, m8a, m8b, msel, gf, imp4,
                         t_sc_, t_pc_, t_pcb, t_ph_, t_pnT, t_rs, t_ri, t_imp, t_score, t_score2, t_m8a, t_m8b, t_msel, t_gf, t_imp4) = CS[ci]
                        bS, bT = 2 * ci, 2 * ci + 1
                        pvT = ps[bT][:].bitcast(BF16)
                        ta = slice(a * 128, (a + 1) * 128)
                        boff = 248 - 8 * a
                        for hp in range(2):
                            def f_mm(hp=hp):
                                for hh in range(2):
                                    h = 2 * hp + hh
                                    fw.op(pe, lambda e: e.matmul(out=ps[bS][:, hh * 256:hh * 256 + 255], lhsT=qstack[:, h, ta],
                                                                 rhs=kcT_all[:, g, 0:255], start=True, stop=False), r=[t_q, t_qm[a], t_kc], w=[pst[bS]])
                                    fw.op(pe, lambda e: e.matmul(out=ps[bS][:, hh * 256:hh * 256 + 255], lhsT=ident_bf[:],
                                                                 rhs=bcb[:, h, boff:boff + 255], start=False, stop=True), r=[t_bc, t_ident], w=[pst[bS]])
                            th.append(f_mm)
                            for hh in range(2):
                                def f_exp(hp=hp, hh=hh):
                                    h = 2 * hp + hh
                                    fw.op(act, lambda e: e.activation(out=pcb[:, h, 0:255], in_=ps[bS][:, hh * 256:hh * 256 + 255], func=AF.Exp,
                                                                      accum_out=rowsum[:, h:h + 1]), r=[pst[bS]], w=[t_pcb, t_rs])
                                th.append(f_exp)

                        def f_rinv():
                            fw.op(dve, lambda e: e.tensor_scalar(out=rinv[:], in0=rowsum[:], scalar1=1e-30, scalar2=None, op0=ALU.max),
                                  r=[t_rs], w=[t_ri])
                            fw.op(dve, lambda e: e.reciprocal(out=rinv[:], in_=rinv[:]), r=[t_ri], w=[t_ri])
                        th.append(f_rinv)

                        def f_pnT():
                            for h in range(4):
                                for nch in range(2):
                                    fw.op(pe, lambda e: e.transpose(out=pvT[:, (h * 2 + nch) * 128:(h * 2 + nch + 1) * 128],
                                                                    in_=pcb[:, h, nch * 128:(nch + 1) * 128], identity=ident_bf[:]),
                                          r=[t_pcb, t_ident], w=[pst[bT]])
                            fw.op(act, lambda e: e.activation(out=pnT[:], in_=pvT.rearrange("p (k t) -> p k t", k=8), func=AF.Copy),
                                  r=[pst[bT]], w=[t_pnT])
                        th.append(f_pnT)
                        th.append(lambda: fw.op(dve, lambda e: e.tensor_tensor(out=gf[:], in0=rinv[:], in1=gt[:, a, 0:12:3], op=ALU.mult),
                                                r=[t_ri, t_gt], w=[t_gf]))

                        def f_oc():
                            for h in range(4):
                                for nch in range(2):
                                    fw.op(pe, lambda e: e.matmul(out=ps[bS][:, h * 64:(h + 1) * 64], lhsT=pnT[:, h * 2 + nch, :],
                                                                 rhs=vc_all[:, g, nch, :], start=(nch == 0), stop=(nch == 1)),
                                          r=[t_pnT, t_vc], w=[pst[bS]])
                                for nch in range(2):
                                    fw.op(pe, lambda e: e.matmul(out=ps[bS][:, 256 + h * 64:256 + (h + 1) * 64], lhsT=pnT[:, h * 2 + nch, :],
                                                                 rhs=ovl[:, nch, :], start=(nch == 0), stop=(nch == 1)),
                                          r=[t_pnT, t_cc], w=[pst[bS]])
                        th.append(f_oc)
                        th.append(lambda: fw.op(dve, lambda e: e.tensor_tensor(out=ocbuf[:, a, :].rearrange("p (h c) -> p h c", h=4),
                                                                               in0=ps[bS][:, 0:256].rearrange("p (h c) -> p h c", h=4),
                                                                               in1=gf[:].unsqueeze(2).to_broadcast([128, 4, 64]), op=ALU.mult),
                                                r=[pst[bS], t_gf], w=[t_oc[a]]))
                        th.append(lambda: fw.op(dve, lambda e: e.tensor_tensor(out=imp4[:], in0=ps[bS][:, 256:512].rearrange("p (h c) -> p h c", h=4),
                                                                               in1=rinv[:].unsqueeze(2).to_broadcast([128, 4, 64]), op=ALU.mult),
                                                r=[pst[bS], t_ri], w=[t_imp4]))

                        def f_imp():
                            fw.op(dve, lambda e: e.tensor_reduce(out=imp[:], in_=imp4[:].rearrange("p h j -> p j h"), axis=AX.X, op=ALU.add),
                                  r=[t_imp4], w=[t_imp])
                            fw.op(dve, lambda e: e.tensor_tensor(out=score[:], in0=imp[:], in1=fbias[:, 62 - 2 * a:62 - 2 * a + 64], op=ALU.add),
                                  r=[t_imp, t_cc], w=[t_score])
                            fw.op(dve, lambda e: e.memset(score[:, 0:1], 100.0), w=[t_score])
                        th.append(f_imp)

                        def f_topk():
                            fw.op(dve, lambda e: e.max(out=m8a[:], in_=score[:]), r=[t_score], w=[t_m8a])
                            fw.op(dve, lambda e: e.match_replace(out=score2[:], in_to_replace=m8a[:], in_values=score[:], imm_value=-1.0e9),
                                  r=[t_score, t_m8a], w=[t_score2])
                            fw.op(dve, lambda e: e.max(out=m8b[:], in_=score2[:]), r=[t_score2], w=[t_m8b])
                            fw.op(dve, lambda e: e.tensor_scalar(out=msel[:, 64:128], in0=score[:], scalar1=m8b[:, 7:8], scalar2=-1.0,
                                                                 op0=ALU.is_ge, op1=ALU.add), r=[t_score, t_m8b], w=[t_msel])
                        th.append(f_topk)

                        def f_mask():
                            fw.op(pe, lambda e: e.transpose(out=pvT[:, 0:128], in_=msel[:], identity=ident_bf[:]), r=[t_msel, t_ident], w=[pst[bT]])
                            for h in range(4):
                                if h % 2 == 0:
                                    fw.op(act, lambda e: e.activation(out=qstack[64:128, h, ta], in_=pvT[64:128, 0:128], func=AF.Copy),
                                          r=[pst[bT]], w=[t_qm[a]])
                                else:
                                    fw.op(dve, lambda e: e.tensor_copy(out=qstack[64:128, h, ta], in_=pvT[64:128, 0:128]), r=[pst[bT]], w=[t_qm[a]])
                        th.append(f_mask)
                        return th

                    def obank(a, kind):
                        return ((3, 3), (4, 4))[kind][a % 2]

                    f4k = [fw.sbuf(st, "f4k", [128, 4], F32) for _ in range(2)]
                    t_f4k = [T(), T()]
                    otmp = [fw.sbuf(st, "otmp", [128, 4, 64], F32) for _ in range(2)]
                    t_otmp = [T(), T()]

                    def finish_thunks(g, a, kind):
                        ya, tya = yacc2[a % 2], t_yacc2[a % 2]
                        ob, tob = osb2[kind], t_osb2[kind]
                        bank = obank(a, kind)
                        gcol = 1 + kind
                        ff, tff = f4k[kind], t_f4k[kind]
                        tmp, ttmp = otmp[kind], t_otmp[kind]
                        ta = slice(a * 128, (a + 1) * 128)
                        th = []
                        th.append(lambda: fw.op(act, lambda e: e.activation(out=ob[:], in_=ps[bank][0:65, :], func=AF.Copy), r=[pst[bank]], w=[tob]))

                        def f_tr():
                            for h in range(4):
                                fw.op(pe, lambda e: e.transpose(out=ps[7][:, h * 65:(h + 1) * 65], in_=ob[:, h * 128:(h + 1) * 128],
                                                                identity=ident_f[0:65, 0:65]), r=[tob, t_ident], w=[pst[7]])
                        th.append(f_tr)
                        th.append(lambda: fw.op(dve, lambda e: e.tensor_scalar(out=ff[:], in0=ps[7][:, 64:260:65], scalar1=1e-30, scalar2=None, op0=ALU.max),
                                                r=[pst[7]], w=[tff]))
                        th.append(lambda: fw.op(dve, lambda e: e.reciprocal(out=ff[:], in_=ff[:]), r=[tff], w=[tff]))
                        th.append(lambda: fw.op(dve, lambda e: e.tensor_tensor(out=ff[:], in0=ff[:], in1=gt[:, a, gcol:12:3], op=ALU.mult),
                                                r=[tff, t_gt], w=[tff]))
                        th.append(lambda: fw.op(dve, lambda e: e.tensor_tensor(out=tmp[:], in0=ps[7][:, 0:260].rearrange("p (h c) -> p h c", c=65)[:, :, 0:64],
                                                                               in1=ff[:].unsqueeze(2).to_broadcast([128, 4, 64]), op=ALU.mult),
                                                r=[pst[7], tff], w=[ttmp]))
                        if kind == 0:
                            th.append(lambda: fw.op(pool, lambda e: e.tensor_tensor(out=ya[:], in0=tmp[:].rearrange("p h c -> p (h c)"), in1=ocbuf[:, a, :], op=ALU.add),
                                                    r=[ttmp, t_oc[a]], w=[tya]))
                        else:
                            th.append(lambda: fw.op(pool, lambda e: e.tensor_tensor(out=ya[:], in0=tmp[:].rearrange("p h c -> p (h c)"), in1=ya[:], op=ALU.add),
                                                    r=[ttmp, tya], w=[tya]))

                            def f_out():
                                yb, tyb = ybf[a % 2], t_ybf[a % 2]
                                fw.op(act, lambda e: e.activation(out=yb[:], in_=ya[:], func=AF.Copy), r=[tya], w=[tyb])
                                fw.dma(sp, Y[ta, g * 256:(g + 1) * 256], yb[:], r=[tyb], w=[tY])
                            th.append(f_out)
                        return th

                    for g in range(_NG):
                        for h in range(4):
                            fw.dma(sp, qstack[0:64, h, :], QT[(4 * g + h) * 64:(4 * g + h + 1) * 64, :], r=[tQT], w=[t_q])
                        fw.dma(sp, ks_sel[0:64, :], KsT[g * 64:(g + 1) * 64, :], r=[tKsT], w=[t_ks])
                        fw.dma(sp, kw[0:64, :], KwT[g * 64:(g + 1) * 64, :], r=[tKwT], w=[t_kw])
                        fw.dma(pool, vs_aug[:, :, 0:64], Vs[:, g * 64:(g + 1) * 64].rearrange("(a p) d -> p a d", p=128), r=[tVs], w=[t_vs])
                        fw.dma(pool, vw_aug[:, :, 0:64], Vw[:, g * 64:(g + 1) * 64].rearrange("(a p) d -> p a d", p=128), r=[tVw], w=[t_vw])
                        fw.dma(sp, gt[:], G[:, g * 12:(g + 1) * 12].rearrange("(a p) c -> p a c", p=128), r=[tG], w=[t_gt])
                        for h in range(4):
                            sl = SLOPES[4 * g + h]
                            for dl in range(33):
                                src = d0[:, 0, :] if dl == 0 else (d0[:, 2, :] if dl == 32 else d0[:, 1, :])
                                off = 512.0 if dl == 32 else 128.0 * dl
                                fw.op(act, lambda e: e.activation(out=A[:, dl, h * 128:(h + 1) * 128], in_=src, func=AF.Exp,
                                                                  scale=-sl, bias=-sl * off), r=[t_cc], w=[t_A])
                            fw.op(dve, lambda e: e.tensor_scalar(out=bcb[:, h, :], in0=distc[:], scalar1=-sl, scalar2=None, op0=ALU.mult),
                                  r=[t_cc], w=[t_bc])
                        for a0 in range(0, _NA, 4):
                            chains = [cmp_thunks(g, a0 + ci, ci) for ci in range(4) if a0 + ci < _NA]
                            while chains:
                                for c_ in chains:
                                    c_.pop(0)()
                                chains = [c_ for c_ in chains if c_]
                        LA = 4
                        SB = (0, 1, 2, 6, 5)
                        pend = []
                        for a in range(_NA):
                            ta = slice(a * 128, (a + 1) * 128)
                            steps = [(0, b) for b in range(a + 1)] + [(1, b) for b in range(max(0, a - 4), a + 1)]
                            n = len(steps)
                            per = -(-len(pend) // max(1, n - 1))
                            slots = {}
                            newp = []
                            for i in range(n + LA):
                                if i < n:
                                    kind, b = steps[i]
                                    sbk = SB[kctr[0] % 5]
                                    k3 = kctr[0] % NPT2
                                    kctr[0] += 1
                                    slots[i] = k3
                                    krows = 128
                                    kT, tk = (ks_sel, t_ks) if kind == 0 else (kw, t_kw)
                                    dl = a - b
                                    ai = dl if (kind == 0 or dl < 4) else 32
                                    fw.op(pe, lambda e: e.matmul(out=ps[sbk][:], lhsT=kT[0:krows, b * 128:(b + 1) * 128],
                                                                 rhs=qstack[0:krows, :, ta], start=True, stop=True),
                                          r=[tk, t_q, t_qm[a]], w=[pst[sbk]])
                                    fw.op(act, lambda e: e.activation(out=pt[k3][:], in_=ps[sbk][:], func=AF.Exp), r=[pst[sbk]], w=[t_pt[k3]])
                                    fw.op(dve, lambda e: e.tensor_tensor(out=pa[k3][:], in0=pt[k3][:], in1=A[:, ai, :], op=ALU.mult),
                                          r=[t_pt[k3], t_A], w=[t_pa[k3]])
                                    if i >= 1:
                                        for _ in range(per):
                                            if pend:
                                                pend.pop(0)()
                                if i >= LA:
                                    j = i - LA
                                    kind, b = steps[j]
                                    k3 = slots[j]
                                    vaug, tv = (vs_aug, t_vs) if kind == 0 else (vw_aug, t_vw)
                                    first = (j == 0) or (steps[j - 1][0] != kind)
                                    last = (j == n - 1) or (steps[j + 1][0] != kind)
                                    bank = obank(a, kind)
                                    fw.op(pe, lambda e: e.matmul(out=ps[bank][0:65, :], lhsT=vaug[:, b, :], rhs=pa[k3][:],
                                                                 start=first, stop=last), r=[tv, t_pa[k3]], w=[pst[bank]])
                                    if last:
                                        newp += finish_thunks(g, a, kind)
                            while pend:
                                pend.pop(0)()
                            pend = newp
                        while pend:
                            pend.pop(0)()

        fw.barrier()
        tXB = [T() for _ in range(16)]

        def phase_ffn(layer, outproj, final):
            with contextlib.ExitStack() as st:
                wi = fw.sbuf(st, "wi", [128, 8, 2 * D_FF], BF16)
                wo2 = fw.sbuf(st, "wo2", [128, 22, D], BF16)
                t_wi, t_wo2 = T(), T()
                gf, tgf = load_gain(st, norm_ffn[layer], "gf")
                stg = [fw.sbuf(st, "stgF", [128, 704], F32) for _ in range(4)]
                tstg = [T() for _ in range(4)]
                if outproj:
                    wo = fw.sbuf(st, "wo", [128, 8, D], BF16)
                    t_wo = T()
                    for k in range(8):
                        load_cast(st, wo[:, k, :], t_wo, nsa_w_out[k * 128:(k + 1) * 128, :], 128, D, stg=stg, tstg=tstg)
                for k in range(8):
                    load_cast(st, wi[:, k, :], t_wi, ffn_w_in[layer, k * 128:(k + 1) * 128, :], 128, 2 * D_FF,
                              gain=gf[:, k:k + 1], tgain=tgf, stg=stg, tstg=tstg)
                for f in range(22):
                    load_cast(st, wo2[:, f, :], t_wo2, ffn_w_out[layer, f * 128:(f + 1) * 128, :], 128, D, stg=stg, tstg=tstg)
                if final:
                    gfin = fw.sbuf(st, "gfin", [128, D], F32)
                    t_gfin = T()
                    fw.dma(sp, gfin[:], norm_final.partition_broadcast(128), w=[t_gfin])
                xb = [fw.sbuf(st, "xb", [128, 2, D], F32) for _ in range(2)]
                t_xb = [T(), T()]
                if outproj:
                    ybt = [fw.sbuf(st, "ybt", [128, 2, D], BF16)] * 2
                    t_ybt = [T()] * 2
                    yT = fw.sbuf(st, "yT", [128, 8, 256], BF16)
                    t_yT = T()
                hn = [fw.sbuf(st, "hnF", [128, D], BF16) for _ in range(2)]
                thn = [T(), T()]
                junk = fw.sbuf(st, "junkF", [128, D], BF16)
                tjunk = T()
                sm = [[fw.sbuf(st, "smF", [128, 1], F32) for _ in range(3)] for _ in range(2)]
                tsm = [T(), T()]
                hnT = fw.sbuf(st, "hnTF", [128, 8, 256], BF16)
                thnT = T()
                actT = fw.sbuf(st, "actT", [128, 22, 256], BF16)
                t_actT = T()
                sg = [fw.sbuf(st, "sg", [128, 256], F32) for _ in range(2)]
                t_sg = [T(), T()]
                if final:
                    ob = [fw.sbuf(st, "ob", [128, D], F32) for _ in range(2)]
                    t_ob = [T(), T()]
                tout = T()
                for blk in range(16):
                    s2 = blk % 2
                    rows = slice(blk * 256, (blk + 1) * 256)
                    X, tX = xb[s2], t_xb[s2]
                    if outproj:
                        fw.dma(sp, X[:], x_in[rows, :].rearrange("(u p) d -> p u d", p=128), w=[tX])
                        fw.dma(pool, ybt[s2][:], Y[rows, :].rearrange("(u p) d -> p u d", p=128), r=[tY], w=[t_ybt[s2]])
                        for u in range(2):
                            transpose_to(ybt[s2][:, u, :], t_ybt[s2], yT, t_yT, 8, ps[6 + u], pst[6 + u], u * 128)
                        for u in range(2):
                            for nh in range(2):
                                pb = 4 + nh
                                for k in range(8):
                                    fw.op(pe, lambda e: e.matmul(out=ps[pb][:], lhsT=yT[:, k, u * 128:(u + 1) * 128],
                                                                 rhs=wo[:, k, nh * 512:(nh + 1) * 512], start=(k == 0), stop=(k == 7)),
                                          r=[t_yT, t_wo], w=[pst[pb]])
                                fw.op(dve, lambda e: e.tensor_tensor(out=X[:, u, nh * 512:(nh + 1) * 512], in0=X[:, u, nh * 512:(nh + 1) * 512],
                                                                     in1=ps[pb][:], op=ALU.add), r=[pst[pb], tX], w=[tX])
                    else:
                        fw.dma(sp, X[:], X1[rows, :].rearrange("(u p) d -> p u d", p=128), r=[tXB[blk]], w=[tX])
                    for u in range(2):
                        rmsnorm_bf(X[:, u, :], tX, hn[u][:], thn[u], junk[:], tjunk, [z[:] for z in sm[u]], tsm[u])
                        transpose_to(hn[u], thn[u], hnT, thnT, 8, ps[6 + u], pst[6 + u], u * 128)
                    for f in range(22):
                        pb = f % 4
                        for half in range(2):
                            c0 = half * D_FF + f * 128
                            for k in range(8):
                                fw.op(pe, lambda e: e.matmul(out=ps[pb][:, half * 256:(half + 1) * 256], lhsT=wi[:, k, c0:c0 + 128],
                                                             rhs=hnT[:, k, :], start=(k == 0), stop=(k == 7)), r=[t_wi, thnT], w=[pst[pb]])
                        fw.op(act, lambda e: e.activation(out=sg[f % 2][:], in_=ps[pb][:, 0:256], func=AF.Silu), r=[pst[pb]], w=[t_sg[f % 2]])
                        fw.op(dve, lambda e: e.tensor_tensor(out=actT[:, f, :], in0=sg[f % 2][:], in1=ps[pb][:, 256:512], op=ALU.mult),
                              r=[pst[pb], t_sg[f % 2]], w=[t_actT])
                    for u in range(2):
                        for nh in range(2):
                            pb = 4 + nh
                            for f in range(22):
                                fw.op(pe, lambda e: e.matmul(out=ps[pb][:], lhsT=actT[:, f, u * 128:(u + 1) * 128],
                                                             rhs=wo2[:, f, nh * 512:(nh + 1) * 512], start=(f == 0), stop=(f == 21)),
                                      r=[t_actT, t_wo2], w=[pst[pb]])
                            fw.op(dve, lambda e: e.tensor_tensor(out=X[:, u, nh * 512:(nh + 1) * 512], in0=X[:, u, nh * 512:(nh + 1) * 512],
                                                                 in1=ps[pb][:], op=ALU.add), r=[pst[pb], tX], w=[tX])
                    if final:
                        for u in range(2):
                            ss, sd, rs = [z[:] for z in sm[u]]
                            fw.op(act, lambda e: e.activation(out=junk[:], in_=X[:, u, :], func=AF.Square, accum_out=ss), r=[tX], w=[tjunk, tsm[u]])
                            fw.op(act, lambda e: e.activation(out=sd, in_=ss, func=AF.Sqrt, bias=epsc[:], scale=1.0 / D), r=[tsm[u], t_eps], w=[tsm[u]])
                            fw.op(dve, lambda e: e.reciprocal(out=rs, in_=sd), r=[tsm[u]], w=[tsm[u]])
                            fw.op(dve, lambda e: e.scalar_tensor_tensor(out=ob[u][:], in0=X[:, u, :], scalar=rs, in1=gfin[:], op0=ALU.mult, op1=ALU.mult),
                                  r=[tX, tsm[u], t_gfin], w=[t_ob[u]])
                            fw.dma(sp, out_ap[blk * 256 + u * 128:blk * 256 + (u + 1) * 128, :], ob[u][:], r=[t_ob[u]], w=[tout])
                    else:
                        fw.dma(sp, X1[rows, :].rearrange("(u p) d -> p u d", p=128), X[:], r=[tX], w=[tXB[blk]])
                fw.barrier()

        if stage >= 4:
            phase_ffn(0, True, False)

        if stage >= 5:
            with contextlib.ExitStack() as st:
                NB = 512
                NU = NB // 128
                wl = fw.sbuf(st, "wl", [128, 8, 2 * D_RNN], BF16)
                t_wl = T()
                gl, tgl = load_gain(st, norm_mix[1], "gl")
                wa = fw.sbuf(st, "wa", [88, 16, 176], BF16)
                wx = fw.sbuf(st, "wx", [88, 16, 176], BF16)
                wlo = fw.sbuf(st, "wlo", [88, 16, D], BF16)
                t_wa, t_wx, t_wlo = T(), T(), T()
                cw = fw.sbuf(st, "cw", [88, 16, 4], F32)
                cb = fw.sbuf(st, "cb", [88, 16], F32)
                nba = fw.sbuf(st, "nba", [88, 16], F32)
                nbx = fw.sbuf(st, "nbx", [88, 16], F32)
                nsp = fw.sbuf(st, "nsp", [88, 16], F32)
                t_small = T()
                for j in range(4):
                    fw.dma(sp, cw[:, :, j], lru_conv_w[j].rearrange("(c p) -> p c", p=88), w=[t_small], allow_slow_non_contiguous=True)
                fw.dma(sp, cb[:], lru_conv_b.rearrange("(c p) -> p c", p=88), w=[t_small], allow_slow_non_contiguous=True)
                fw.dma(sp, nba[:], lru_b_a.rearrange("(c p) -> p c", p=88), w=[t_small], allow_slow_non_contiguous=True)
                fw.dma(sp, nbx[:], lru_b_x.rearrange("(c p) -> p c", p=88), w=[t_small], allow_slow_non_contiguous=True)
                fw.dma(sp, nsp[:], lru_lambda.rearrange("(c p) -> p c", p=88), w=[t_small], allow_slow_non_contiguous=True)
                fw.op(dve, lambda e: e.tensor_scalar(out=nba[:], in0=nba[:], scalar1=-1.0, scalar2=None, op0=ALU.mult), r=[t_small], w=[t_small])
                fw.op(dve, lambda e: e.tensor_scalar(out=nbx[:], in0=nbx[:], scalar1=-1.0, scalar2=None, op0=ALU.mult), r=[t_small], w=[t_small])
                fw.op(act, lambda e: e.activation(out=nsp[:], in_=nsp[:], func=AF.Exp, scale=-1.0), r=[t_small], w=[t_small])
                fw.op(act, lambda e: e.activation(out=nsp[:], in_=nsp[:], func=AF.Ln, bias=1.0), r=[t_small], w=[t_small])
                fw.op(dve, lambda e: e.tensor_scalar(out=nsp[:], in0=nsp[:], scalar1=-8.0, scalar2=None, op0=ALU.mult), r=[t_small], w=[t_small])
                halo = fw.sbuf(st, "halo", [88, 16, 3], F32)
                hlast = fw.sbuf(st, "hlast", [88, 16], F32)
                t_halo, t_hl = T(), T()
                fw.op(pool, lambda e: e.memset(halo[:], 0.0), w=[t_halo])
                fw.op(pool, lambda e: e.memset(hlast[:], 0.0), w=[t_hl])
                X2 = [fw.sbuf(st, "xbL", [128, NU, D], F32) for _ in range(2)]
                tX2 = [T(), T()]
                cur = [0]
                hn = [fw.sbuf(st, "hnL", [128, D], BF16) for _ in range(2)]
                thn = [T(), T()]
                junk = fw.sbuf(st, "junkL", [128, D], BF16)
                tjunk = T()
                sm = [[fw.sbuf(st, "smL", [128, 1], F32) for _ in range(3)] for _ in range(2)]
                tsm = [T(), T()]
                hnT_1 = fw.sbuf(st, "hnTL", [128, 8, NB], BF16)
                thnT_1 = T()
                hnT2 = [hnT_1, hnT_1]
                thnT2 = [thnT_1, thnT_1]
                g16 = [fw.sbuf(st, "gate16", [88, 16, NB], BF16) for _ in range(2)]
                t_g16_2 = [T(), T()]

                with contextlib.ExitStack() as st_w:
                    stg = [fw.sbuf(st_w, "stgL", [128, 704], F32) for _ in range(4)]
                    tstg = [T() for _ in range(4)]
                    for k in range(8):
                        load_cast(st_w, wl[:, k, :], t_wl, lru_w_in[k * 128:(k + 1) * 128, :], 128, 2 * D_RNN,
                                  gain=gl[:, k:k + 1], tgain=tgl, stg=stg, tstg=tstg)
                    for (dst, tdst, src) in ((wa, t_wa, lru_w_a), (wx, t_wx, lru_w_x)):
                        for n in range(8):
                            for ih in range(2):
                                load_cast(st_w, dst[:, 2 * n + ih, :], tdst, src[n, ih * 88:(ih + 1) * 88, :], 88, 176, stg=stg, tstg=tstg)
                    for c in range(16):
                        load_cast(st_w, wlo[:, c, :], t_wlo, lru_w_out[c * 88:(c + 1) * 88, :], 88, D, stg=stg, tstg=tstg)
                    fw.barrier()

                def mk(name, dt=F32, w=NB, nbuf=1):
                    return [fw.sbuf(st, name, [88, 2, w], dt) for _ in range(nbuf)], [[T(), T()] for _ in range(nbuf)]
                recb, t_recb = mk("recb", F32, NB + 3, 2)
                xr, t_xr = mk("xr", F32, NB, 2)
                xrb, t_xrb = mk("xrb", BF16, NB, 2)
                rr_, t_rr = mk("rr")
                ii_, t_ii = mk("ii")
                aa, t_aa = mk("aa")
                uu, t_uu = mk("uu")
                hh, t_hh = mk("hh")
                rr_, ii_, aa, uu, hh = rr_[0], ii_[0], aa[0], uu[0], hh[0]
                t_rr, t_ii, t_aa, t_uu, t_hh = t_rr[0], t_ii[0], t_aa[0], t_uu[0], t_hh[0]

                def conv_chain(n, half):
                    q = n % 2
                    c = 2 * n + half
                    pb = half
                    RB, XR, XB = recb[q], xr[q], xrb[q]
                    tRB, tXR, tXB_ = t_recb[q][half], t_xr[q][half], t_xrb[q][half]
                    th = []

                    def f0():
                        for k in range(8):
                            fw.op(pe, lambda e: e.matmul(out=ps[pb][0:88, :], lhsT=wl[:, k, D_RNN + c * 88:D_RNN + (c + 1) * 88], rhs=hnT2[cur[0] % 2][:, k, :],
                                                         start=(k == 0), stop=(k == 7)), r=[t_wl, thnT2[cur[0] % 2]], w=[pst[pb]])
                        fw.op(act, lambda e: e.activation(out=RB[:, half, 0:3], in_=halo[:, c, :], func=AF.Copy), r=[t_halo], w=[tRB])
                    th.append(f0)

                    def f1():
                        fw.op(act, lambda e: e.activation(out=RB[:, half, 3:3 + NB], in_=ps[pb][0:88, :], func=AF.Copy), r=[pst[pb]], w=[tRB])
                        fw.op(act, lambda e: e.activation(out=halo[:, c, :], in_=RB[:, half, NB:NB + 3], func=AF.Copy), r=[tRB], w=[t_halo])
                    th.append(f1)
                    th.append(lambda: fw.op(dve, lambda e: e.tensor_scalar(out=XR[:, half, :], in0=RB[:, half, 0:NB], scalar1=cw[:, c, 0:1],
                                                                           scalar2=cb[:, c:c + 1], op0=ALU.mult, op1=ALU.add),
                                            r=[tRB, t_small], w=[tXR]))
                    for j in range(1, 4):
                        th.append(lambda j=j: fw.op(dve, lambda e: e.scalar_tensor_tensor(out=XR[:, half, :], in0=RB[:, half, j:j + NB], scalar=cw[:, c, j:j + 1],
                                                                                           in1=XR[:, half, :], op0=ALU.mult, op1=ALU.add),
                                                    r=[tRB, t_small, tXR], w=[tXR]))
                    th.append(lambda: fw.op(act, lambda e: e.activation(out=XB[:, half, :], in_=XR[:, half, :], func=AF.Copy), r=[tXR], w=[tXB_]))
                    return th

                def gate_chain(n, oh):
                    q = n % 2
                    XR, XB = xr[q], xrb[q]
                    c = 2 * n + oh
                    pr, pi = 2 + 2 * oh, 3 + 2 * oh
                    R_, I_, A_, U_, H_ = rr_[:, oh, :], ii_[:, oh, :], aa[:, oh, :], uu[:, oh, :], hh[:, oh, :]
                    tr, ti, ta_, tu, th_ = [t_rr[oh]], [t_ii[oh]], [t_aa[oh]], [t_uu[oh]], [t_hh[oh]]
                    th = []

                    def f0():
                        for (wgt, twg, pbG) in ((wa, t_wa, pr), (wx, t_wx, pi)):
                            for ih in range(2):
                                fw.op(pe, lambda e: e.matmul(out=ps[pbG][0:88, :], lhsT=wgt[:, 2 * n + ih, oh * 88:(oh + 1) * 88],
                                                             rhs=XB[:, ih, :], start=(ih == 0), stop=(ih == 1)),
                                      r=[twg, t_xrb[q][0], t_xrb[q][1]], w=[pst[pbG]])
                    th.append(f0)
                    th.append(lambda: fw.op(act, lambda e: e.activation(out=R_, in_=ps[pr][0:88, :], func=AF.Exp, bias=nba[:, c:c + 1], scale=-1.0),
                                            r=[pst[pr], t_small], w=tr))
                    th.append(lambda: fw.op(act, lambda e: e.activation(out=I_, in_=ps[pi][0:88, :], func=AF.Exp, bias=nbx[:, c:c + 1], scale=-1.0),
                                            r=[pst[pi], t_small], w=ti))
                    th.append(lambda: fw.op(act, lambda e: e.activation(out=R_, in_=R_, func=AF.Ln, bias=1.0), r=tr, w=tr))
                    th.append(lambda: fw.op(act, lambda e: e.activation(out=I_, in_=I_, func=AF.Ln, bias=1.0), r=ti, w=ti))
                    th.append(lambda: fw.op(act, lambda e: e.activation(out=R_, in_=R_, func=AF.Exp, scale=-1.0), r=tr, w=tr))
                    th.append(lambda: fw.op(act, lambda e: e.activation(out=I_, in_=I_, func=AF.Exp, scale=-1.0), r=ti, w=ti))
                    th.append(lambda: fw.op(act, lambda e: e.activation(out=A_, in_=R_, func=AF.Exp, scale=nsp[:, c:c + 1]), r=tr + [t_small], w=ta_))
                    th.append(lambda: fw.op(dve, lambda e: e.tensor_tensor(out=I_, in0=I_, in1=XR[:, oh, :], op=ALU.mult), r=ti + [t_xr[q][oh]], w=ti))
                    th.append(lambda: fw.op(dve, lambda e: e.scalar_tensor_tensor(out=U_, in0=A_, scalar=-1.0, in1=A_, op0=ALU.mult, op1=ALU.mult),
                                            r=ta_, w=tu))
                    th.append(lambda: fw.op(dve, lambda e: e.tensor_scalar(out=U_, in0=U_, scalar1=1.0, scalar2=1e-30, op0=ALU.add, op1=ALU.max),
                                            r=tu, w=tu))
                    th.append(lambda: fw.op(act, lambda e: e.activation(out=U_, in_=U_, func=AF.Ln), r=tu, w=tu))
                    th.append(lambda: fw.op(act, lambda e: e.activation(out=U_, in_=U_, func=AF.Exp, scale=0.5), r=tu, w=tu))
                    th.append(lambda: fw.op(dve, lambda e: e.tensor_tensor(out=U_, in0=U_, in1=I_, op=ALU.mult), r=tu + ti, w=tu))

                    def f_scan():
                        fw.op(dve, lambda e: e.tensor_tensor_scan(out=H_, data0=A_, data1=U_, initial=hlast[:, c:c + 1], op0=ALU.mult, op1=ALU.add),
                              r=ta_ + tu + [t_hl], w=th_)
                        fw.op(dve, lambda e: e.tensor_copy(out=hlast[:, c:c + 1], in_=hh[:, oh, NB - 1:NB]), r=th_, w=[t_hl])
                    th.append(f_scan)
                    th.append(lambda: fw.op(dve, lambda e: e.tensor_tensor(out=g16[cur[0] % 2][:, c, :], in0=H_, in1=g16[cur[0] % 2][:, c, :], op=ALU.mult),
                                            r=th_, w=[t_g16_2[cur[0] % 2]]))
                    return th

                def zip_emit(chains):
                    chains = [c for c in chains if c]
                    while chains:
                        for c in chains:
                            c.pop(0)()
                        chains = [c for c in chains if c]

                def blk_rows(b):
                    return slice(b * NB, (b + 1) * NB), [tXB[b * (NB // 256) + q] for q in range(NB // 256)]

                def prep_load(b):
                    rows, tblks = blk_rows(b)
                    fw.dma(sp, X2[b % 2][:], X1[rows, :].rearrange("(u p) d -> p u d", p=128), r=tblks, w=[tX2[b % 2]])

                def prep_norm(b, u):
                    rmsnorm_bf(X2[b % 2][:, u, :], tX2[b % 2], hn[u % 2][:], thn[u % 2], junk[:], tjunk, [z[:] for z in sm[u % 2]], tsm[u % 2])
                    transpose_to(hn[u % 2], thn[u % 2], hnT2[b % 2], thnT2[b % 2], 8, ps[6 + u % 2], pst[6 + u % 2], u * 128)

                def gelu_stage(b):
                    for c in range(16):
                        pb = c % 2
                        for k in range(8):
                            fw.op(pe, lambda e: e.matmul(out=ps[pb][0:88, :], lhsT=wl[:, k, c * 88:(c + 1) * 88], rhs=hnT2[b % 2][:, k, :],
                                                         start=(k == 0), stop=(k == 7)), r=[t_wl, thnT2[b % 2]], w=[pst[pb]])
                        fw.op(act, lambda e: e.activation(out=g16[b % 2][:, c, :], in_=ps[pb][0:88, :], func=AF.Gelu_apprx_tanh),
                              r=[pst[pb]], w=[t_g16_2[b % 2]])

                def outproj_unit(b, u, nh):
                    pb = 6 + nh
                    Xb, tXb = X2[b % 2], tX2[b % 2]
                    for c in range(16):
                        fw.op(pe, lambda e: e.matmul(out=ps[pb][:], lhsT=g16[b % 2][:, c, u * 128:(u + 1) * 128],
                                                     rhs=wlo[:, c, nh * 512:(nh + 1) * 512], start=(c == 0), stop=(c == 15)),
                              r=[t_g16_2[b % 2], t_wlo], w=[pst[pb]])
                    fw.op(dve, lambda e: e.tensor_tensor(out=Xb[:, u, nh * 512:(nh + 1) * 512], in0=Xb[:, u, nh * 512:(nh + 1) * 512],
                                                         in1=ps[pb][:], op=ALU.add), r=[pst[pb], tXb], w=[tXb])

                def store(b):
                    rows, tblks = blk_rows(b)
                    fw.dma(sp, X1[rows, :].rearrange("(u p) d -> p u d", p=128), X2[b % 2][:], r=[tX2[b % 2]], w=tblks)

                NBLK = S // NB
                prep_load(0)
                for u in range(NU):
                    prep_norm(0, u)
                gelu_stage(0)
                for blk in range(NBLK):
                    cur[0] = blk
                    zip_emit([conv_chain(0, 0), conv_chain(0, 1)])
                    for n in range(8):
                        chains = [gate_chain(n, 0), gate_chain(n, 1)]
                        if n + 1 < 8:
                            chains = [conv_chain(n + 1, 0), conv_chain(n + 1, 1)] + chains
                        zip_emit(chains)
                        if blk >= 1 and n < NU:
                            outproj_unit(blk - 1, n, 0)
                            outproj_unit(blk - 1, n, 1)
                            if n == NU - 1:
                                store(blk - 1)
                        if blk + 1 < NBLK:
                            if n == NU - 1:
                                prep_load(blk + 1)
                            if n in (6, 7):
                                prep_norm(blk + 1, 2 * (n - 6))
                                prep_norm(blk + 1, 2 * (n - 6) + 1)
                    if blk + 1 < NBLK:
                        gelu_stage(blk + 1)
                for u in range(NU):
                    outproj_unit(NBLK - 1, u, 0)
                    outproj_unit(NBLK - 1, u, 1)
                store(NBLK - 1)
                fw.barrier()

        if stage >= 6:
            phase_ffn(1, False, True)

        fw.finish()
    return nc


_CONSTS = None


def make_in_map(inp, b):
    global _CONSTS
    if _CONSTS is None:
        _CONSTS = host_consts()
    f = lambda a: np.ascontiguousarray(np.asarray(a, dtype=np.float32))
    m = {
        "x": f(inp["x"][b]),
        "norm_mix": f(inp["norm_mix"]), "norm_ffn": f(inp["norm_ffn"]), "norm_final": f(inp["norm_final"]),
        "nsa_w_in": f(inp["nsa_w_in"][0]), "nsa_b_gate": f(inp["nsa_b_gate"][0]),
        "nsa_cmp_pos": f(inp["nsa_cmp_pos"][0]), "nsa_cmp_w1": f(inp["nsa_cmp_w1"][0]),
        "nsa_cmp_b1": f(inp["nsa_cmp_b1"][0]), "nsa_cmp_w2": f(inp["nsa_cmp_w2"][0]),
        "nsa_cmp_b2": f(inp["nsa_cmp_b2"][0]), "nsa_w_out": f(inp["nsa_w_out"][0]),
        "lru_w_in": f(inp["lru_w_in"][0]), "lru_conv_w": f(inp["lru_conv_w"][0]),
        "lru_conv_b": f(inp["lru_conv_b"][0]), "lru_w_a": f(inp["lru_w_a"][0]), "lru_b_a": f(inp["lru_b_a"][0]),
        "lru_w_x": f(inp["lru_w_x"][0]), "lru_b_x": f(inp["lru_b_x"][0]), "lru_lambda": f(inp["lru_lambda"][0]),
        "lru_w_out": f(inp["lru_w_out"][0]), "ffn_w_in": f(inp["ffn_w_in"]), "ffn_w_out": f(inp["ffn_w_out"]),
    }
    m.update(_CONSTS)
    return m


def kernel(**inputs):
    nc = build(debug=False)
    n = 4
    maps = [make_in_map(inputs, b) for b in range(n)]
    res = run_bass_kernel_spmd(nc, maps, core_ids=list(range(n)))
    return np.stack([np.asarray(res.results[b]["out"], dtype=np.float32) for b in range(n)], axis=0)
```

```python
import contextlib
import numpy as np
import ml_dtypes
import concourse.bass as bass
import concourse.mybir as mybir
from concourse.bass_utils import run_bass_kernel_spmd

F32 = mybir.dt.float32
BF16 = mybir.dt.bfloat16
AF = mybir.ActivationFunctionType
ALU = mybir.AluOpType
AX = mybir.AxisListType

S = 4096
D = 1024
NT = S // 128
NSA_IN = 2608
D_RNN = 1408
D_FF = 2816
EPS = 1e-6
SLOPES = [2.0 ** (-8.0 * (h + 1) / 16) for h in range(16)]
BIGD = 1.0e6


class Dom:
    def __init__(self, fw, name, unit):
        self.sem = fw.es.enter_context(fw.nc.semaphore(name))
        self.unit = unit
        self.count = 0


class T:
    __slots__ = ("w", "r", "dd")

    def __init__(self):
        self.w = None
        self.r = {}
        self.dd = None


class Eng:
    def __init__(self, fw, name, eng, is_pe=False, has_dom=True):
        self.name = name
        self.eng = eng
        self.is_pe = is_pe
        self.dom = Dom(fw, "c_" + name, 1) if has_dom else None
        self.known = {}


class FW:
    def __init__(self, nc):
        self.nc = nc
        self.es = contextlib.ExitStack()
        self.pe = Eng(self, "pe", nc.tensor, is_pe=True)
        self.act = Eng(self, "act", nc.scalar)
        self.dve = Eng(self, "dve", nc.vector)
        self.pool = Eng(self, "pool", nc.gpsimd)
        self.sp = Eng(self, "sp", nc.sync, has_dom=False)
        self.dma_doms = []
        self.free_doms = []
        self.uid = 0

    def sbuf(self, st, name, shape, dt):
        self.uid += 1
        return st.enter_context(self.nc.sbuf_tensor("%s_%d" % (name, self.uid), list(shape), dt))

    def _waits(self, E, r, w):
        deps = {}
        for t in r:
            if t.w is not None and deps.get(t.w[0], 0) < t.w[1]:
                deps[t.w[0]] = t.w[1]
        for t in w:
            if t.w is not None and deps.get(t.w[0], 0) < t.w[1]:
                deps[t.w[0]] = t.w[1]
            for d, s in t.r.items():
                if deps.get(d, 0) < s:
                    deps[d] = s
        for d, s in deps.items():
            if E.is_pe and d is E.dom:
                continue
            if E.known.get(d, 0) >= s:
                continue
            E.eng.wait_ge(d.sem, s * d.unit)
            E.known[d] = s

    def op(self, E, fn, r=(), w=()):
        self._waits(E, r, w)
        ins = fn(E.eng)
        d = E.dom
        d.count += 1
        ins.then_inc(d.sem, 1)
        for t in r:
            t.r[d] = d.count
        for t in w:
            t.w = (d, d.count)
            t.r = {}
        return ins

    def dma(self, E, out, in_, r=(), w=(), **kw):
        self._waits(E, r, w)
        t0 = w[0] if len(w) else r[0]
        if t0.dd is None:
            t0.dd = Dom(self, "d%d" % len(self.dma_doms), 16)
            self.dma_doms.append(t0.dd)
        d = t0.dd
        ins = E.eng.dma_start(out=out, in_=in_, **kw)
        d.count += 1
        ins.then_inc(d.sem, 16)
        for t in r:
            t.r[d] = d.count
        for t in w:
            t.w = (d, d.count)
            t.r = {}
        return ins

    def barrier(self):
        doms = [d for d in self.dma_doms if d.count] + [X.dom for X in (self.pe, self.act, self.dve, self.pool) if X.dom.count]
        for E in (self.pe, self.act, self.dve, self.pool, self.sp):
            for d in doms:
                if E.known.get(d, 0) < d.count:
                    E.eng.wait_ge(d.sem, d.count * d.unit)
                    E.known[d] = d.count

    def finish(self):
        E = self.sp
        for d in self.dma_doms:
            if d.count:
                E.eng.wait_ge(d.sem, d.count * d.unit)
        for X in (self.pe, self.act, self.dve, self.pool):
            if X.dom.count:
                E.eng.wait_ge(X.dom.sem, X.dom.count)


def host_consts():
    c = {}
    c["ident_bf"] = np.eye(128, dtype=np.float32).astype(ml_dtypes.bfloat16)
    c["ident_f"] = np.eye(128, dtype=np.float32)
    e = np.zeros((64, S), np.float32)
    for j in range(64):
        e[j, j * 64:(j + 1) * 64] = 256.0
    c["e256"] = e.astype(ml_dtypes.bfloat16)
    sr = np.arange(128)[:, None].astype(np.float32)
    tr = np.arange(128)[None, :].astype(np.float32)
    d0 = tr - sr
    c["d0"] = np.stack([np.where(d0 >= 0, d0, BIGD), d0, np.where(d0 < 0, d0, BIGD)]).astype(np.float32)
    i = np.arange(128)[:, None]
    m = np.arange(-248, 256)[None, :]
    dc = (i - 16 * m - 31).astype(np.float32)
    c["distc"] = np.where(dc >= 0, dc, BIGD).astype(np.float32)
    jp = np.arange(-62, 64)[None, :]
    ci = (np.arange(128)[:, None] // 64)
    fb = np.zeros((128, 126), np.float32)
    fb = np.where((jp == ci) | (jp == ci - 1), 100.0, fb)
    fb = np.where(jp > ci, -100.0, fb)
    c["fbias"] = fb.astype(np.float32)
    ov = np.zeros((256, 64), np.float32)
    for n in range(255):
        for j in range(64):
            if 16 * n < 64 * j + 64 and 16 * n + 32 > 64 * j:
                ov[n, j] = 1.0
    c["ovl"] = ov.reshape(2, 128, 64).astype(ml_dtypes.bfloat16)
    return c


def build(debug=False, stage=99):
    nc = bass.Bass("TRN2", target_bir_lowering=False)
    fw = FW(nc)
    pe, act, dve, pool, sp = fw.pe, fw.act, fw.dve, fw.pool, fw.sp

    def din(name, shape, dt=F32):
        return nc.dram_tensor(name, list(shape), dt, kind="ExternalInput").ap()

    def dscr(name, shape, dt):
        return nc.dram_tensor(name, list(shape), dt, kind="ExternalOutput" if debug else "Internal").ap()

    x_in = din("x", [S, D])
    norm_mix = din("norm_mix", [2, D])
    norm_ffn = din("norm_ffn", [2, D])
    norm_final = din("norm_final", [D])
    nsa_w_in = din("nsa_w_in", [D, NSA_IN])
    nsa_b_gate = din("nsa_b_gate", [48])
    nsa_cmp_pos = din("nsa_cmp_pos", [2, 32, 64])
    nsa_cmp_w1 = din("nsa_cmp_w1", [2, 2048, 256])
    nsa_cmp_b1 = din("nsa_cmp_b1", [2, 256])
    nsa_cmp_w2 = din("nsa_cmp_w2", [2, 256, 64])
    nsa_cmp_b2 = din("nsa_cmp_b2", [2, 64])
    nsa_w_out = din("nsa_w_out", [D, D])
    lru_w_in = din("lru_w_in", [D, 2 * D_RNN])
    lru_conv_w = din("lru_conv_w", [4, D_RNN])
    lru_conv_b = din("lru_conv_b", [D_RNN])
    lru_w_a = din("lru_w_a", [8, 176, 176])
    lru_b_a = din("lru_b_a", [D_RNN])
    lru_w_x = din("lru_w_x", [8, 176, 176])
    lru_b_x = din("lru_b_x", [D_RNN])
    lru_lambda = din("lru_lambda", [D_RNN])
    lru_w_out = din("lru_w_out", [D_RNN, D])
    ffn_w_in = din("ffn_w_in", [2, D, 2 * D_FF])
    ffn_w_out = din("ffn_w_out", [2, D_FF, D])
    c_ident_bf = din("ident_bf", [128, 128], BF16)
    c_ident_f = din("ident_f", [128, 128])
    c_e256 = din("e256", [64, S], BF16)
    c_d0 = din("d0", [3, 128, 128])
    c_distc = din("distc", [128, 504])
    c_fbias = din("fbias", [128, 126])
    c_ovl = din("ovl", [2, 128, 64], BF16)

    out_ap = nc.dram_tensor("out", [S, D], F32, kind="ExternalOutput").ap()

    QT = dscr("QT", [1024, S], BF16)
    KcT = dscr("KcT", [256, S], BF16)
    VcT = dscr("VcT", [256, S], BF16)
    KsT = dscr("KsT", [256, S], BF16)
    KwT = dscr("KwT", [256, S], BF16)
    Vs = dscr("Vs", [S, 256], BF16)
    Vw = dscr("Vw", [S, 256], BF16)
    G = dscr("G", [S, 48], F32)
    Y = dscr("Y", [S, D], BF16)
    X1 = dscr("X1", [S, D], F32)
    tQT, tKcT, tVcT, tKsT, tKwT, tVs, tVw, tG, tY, tX1 = [T() for _ in range(10)]

    with fw.es:
        gst = fw.es
        ps = [gst.enter_context(nc.psum_tensor("ps%d" % i, [128, 512], F32)) for i in range(8)]
        pst = [T() for _ in range(8)]
        ident_bf = fw.sbuf(gst, "identbf", [128, 128], BF16)
        ident_f = fw.sbuf(gst, "identf", [128, 128], F32)
        t_ident = T()
        fw.dma(sp, ident_bf[:], c_ident_bf, w=[t_ident])
        fw.dma(sp, ident_f[:], c_ident_f, w=[t_ident])
        epsc = fw.sbuf(gst, "epsc", [128, 1], F32)
        t_eps = T()
        fw.op(dve, lambda e: e.memset(epsc[:], EPS), w=[t_eps])

        rr = [0]

        def evac(out, in_, r, w, scale=None):
            rr[0] += 1
            if rr[0] % 2 == 0:
                if scale is None:
                    fw.op(act, lambda e: e.activation(out=out, in_=in_, func=AF.Copy), r=r, w=w)
                else:
                    fw.op(act, lambda e: e.activation(out=out, in_=in_, func=AF.Copy, scale=scale), r=r, w=w)
            else:
                if scale is None:
                    fw.op(dve, lambda e: e.tensor_copy(out=out, in_=in_), r=r, w=w)
                else:
                    fw.op(dve, lambda e: e.tensor_scalar(out=out, in0=in_, scalar1=scale, scalar2=None, op0=ALU.mult), r=r, w=w)

        def load_gain(st, g_ap, name):
            gt = fw.sbuf(st, name, [128, 8], F32)
            tg = T()
            fw.dma(sp, gt[:], g_ap.rearrange("(k p) -> p k", p=128), w=[tg], allow_slow_non_contiguous=True)
            return gt, tg

        cast_rr = [0]

        def load_cast(st, dst, tdst, src, nrows, ncols, gain=None, tgain=None, stg=None, tstg=None):
            CH = stg[0].shape[1]
            for c0 in range(0, ncols, CH):
                cw = min(CH, ncols - c0)
                k = cast_rr[0] % len(stg)
                cast_rr[0] += 1
                s_, ts_ = stg[k], tstg[k]
                fw.dma((sp, pool, act, sp)[k % 4], s_[0:nrows, 0:cw], src[:, c0:c0 + cw], w=[ts_])
                if k % 2 == 0:
                    if gain is None:
                        fw.op(dve, lambda e: e.tensor_copy(out=dst[:, c0:c0 + cw], in_=s_[0:nrows, 0:cw]), r=[ts_], w=[tdst])
                    else:
                        fw.op(dve, lambda e: e.tensor_scalar(out=dst[:, c0:c0 + cw], in0=s_[0:nrows, 0:cw], scalar1=gain,
                                                             scalar2=None, op0=ALU.mult), r=[ts_, tgain], w=[tdst])
                else:
                    if gain is None:
                        fw.op(act, lambda e: e.activation(out=dst[:, c0:c0 + cw], in_=s_[0:nrows, 0:cw], func=AF.Copy), r=[ts_], w=[tdst])
                    else:
                        fw.op(act, lambda e: e.activation(out=dst[:, c0:c0 + cw], in_=s_[0:nrows, 0:cw], func=AF.Copy, scale=gain),
                              r=[ts_, tgain], w=[tdst])

        def rmsnorm_bf(xt, tx, hn, thn, junk, tjunk, st_small, tsm):
            ss, sd, rs = st_small
            fw.op(act, lambda e: e.activation(out=junk, in_=xt, func=AF.Square, accum_out=ss), r=[tx], w=[tjunk, tsm])
            fw.op(act, lambda e: e.activation(out=sd, in_=ss, func=AF.Sqrt, bias=epsc[:], scale=1.0 / D), r=[tsm, t_eps], w=[tsm])
            fw.op(dve, lambda e: e.reciprocal(out=rs, in_=sd), r=[tsm], w=[tsm])
            fw.op(dve, lambda e: e.tensor_scalar(out=hn, in0=xt, scalar1=rs, scalar2=None, op0=ALU.mult), r=[tx, tsm], w=[thn])

        def transpose_to(hn, thn, dstT, tdst, nchunk, pbank, tpbank, col0, rows=128):
            pv = pbank[:].bitcast(BF16)
            for k0 in range(0, nchunk, 8):
                kn = min(8, nchunk - k0)
                for k in range(kn):
                    fw.op(pe, lambda e: e.transpose(out=pv[:, k * 128:(k + 1) * 128], in_=hn[:, (k0 + k) * 128:(k0 + k + 1) * 128],
                                                    identity=ident_bf[:]), r=[thn, t_ident], w=[tpbank])
                evac(dstT[:, k0:k0 + kn, col0:col0 + 128], pv[:, 0:kn * 128].rearrange("p (k t) -> p k t", k=kn), r=[tpbank], w=[tdst])

        if stage >= 1:
            with contextlib.ExitStack() as st:
                w_in = fw.sbuf(st, "w_in", [128, 8, NSA_IN], BF16)
                t_w = T()
                g0, tg0 = load_gain(st, norm_mix[0], "g0")
                stg = [fw.sbuf(st, "stg", [128, 2608], F32) for _ in range(2)]
                tstg = [T(), T()]
                for k in range(8):
                    load_cast(st, w_in[:, k, :], t_w, nsa_w_in[k * 128:(k + 1) * 128, :], 128, NSA_IN,
                              gain=g0[:, k:k + 1], tgain=tg0, stg=stg, tstg=tstg)
                bg = fw.sbuf(st, "bg", [128, 48], F32)
                t_bg = T()
                fw.dma(sp, bg[:], nsa_b_gate.partition_broadcast(128), w=[t_bg])
                xt = [fw.sbuf(st, "xt", [128, D], F32) for _ in range(2)]
                txt = [T(), T()]
                hn = [fw.sbuf(st, "hn", [128, D], BF16) for _ in range(2)]
                thn = [T(), T()]
                junk = fw.sbuf(st, "junk", [128, D], BF16)
                tjunk = T()
                sm = [[fw.sbuf(st, "sm", [128, 1], F32) for _ in range(3)] for _ in range(2)]
                tsm = [T(), T()]
                hnT = [fw.sbuf(st, "hnT", [128, 8, 512], BF16) for _ in range(2)]
                thnT = [T(), T()]
                fst = [fw.sbuf(st, "fst", [128, 16, 512], BF16) for _ in range(2)]
                tfst = [T(), T()]
                tst = [fw.sbuf(st, "tst", [128, 4, 512], BF16) for _ in range(2)]
                ttst = [T(), T()]
                gst_ = [fw.sbuf(st, "gst", [128, 4, 48], F32) for _ in range(2)]
                tgst = [T(), T()]
                fcols = [c * 128 for c in range(8)] + [1024, 1152, 1280, 1408, 1536, 1664, 2048, 2176]
                it = 0
                for R in range(8):
                    b = R % 2
                    for u in range(4):
                        ti = 4 * R + u
                        s2 = it % 2
                        it += 1
                        fw.dma(sp, xt[s2][:], x_in[ti * 128:(ti + 1) * 128, :], w=[txt[s2]])
                        rmsnorm_bf(xt[s2][:], txt[s2], hn[s2][:], thn[s2], junk[:], tjunk, [z[:] for z in sm[s2]], tsm[s2])
                        transpose_to(hn[s2], thn[s2], hnT[b], thnT[b], 8, ps[6 + s2], pst[6 + s2], u * 128)
                    for ci, c0 in enumerate(fcols):
                        pb = ci % 4
                        for k in range(8):
                            fw.op(pe, lambda e: e.matmul(out=ps[pb][:], lhsT=w_in[:, k, c0:c0 + 128], rhs=hnT[b][:, k, :],
                                                         start=(k == 0), stop=(k == 7)), r=[t_w, thnT[b]], w=[pst[pb]])
                        evac(fst[b][:, ci, :], ps[pb][:], r=[pst[pb]], w=[tfst[b]], scale=(0.125 if ci < 8 else None))
                    cs = slice(R * 512, (R + 1) * 512)
                    fw.dma(sp, QT.rearrange("(c p) t -> p c t", p=128)[:, :, cs], fst[b][:, 0:8, :], r=[tfst[b]], w=[tQT])
                    fw.dma(pool, KcT.rearrange("(c p) t -> p c t", p=128)[:, :, cs], fst[b][:, 8:10, :], r=[tfst[b]], w=[tKcT])
                    fw.dma(pool, VcT.rearrange("(c p) t -> p c t", p=128)[:, :, cs], fst[b][:, 10:12, :], r=[tfst[b]], w=[tVcT])
                    fw.dma(sp, KsT.rearrange("(c p) t -> p c t", p=128)[:, :, cs], fst[b][:, 12:14, :], r=[tfst[b]], w=[tKsT])
                    fw.dma(pool, KwT.rearrange("(c p) t -> p c t", p=128)[:, :, cs], fst[b][:, 14:16, :], r=[tfst[b]], w=[tKwT])
                    for u in range(4):
                        pb = 4 + (u % 2)
                        for (c0, cw, o0) in ((1792, 256, 0), (2304, 256, 256)):
                            for k in range(8):
                                fw.op(pe, lambda e: e.matmul(out=ps[pb][:, o0:o0 + cw], lhsT=hnT[b][:, k, u * 128:(u + 1) * 128],
                                                             rhs=w_in[:, k, c0:c0 + cw], start=(k == 0), stop=(k == 7)),
                                      r=[t_w, thnT[b]], w=[pst[pb]])
                        evac(tst[b][:, u, :], ps[pb][:], r=[pst[pb]], w=[ttst[b]])
                        for k in range(8):
                            fw.op(pe, lambda e: e.matmul(out=ps[pb][:, 0:48], lhsT=hnT[b][:, k, u * 128:(u + 1) * 128],
                                                         rhs=w_in[:, k, 2560:2608], start=(k == 0), stop=(k == 7)),
                                  r=[t_w, thnT[b]], w=[pst[pb]])
                        fw.op(dve, lambda e: e.tensor_tensor(out=gst_[b][:, u, :], in0=ps[pb][:, 0:48], in1=bg[:], op=ALU.add),
                              r=[pst[pb], t_bg], w=[tgst[b]])
                    fw.op(act, lambda e: e.activation(out=gst_[b][:], in_=gst_[b][:], func=AF.Sigmoid), r=[tgst[b]], w=[tgst[b]])
                    rs_ = slice(R * 512, (R + 1) * 512)
                    fw.dma(sp, Vs[rs_, :].rearrange("(u p) c -> p u c", p=128), tst[b][:, :, 0:256], r=[ttst[b]], w=[tVs])
                    fw.dma(pool, Vw[rs_, :].rearrange("(u p) c -> p u c", p=128), tst[b][:, :, 256:512], r=[ttst[b]], w=[tVw])
                    fw.dma(sp, G[rs_, :].rearrange("(u p) c -> p u c", p=128), gst_[b][:], r=[tgst[b]], w=[tG])

        fw.barrier()
        if stage >= 2:
            with contextlib.ExitStack() as st:
                kcT_all = fw.sbuf(st, "kcT", [128, 4, 256], BF16)
                t_kc = T()
                vc_all = fw.sbuf(st, "vc", [128, 4, 2, 64], BF16)
                t_vc = T()
                fw.op(pool, lambda e: e.memset(vc_all[:], 0.0), w=[t_vc])
                fw.op(pool, lambda e: e.memset(kcT_all[:], 0.0), w=[t_kc])
                with contextlib.ExitStack() as sb:
                    stgB = fw.sbuf(sb, "stgB", [64, 32, 256], F32)
                    t_stgB = T()
                    w1 = fw.sbuf(sb, "w1", [64, 32, 256], BF16)
                    t_w1 = T()
                    w2f = fw.sbuf(sb, "w2f", [128, 2, 64], F32)
                    w2 = fw.sbuf(sb, "w2", [128, 2, 64], BF16)
                    t_w2f, t_w2 = T(), T()
                    posf = fw.sbuf(sb, "posf", [64, 32], F32)
                    posT = fw.sbuf(sb, "posT", [64, 32], BF16)
                    t_posf, t_posT = T(), T()
                    b1t = fw.sbuf(sb, "b1t", [128, 2], F32)
                    c1b = fw.sbuf(sb, "c1b", [128, 2], F32)
                    b2col = fw.sbuf(sb, "b2col", [64, 1], F32)
                    b2row = fw.sbuf(sb, "b2row", [128, 64], F32)
                    t_b1, t_c1b, t_b2c, t_b2r = T(), T(), T(), T()
                    rawT = [fw.sbuf(sb, "rawT", [64, S], BF16) for _ in range(2)]
                    t_raw = [T(), T()]
                    hidT = fw.sbuf(sb, "hidT", [128, 2, 256], BF16)
                    t_hid = T()
                    for kv in range(2):
                        fw.dma(sp, stgB[:], nsa_cmp_w1[kv].rearrange("(l d) h -> d l h", d=64), w=[t_stgB])
                        for q4 in range(4):
                            E = (dve, pool, act, dve)[q4]
                            if E is act:
                                fw.op(E, lambda e: e.activation(out=w1[:, q4 * 8:(q4 + 1) * 8, :], in_=stgB[:, q4 * 8:(q4 + 1) * 8, :], func=AF.Copy),
                                      r=[t_stgB], w=[t_w1])
                            else:
                                fw.op(E, lambda e: e.tensor_copy(out=w1[:, q4 * 8:(q4 + 1) * 8, :], in_=stgB[:, q4 * 8:(q4 + 1) * 8, :]),
                                      r=[t_stgB], w=[t_w1])
                        fw.dma(sp, w2f[:], nsa_cmp_w2[kv].rearrange("(c p) d -> p c d", p=128), w=[t_w2f])
                        fw.op(dve, lambda e: e.tensor_copy(out=w2[:], in_=w2f[:]), r=[t_w2f], w=[t_w2])
                        fw.dma(sp, posf[:], nsa_cmp_pos[kv].rearrange("l d -> d l"), w=[t_posf], allow_slow_non_contiguous=True)
                        fw.op(dve, lambda e: e.tensor_copy(out=posT[:], in_=posf[:]), r=[t_posf], w=[t_posT])
                        fw.dma(sp, b1t[:], nsa_cmp_b1[kv].rearrange("(c p) -> p c", p=128), w=[t_b1], allow_slow_non_contiguous=True)
                        fw.dma(sp, b2col[:], nsa_cmp_b2[kv].rearrange("(d o) -> d o", o=1), w=[t_b2c], allow_slow_non_contiguous=True)
                        fw.dma(sp, b2row[:], nsa_cmp_b2[kv].partition_broadcast(128), w=[t_b2r])
                        for c in range(2):
                            for l in range(32):
                                fw.op(pe, lambda e: e.matmul(out=ps[0][:, c:c + 1], lhsT=w1[:, l, c * 128:(c + 1) * 128], rhs=posT[:, l:l + 1],
                                                             start=(l == 0), stop=(l == 31)), r=[t_w1, t_posT], w=[pst[0]])
                        fw.op(dve, lambda e: e.tensor_tensor(out=c1b[:], in0=ps[0][:, 0:2], in1=b1t[:], op=ALU.add), r=[pst[0], t_b1], w=[t_c1b])
                        for g in range(4):
                            rt, trt = rawT[g % 2], t_raw[g % 2]
                            src = (KcT, VcT)[kv]
                            fw.dma(sp, rt[:], src[g * 64:(g + 1) * 64, :], r=[(tKcT, tVcT)[kv]], w=[trt])
                            for c in range(2):
                                for l in range(32):
                                    fw.op(pe, lambda e: e.matmul(out=ps[1 + c][:, 0:255], lhsT=w1[:, l, c * 128:(c + 1) * 128],
                                                                 rhs=rt[:, l:l + 16 * 254 + 1:16], start=(l == 0), stop=(l == 31)),
                                          r=[t_w1, trt], w=[pst[1 + c]])
                                fw.op(act, lambda e: e.activation(out=hidT[:, c, 0:255], in_=ps[1 + c][:, 0:255], func=AF.Gelu_apprx_tanh,
                                                                  bias=c1b[:, c:c + 1]), r=[pst[1 + c], t_c1b], w=[t_hid])
                            if kv == 0:
                                for c in range(2):
                                    fw.op(pe, lambda e: e.matmul(out=ps[3][0:64, 0:255], lhsT=w2[:, c, :], rhs=hidT[:, c, 0:255],
                                                                 start=(c == 0), stop=(c == 1)), r=[t_w2, t_hid], w=[pst[3]])
                                fw.op(dve, lambda e: e.tensor_scalar(out=kcT_all[0:64, g, 0:255], in0=ps[3][0:64, 0:255], scalar1=b2col[:],
                                                                     scalar2=None, op0=ALU.add), r=[pst[3], t_b2c], w=[t_kc])
                            else:
                                for nch, (n0, nn) in enumerate(((0, 128), (128, 127))):
                                    for c in range(2):
                                        fw.op(pe, lambda e: e.matmul(out=ps[3][0:nn, nch * 64:(nch + 1) * 64], lhsT=hidT[:, c, n0:n0 + nn],
                                                                     rhs=w2[:, c, :], start=(c == 0), stop=(c == 1)), r=[t_w2, t_hid], w=[pst[3]])
                                    fw.op(dve, lambda e: e.tensor_tensor(out=vc_all[0:nn, g, nch, :], in0=ps[3][0:nn, nch * 64:(nch + 1) * 64],
                                                                         in1=b2row[0:nn, :], op=ALU.add), r=[pst[3], t_b2r], w=[t_vc])
                fw.barrier()
                if debug:
                    dbg_kc = nc.dram_tensor("dbg_kc", [128, 4, 256], BF16, kind="ExternalOutput").ap()
                    dbg_vc = nc.dram_tensor("dbg_vc", [128, 4, 2, 64], BF16, kind="ExternalOutput").ap()
                    fw.dma(sp, dbg_kc, kcT_all[:], r=[t_kc])
                    fw.dma(sp, dbg_vc, vc_all[:], r=[t_vc])

                if stage >= 3:
                    d0 = fw.sbuf(st, "d0", [128, 3, 128], F32)
                    distc = fw.sbuf(st, "distc", [128, 504], F32)
                    fbias = fw.sbuf(st, "fbias", [128, 126], F32)
                    t_cc = T()
                    fw.dma(sp, d0[:], c_d0.rearrange("k p t -> p k t"), w=[t_cc])
                    fw.dma(sp, distc[:], c_distc, w=[t_cc])
                    fw.dma(sp, fbias[:], c_fbias, w=[t_cc])
                    ovl = fw.sbuf(st, "ovl", [128, 2, 64], BF16)
                    fw.dma(sp, ovl[:], c_ovl.rearrange("k p j -> p k j"), w=[t_cc])
                    bcb = fw.sbuf(st, "bcb", [128, 4, 504], BF16)
                    ks_sel = fw.sbuf(st, "ks_sel", [128, S], BF16)
                    t_ks = T()
                    fw.dma(sp, ks_sel[64:128, :], c_e256, w=[t_ks])
                    kw = fw.sbuf(st, "kw", [128, S], BF16)
                    t_kw = T()
                    fw.op(pool, lambda e: e.memset(kw[64:128, :], 0.0), w=[t_kw])
                    vs_aug = fw.sbuf(st, "vs_aug", [128, 32, 65], BF16)
                    vw_aug = fw.sbuf(st, "vw_aug", [128, 32, 65], BF16)
                    t_vs, t_vw = T(), T()
                    fw.op(pool, lambda e: e.memset(vs_aug[:], 1.0), w=[t_vs])
                    fw.op(pool, lambda e: e.memset(vw_aug[:], 1.0), w=[t_vw])
                    qstack = fw.sbuf(st, "qstack", [128, 4, S], BF16)
                    t_q = T()
                    fw.op(pool, lambda e: e.memset(qstack[:], 0.0), w=[t_q])
                    t_qm = [T() for _ in range(32)]
                    A = fw.sbuf(st, "A", [128, 33, 512], BF16)
                    t_A = T()
                    bc = fw.sbuf(st, "bc", [128, 4, 504], F32)
                    t_bc = T()
                    gt = fw.sbuf(st, "gt", [128, 32, 12], F32)
                    t_gt = T()
                    sc = fw.sbuf(st, "sc", [128, 4, 256], F32)
                    pc = fw.sbuf(st, "pc", [128, 4, 256], F32)
                    t_sc, t_pc = T(), T()
                    rowsum = fw.sbuf(st, "rowsum", [128, 4], F32)
                    rinv = fw.sbuf(st, "rinv", [128, 4], F32)
                    t_rs, t_ri = T(), T()
                    pn = fw.sbuf(st, "pn", [128, 4, 256], BF16)
                    t_pn = T()
                    fw.op(pool, lambda e: e.memset(pn[:], 0.0), w=[t_pn])
                    psumh = fw.sbuf(st, "psumh", [128, 256], F32)
                    t_ph = T()
                    fw.op(pool, lambda e: e.memset(psumh[:], 0.0), w=[t_ph])
                    pnT = fw.sbuf(st, "pnT", [128, 8, 128], BF16)
                    t_pnT = T()
                    imp = fw.sbuf(st, "imp", [128, 64], F32)
                    score = fw.sbuf(st, "score", [128, 64], F32)
                    score2 = fw.sbuf(st, "score2", [128, 64], F32)
                    m8a = fw.sbuf(st, "m8a", [128, 8], F32)
                    m8b = fw.sbuf(st, "m8b", [128, 8], F32)
                    t_imp, t_score, t_score2, t_m8a, t_m8b = T(), T(), T(), T(), T()
                    msel = fw.sbuf(st, "msel", [128, 128], BF16)
                    t_msel = T()
                    fw.op(pool, lambda e: e.memset(msel[:], 0.0), w=[t_msel])
                    NPT = 3
                    pt = [fw.sbuf(st, "pt", [128, 512], BF16) for _ in range(NPT)]
                    pa = [fw.sbuf(st, "pa", [128, 512], BF16) for _ in range(NPT)]
                    t_pt = [T() for _ in range(NPT)]
                    t_pa = [T() for _ in range(NPT)]
                    osb = fw.sbuf(st, "osb", [65, 512], F32)
                    t_osb = T()
                    f4 = fw.sbuf(st, "f4", [128, 4], F32)
                    t_f4 = T()
                    yacc = fw.sbuf(st, "yacc", [128, 256], F32)
                    t_yacc = T()
                    ybf = [fw.sbuf(st, "ybf", [128, 256], BF16) for _ in range(2)]
                    t_ybf = [T(), T()]
                    pv6 = ps[6][:].bitcast(BF16)
                    pst7b = pst[7]
                    yacc2 = [yacc, fw.sbuf(st, "yacc2", [128, 256], F32)]
                    t_yacc2 = [t_yacc, T()]
                    osb2 = [osb, fw.sbuf(st, "osb2", [65, 512], F32)]
                    t_osb2 = [t_osb, T()]
                    f2 = [fw.sbuf(st, "f2", [128, 2], F32) for _ in range(2)]
                    t_f2 = [T(), T()]
                    NPT2 = 6
                    pt = pt + [fw.sbuf(st, "pt", [128, 512], BF16) for _ in range(3)]
                    pa = pa + [fw.sbuf(st, "pa", [128, 512], BF16) for _ in range(3)]
                    t_pt = t_pt + [T(), T(), T()]
                    t_pa = t_pa + [T(), T(), T()]
                    kctr = [0]
                    import os as _os
                    _NG = int(_os.environ.get('NSA_G', '4')); _NA = int(_os.environ.get('NSA_A', '32'))

                    CS = []
                    for ci in range(4):
                        if ci == 0:
                            tiles = [sc, pc, pn, psumh, pnT, rowsum, rinv, imp, score, score2, m8a, m8b, msel]
                            trk = [t_sc, t_pc, t_pn, t_ph, t_pnT, t_rs, t_ri, t_imp, t_score, t_score2, t_m8a, t_m8b, t_msel]
                        else:
                            tiles = [fw.sbuf(st, "sc", [128, 4], F32), fw.sbuf(st, "pc", [128, 4], F32),
                                     fw.sbuf(st, "pn", [128, 4, 256], BF16), fw.sbuf(st, "psumh", [128, 4], F32),
                                     fw.sbuf(st, "pnT", [128, 8, 128], BF16), fw.sbuf(st, "rowsum", [128, 4], F32),
                                     fw.sbuf(st, "rinv", [128, 4], F32), fw.sbuf(st, "imp", [128, 64], F32),
                                     fw.sbuf(st, "score", [128, 64], F32), fw.sbuf(st, "score2", [128, 64], F32),
                                     fw.sbuf(st, "m8a", [128, 8], F32), fw.sbuf(st, "m8b", [128, 8], F32),
                                     fw.sbuf(st, "msel", [128, 128], BF16)]
                            trk = [T() for _ in range(13)]
                            fw.op(pool, lambda e: e.memset(tiles[2][:], 0.0), w=[trk[2]])
                            fw.op(pool, lambda e: e.memset(tiles[12][:], 0.0), w=[trk[12]])
                        tiles = tiles + [fw.sbuf(st, "gf", [128, 4], F32), fw.sbuf(st, "imp4", [128, 4, 64], F32)]
                        trk = trk + [T(), T()]
                        CS.append(tuple(tiles + trk))
                    ocbuf = fw.sbuf(st, "ocbuf", [128, 32, 256], BF16)
                    t_oc = [T() for _ in range(32)]

                    def cmp_thunks(g, a, ci):
                        th = []
                        (sc_, pc_, pcb, psumh_, pnT, rowsum, rinv, imp, score, score2, m8a, m8b, msel, gf, imp4,
                         t_sc_, t_pc_, t_pcb, t_ph_, t_pnT, t_rs, t_ri, t_imp, t_score, t_score2, t_m8a, t_m8b, t_msel, t_gf, t_imp4) = CS[ci]
                        bS, bT = 2 * ci, 2 * ci + 1
                        pvT = ps[bT][:].bitcast(BF16)
                        ta = slice(a * 128, (a + 1) * 128)
                        boff = 248 - 8 * a
                        for hp in range(2):
                            def f_mm(hp=hp):
                                for hh in range(2):
                                    h = 2 * hp + hh
                                    fw.op(pe, lambda e: e.matmul(out=ps[bS][:, hh * 256:hh * 256 + 255], lhsT=qstack[:, h, ta],
                                                                 rhs=kcT_all[:, g, 0:255], start=True, stop=False), r=[t_q, t_qm[a], t_kc], w=[pst[bS]])
                                    fw.op(pe, lambda e: e.matmul(out=ps[bS][:, hh * 256:hh * 256 + 255], lhsT=ident_bf[:],
                                                                 rhs=bcb[:, h, boff:boff + 255], start=False, stop=True), r=[t_bc, t_ident], w=[pst[bS]])
                            th.append(f_mm)
                            for hh in range(2):
                                def f_exp(hp=hp, hh=hh):
                                    h = 2 * hp + hh
                                    fw.op(act, lambda e: e.activation(out=pcb[:, h, 0:255], in_=ps[bS][:, hh * 256:hh * 256 + 255], func=AF.Exp,
                                                                      accum_out=rowsum[:, h:h + 1]), r=[pst[bS]], w=[t_pcb, t_rs])
                                th.append(f_exp)

                        def f_rinv():
                            fw.op(dve, lambda e: e.tensor_scalar(out=rinv[:], in0=rowsum[:], scalar1=1e-30, scalar2=None, op0=ALU.max),
                                  r=[t_rs], w=[t_ri])
                            fw.op(dve, lambda e: e.reciprocal(out=rinv[:], in_=rinv[:]), r=[t_ri], w=[t_ri])
                        th.append(f_rinv)

                        def f_pnT():
                            for h in range(4):
                                for nch in range(2):
                                    fw.op(pe, lambda e: e.transpose(out=pvT[:, (h * 2 + nch) * 128:(h * 2 + nch + 1) * 128],
                                                                    in_=pcb[:, h, nch * 128:(nch + 1) * 128], identity=ident_bf[:]),
                                          r=[t_pcb, t_ident], w=[pst[bT]])
                            fw.op(act, lambda e: e.activation(out=pnT[:], in_=pvT.rearrange("p (k t) -> p k t", k=8), func=AF.Copy),
                                  r=[pst[bT]], w=[t_pnT])
                        th.append(f_pnT)
                        th.append(lambda: fw.op(dve, lambda e: e.tensor_tensor(out=gf[:], in0=rinv[:], in1=gt[:, a, 0:12:3], op=ALU.mult),
                                                r=[t_ri, t_gt], w=[t_gf]))

                        def f_oc():
                            for h in range(4):
                                for nch in range(2):
                                    fw.op(pe, lambda e: e.matmul(out=ps[bS][:, h * 64:(h + 1) * 64], lhsT=pnT[:, h * 2 + nch, :],
                                                                 rhs=vc_all[:, g, nch, :], start=(nch == 0), stop=(nch == 1)),
                                          r=[t_pnT, t_vc], w=[pst[bS]])
                                for nch in range(2):
                                    fw.op(pe, lambda e: e.matmul(out=ps[bS][:, 256 + h * 64:256 + (h + 1) * 64], lhsT=pnT[:, h * 2 + nch, :],
                                                                 rhs=ovl[:, nch, :], start=(nch == 0), stop=(nch == 1)),
                                          r=[t_pnT, t_cc], w=[pst[bS]])
                        th.append(f_oc)
                        th.append(lambda: fw.op(dve, lambda e: e.tensor_tensor(out=ocbuf[:, a, :].rearrange("p (h c) -> p h c", h=4),
                                                                               in0=ps[bS][:, 0:256].rearrange("p (h c) -> p h c", h=4),
                                                                               in1=gf[:].unsqueeze(2).to_broadcast([128, 4, 64]), op=ALU.mult),
                                                r=[pst[bS], t_gf], w=[t_oc[a]]))
                        th.append(lambda: fw.op(dve, lambda e: e.tensor_tensor(out=imp4[:], in0=ps[bS][:, 256:512].rearrange("p (h c) -> p h c", h=4),
                                                                               in1=rinv[:].unsqueeze(2).to_broadcast([128, 4, 64]), op=ALU.mult),
                                                r=[pst[bS], t_ri], w=[t_imp4]))

                        def f_imp():
                            fw.op(dve, lambda e: e.tensor_reduce(out=imp[:], in_=imp4[:].rearrange("p h j -> p j h"), axis=AX.X, op=ALU.add),
                                  r=[t_imp4], w=[t_imp])
                            fw.op(dve, lambda e: e.tensor_tensor(out=score[:], in0=imp[:], in1=fbias[:, 62 - 2 * a:62 - 2 * a + 64], op=ALU.add),
                                  r=[t_imp, t_cc], w=[t_score])
                            fw.op(dve, lambda e: e.memset(score[:, 0:1], 100.0), w=[t_score])
                        th.append(f_imp)

                        def f_topk():
                            fw.op(dve, lambda e: e.max(out=m8a[:], in_=score[:]), r=[t_score], w=[t_m8a])
                            fw.op(dve, lambda e: e.match_replace(out=score2[:], in_to_replace=m8a[:], in_values=score[:], imm_value=-1.0e9),
                                  r=[t_score, t_m8a], w=[t_score2])
                            fw.op(dve, lambda e: e.max(out=m8b[:], in_=score2[:]), r=[t_score2], w=[t_m8b])
                            fw.op(dve, lambda e: e.tensor_scalar(out=msel[:, 64:128], in0=score[:], scalar1=m8b[:, 7:8], scalar2=-1.0,
                                                                 op0=ALU.is_ge, op1=ALU.add), r=[t_score, t_m8b], w=[t_msel])
                        th.append(f_topk)

                        def f_mask():
                            fw.op(pe, lambda e: e.transpose(out=pvT[:, 0:128], in_=msel[:], identity=ident_bf[:]), r=[t_msel, t_ident], w=[pst[bT]])
                            for h in range(4):
                                if h % 2 == 0:
                                    fw.op(act, lambda e: e.activation(out=qstack[64:128, h, ta], in_=pvT[64:128, 0:128], func=AF.Copy),
                                          r=[pst[bT]], w=[t_qm[a]])
                                else:
                                    fw.op(dve, lambda e: e.tensor_copy(out=qstack[64:128, h, ta], in_=pvT[64:128, 0:128]), r=[pst[bT]], w=[t_qm[a]])
                        th.append(f_mask)
                        return th

                    def obank(a, kind):
                        return ((3, 3), (4, 4))[kind][a % 2]

                    f4k = [fw.sbuf(st, "f4k", [128, 4], F32) for _ in range(2)]
                    t_f4k = [T(), T()]
                    otmp = [fw.sbuf(st, "otmp", [128, 4, 64], F32) for _ in range(2)]
                    t_otmp = [T(), T()]

                    def finish_thunks(g, a, kind):
                        ya, tya = yacc2[a % 2], t_yacc2[a % 2]
                        ob, tob = osb2[kind], t_osb2[kind]
                        bank = obank(a, kind)
                        gcol = 1 + kind
                        ff, tff = f4k[kind], t_f4k[kind]
                        tmp, ttmp = otmp[kind], t_otmp[kind]
                        ta = slice(a * 128, (a + 1) * 128)
                        th = []
                        th.append(lambda: fw.op(act, lambda e: e.activation(out=ob[:], in_=ps[bank][0:65, :], func=AF.Copy), r=[pst[bank]], w=[tob]))

                        def f_tr():
                            for h in range(4):
                                fw.op(pe, lambda e: e.transpose(out=ps[7][:, h * 65:(h + 1) * 65], in_=ob[:, h * 128:(h + 1) * 128],
                                                                identity=ident_f[0:65, 0:65]), r=[tob, t_ident], w=[pst[7]])
                        th.append(f_tr)
                        th.append(lambda: fw.op(dve, lambda e: e.tensor_scalar(out=ff[:], in0=ps[7][:, 64:260:65], scalar1=1e-30, scalar2=None, op0=ALU.max),
                                                r=[pst[7]], w=[tff]))
                        th.append(lambda: fw.op(dve, lambda e: e.reciprocal(out=ff[:], in_=ff[:]), r=[tff], w=[tff]))
                        th.append(lambda: fw.op(dve, lambda e: e.tensor_tensor(out=ff[:], in0=ff[:], in1=gt[:, a, gcol:12:3], op=ALU.mult),
                                                r=[tff, t_gt], w=[tff]))
                        th.append(lambda: fw.op(dve, lambda e: e.tensor_tensor(out=tmp[:], in0=ps[7][:, 0:260].rearrange("p (h c) -> p h c", c=65)[:, :, 0:64],
                                                                               in1=ff[:].unsqueeze(2).to_broadcast([128, 4, 64]), op=ALU.mult),
                                                r=[pst[7], tff], w=[ttmp]))
                        if kind == 0:
                            th.append(lambda: fw.op(pool, lambda e: e.tensor_tensor(out=ya[:], in0=tmp[:].rearrange("p h c -> p (h c)"), in1=ocbuf[:, a, :], op=ALU.add),
                                                    r=[ttmp, t_oc[a]], w=[tya]))
                        else:
                            th.append(lambda: fw.op(pool, lambda e: e.tensor_tensor(out=ya[:], in0=tmp[:].rearrange("p h c -> p (h c)"), in1=ya[:], op=ALU.add),
                                                    r=[ttmp, tya], w=[tya]))

                            def f_out():
                                yb, tyb = ybf[a % 2], t_ybf[a % 2]
                                fw.op(act, lambda e: e.activation(out=yb[:], in_=ya[:], func=AF.Copy), r=[tya], w=[tyb])
                                fw.dma(sp, Y[ta, g * 256:(g + 1) * 256], yb[:], r=[tyb], w=[tY])
                            th.append(f_out)
                        return th

                    for g in range(_NG):
                        for h in range(4):
                            fw.dma(sp, qstack[0:64, h, :], QT[(4 * g + h) * 64:(4 * g + h + 1) * 64, :], r=[tQT], w=[t_q])
                        fw.dma(sp, ks_sel[0:64, :], KsT[g * 64:(g + 1) * 64, :], r=[tKsT], w=[t_ks])
                        fw.dma(sp, kw[0:64, :], KwT[g * 64:(g + 1) * 64, :], r=[tKwT], w=[t_kw])
                        fw.dma(pool, vs_aug[:, :, 0:64], Vs[:, g * 64:(g + 1) * 64].rearrange("(a p) d -> p a d", p=128), r=[tVs], w=[t_vs])
                        fw.dma(pool, vw_aug[:, :, 0:64], Vw[:, g * 64:(g + 1) * 64].rearrange("(a p) d -> p a d", p=128), r=[tVw], w=[t_vw])
                        fw.dma(sp, gt[:], G[:, g * 12:(g + 1) * 12].rearrange("(a p) c -> p a c", p=128), r=[tG], w=[t_gt])
                        for h in range(4):
                            sl = SLOPES[4 * g + h]
                            for dl in range(33):
                                src = d0[:, 0, :] if dl == 0 else (d0[:, 2, :] if dl == 32 else d0[:, 1, :])
                                off = 512.0 if dl == 32 else 128.0 * dl
                                fw.op(act, lambda e: e.activation(out=A[:, dl, h * 128:(h + 1) * 128], in_=src, func=AF.Exp,
                                                                  scale=-sl, bias=-sl * off), r=[t_cc], w=[t_A])
                            fw.op(dve, lambda e: e.tensor_scalar(out=bcb[:, h, :], in0=distc[:], scalar1=-sl, scalar2=None, op0=ALU.mult),
                                  r=[t_cc], w=[t_bc])
                        for a0 in range(0, _NA, 4):
                            chains = [cmp_thunks(g, a0 + ci, ci) for ci in range(4) if a0 + ci < _NA]
                            while chains:
                                for c_ in chains:
                                    c_.pop(0)()
                                chains = [c_ for c_ in chains if c_]
                        LA = 4
                        SB = (0, 1, 2, 6, 5)
                        pend = []
                        for a in range(_NA):
                            ta = slice(a * 128, (a + 1) * 128)
                            steps = [(0, b) for b in range(a + 1)] + [(1, b) for b in range(max(0, a - 4), a + 1)]
                            n = len(steps)
                            per = -(-len(pend) // max(1, n - 1))
                            slots = {}
                            newp = []
                            for i in range(n + LA):
                                if i < n:
                                    kind, b = steps[i]
                                    sbk = SB[kctr[0] % 5]
                                    k3 = kctr[0] % NPT2
                                    kctr[0] += 1
                                    slots[i] = k3
                                    krows = 128
                                    kT, tk = (ks_sel, t_ks) if kind == 0 else (kw, t_kw)
                                    dl = a - b
                                    ai = dl if (kind == 0 or dl < 4) else 32
                                    fw.op(pe, lambda e: e.matmul(out=ps[sbk][:], lhsT=kT[0:krows, b * 128:(b + 1) * 128],
                                                                 rhs=qstack[0:krows, :, ta], start=True, stop=True),
                                          r=[tk, t_q, t_qm[a]], w=[pst[sbk]])
                                    fw.op(act, lambda e: e.activation(out=pt[k3][:], in_=ps[sbk][:], func=AF.Exp), r=[pst[sbk]], w=[t_pt[k3]])
                                    fw.op(dve, lambda e: e.tensor_tensor(out=pa[k3][:], in0=pt[k3][:], in1=A[:, ai, :], op=ALU.mult),
                                          r=[t_pt[k3], t_A], w=[t_pa[k3]])
                                    if i >= 1:
                                        for _ in range(per):
                                            if pend:
                                                pend.pop(0)()
                                if i >= LA:
                                    j = i - LA
                                    kind, b = steps[j]
                                    k3 = slots[j]
                                    vaug, tv = (vs_aug, t_vs) if kind == 0 else (vw_aug, t_vw)
                                    first = (j == 0) or (steps[j - 1][0] != kind)
                                    last = (j == n - 1) or (steps[j + 1][0] != kind)
                                    bank = obank(a, kind)
                                    fw.op(pe, lambda e: e.matmul(out=ps[bank][0:65, :], lhsT=vaug[:, b, :], rhs=pa[k3][:],
                                                                 start=first, stop=last), r=[tv, t_pa[k3]], w=[pst[bank]])
                                    if last:
                                        newp += finish_thunks(g, a, kind)
                            while pend:
                                pend.pop(0)()
                            pend = newp
                        while pend:
                            pend.pop(0)()

        fw.barrier()
        tXB = [T() for _ in range(16)]

        def phase_ffn(layer, outproj, final):
            with contextlib.ExitStack() as st:
                wi = fw.sbuf(st, "wi", [128, 8, 2 * D_FF], BF16)
                wo2 = fw.sbuf(st, "wo2", [128, 22, D], BF16)
                t_wi, t_wo2 = T(), T()
                gf, tgf = load_gain(st, norm_ffn[layer], "gf")
                stg = [fw.sbuf(st, "stgF", [128, 704], F32) for _ in range(4)]
                tstg = [T() for _ in range(4)]
                if outproj:
                    wo = fw.sbuf(st, "wo", [128, 8, D], BF16)
                    t_wo = T()
                    for k in range(8):
                        load_cast(st, wo[:, k, :], t_wo, nsa_w_out[k * 128:(k + 1) * 128, :], 128, D, stg=stg, tstg=tstg)
                for k in range(8):
                    load_cast(st, wi[:, k, :], t_wi, ffn_w_in[layer, k * 128:(k + 1) * 128, :], 128, 2 * D_FF,
                              gain=gf[:, k:k + 1], tgain=tgf, stg=stg, tstg=tstg)
                for f in range(22):
                    load_cast(st, wo2[:, f, :], t_wo2, ffn_w_out[layer, f * 128:(f + 1) * 128, :], 128, D, stg=stg, tstg=tstg)
                if final:
                    gfin = fw.sbuf(st, "gfin", [128, D], F32)
                    t_gfin = T()
                    fw.dma(sp, gfin[:], norm_final.partition_broadcast(128), w=[t_gfin])
                xb = [fw.sbuf(st, "xb", [128, 2, D], F32) for _ in range(2)]
                t_xb = [T(), T()]
                if outproj:
                    ybt = [fw.sbuf(st, "ybt", [128, 2, D], BF16)] * 2
                    t_ybt = [T()] * 2
                    yT = fw.sbuf(st, "yT", [128, 8, 256], BF16)
                    t_yT = T()
                hn = [fw.sbuf(st, "hnF", [128, D], BF16) for _ in range(2)]
                thn = [T(), T()]
                junk = fw.sbuf(st, "junkF", [128, D], BF16)
                tjunk = T()
                sm = [[fw.sbuf(st, "smF", [128, 1], F32) for _ in range(3)] for _ in range(2)]
                tsm = [T(), T()]
                hnT = fw.sbuf(st, "hnTF", [128, 8, 256], BF16)
                thnT = T()
                actT = fw.sbuf(st, "actT", [128, 22, 256], BF16)
                t_actT = T()
                sg = [fw.sbuf(st, "sg", [128, 256], F32) for _ in range(2)]
                t_sg = [T(), T()]
                if final:
                    ob = [fw.sbuf(st, "ob", [128, D], F32) for _ in range(2)]
                    t_ob = [T(), T()]
                tout = T()
                for blk in range(16):
                    s2 = blk % 2
                    rows = slice(blk * 256, (blk + 1) * 256)
                    X, tX = xb[s2], t_xb[s2]
                    if outproj:
                        fw.dma(sp, X[:], x_in[rows, :].rearrange("(u p) d -> p u d", p=128), w=[tX])
                        fw.dma(pool, ybt[s2][:], Y[rows, :].rearrange("(u p) d -> p u d", p=128), r=[tY], w=[t_ybt[s2]])
                        for u in range(2):
                            transpose_to(ybt[s2][:, u, :], t_ybt[s2], yT, t_yT, 8, ps[6 + u], pst[6 + u], u * 128)
                        for u in range(2):
                            for nh in range(2):
                                pb = 4 + nh
                                for k in range(8):
                                    fw.op(pe, lambda e: e.matmul(out=ps[pb][:], lhsT=yT[:, k, u * 128:(u + 1) * 128],
                                                                 rhs=wo[:, k, nh * 512:(nh + 1) * 512], start=(k == 0), stop=(k == 7)),
                                          r=[t_yT, t_wo], w=[pst[pb]])
                                fw.op(dve, lambda e: e.tensor_tensor(out=X[:, u, nh * 512:(nh + 1) * 512], in0=X[:, u, nh * 512:(nh + 1) * 512],
                                                                     in1=ps[pb][:], op=ALU.add), r=[pst[pb], tX], w=[tX])
                    else:
                        fw.dma(sp, X[:], X1[rows, :].rearrange("(u p) d -> p u d", p=128), r=[tXB[blk]], w=[tX])
                    for u in range(2):
                        rmsnorm_bf(X[:, u, :], tX, hn[u][:], thn[u], junk[:], tjunk, [z[:] for z in sm[u]], tsm[u])
                        transpose_to(hn[u], thn[u], hnT, thnT, 8, ps[6 + u], pst[6 + u], u * 128)
                    for f in range(22):
                        pb = f % 4
                        for half in range(2):
                            c0 = half * D_FF + f * 128
                            for k in range(8):
                                fw.op(pe, lambda e: e.matmul(out=ps[pb][:, half * 256:(half + 1) * 256], lhsT=wi[:, k, c0:c0 + 128],
                                                             rhs=hnT[:, k, :], start=(k == 0), stop=(k == 7)), r=[t_wi, thnT], w=[pst[pb]])
                        fw.op(act, lambda e: e.activation(out=sg[f % 2][:], in_=ps[pb][:, 0:256], func=AF.Silu), r=[pst[pb]], w=[t_sg[f % 2]])
                        fw.op(dve, lambda e: e.tensor_tensor(out=actT[:, f, :], in0=sg[f % 2][:], in1=ps[pb][:, 256:512], op=ALU.mult),
                              r=[pst[pb], t_sg[f % 2]], w=[t_actT])
                    for u in range(2):
                        for nh in range(2):
                            pb = 4 + nh
                            for f in range(22):
                                fw.op(pe, lambda e: e.matmul(out=ps[pb][:], lhsT=actT[:, f, u * 128:(u + 1) * 128],
                                                             rhs=wo2[:, f, nh * 512:(nh + 1) * 512], start=(f == 0), stop=(f == 21)),
                                      r=[t_actT, t_wo2], w=[pst[pb]])
                            fw.op(dve, lambda e: e.tensor_tensor(out=X[:, u, nh * 512:(nh + 1) * 512], in0=X[:, u, nh * 512:(nh + 1) * 512],
                                                                 in1=ps[pb][:], op=ALU.add), r=[pst[pb], tX], w=[tX])
                    if final:
                        for u in range(2):
                            ss, sd, rs = [z[:] for z in sm[u]]
                            fw.op(act, lambda e: e.activation(out=junk[:], in_=X[:, u, :], func=AF.Square, accum_out=ss), r=[tX], w=[tjunk, tsm[u]])
                            fw.op(act, lambda e: e.activation(out=sd, in_=ss, func=AF.Sqrt, bias=epsc[:], scale=1.0 / D), r=[tsm[u], t_eps], w=[tsm[u]])
                            fw.op(dve, lambda e: e.reciprocal(out=rs, in_=sd), r=[tsm[u]], w=[tsm[u]])
                            fw.op(dve, lambda e: e.scalar_tensor_tensor(out=ob[u][:], in0=X[:, u, :], scalar=rs, in1=gfin[:], op0=ALU.mult, op1=ALU.mult),
                                  r=[tX, tsm[u], t_gfin], w=[t_ob[u]])
                            fw.dma(sp, out_ap[blk * 256 + u * 128:blk * 256 + (u + 1) * 128, :], ob[u][:], r=[t_ob[u]], w=[tout])
                    else:
                        fw.dma(sp, X1[rows, :].rearrange("(u p) d -> p u d", p=128), X[:], r=[tX], w=[tXB[blk]])
                fw.barrier()

        if stage >= 4:
            phase_ffn(0, True, False)

        if stage >= 5:
            with contextlib.ExitStack() as st:
                NB = 512
                NU = NB // 128
                wl = fw.sbuf(st, "wl", [128, 8, 2 * D_RNN], BF16)
                t_wl = T()
                gl, tgl = load_gain(st, norm_mix[1], "gl")
                wa = fw.sbuf(st, "wa", [88, 16, 176], BF16)
                wx = fw.sbuf(st, "wx", [88, 16, 176], BF16)
                wlo = fw.sbuf(st, "wlo", [88, 16, D], BF16)
                t_wa, t_wx, t_wlo = T(), T(), T()
                cw = fw.sbuf(st, "cw", [88, 16, 4], F32)
                cb = fw.sbuf(st, "cb", [88, 16], F32)
                nba = fw.sbuf(st, "nba", [88, 16], F32)
                nbx = fw.sbuf(st, "nbx", [88, 16], F32)
                nsp = fw.sbuf(st, "nsp", [88, 16], F32)
                t_small = T()
                for j in range(4):
                    fw.dma(sp, cw[:, :, j], lru_conv_w[j].rearrange("(c p) -> p c", p=88), w=[t_small], allow_slow_non_contiguous=True)
                fw.dma(sp, cb[:], lru_conv_b.rearrange("(c p) -> p c", p=88), w=[t_small], allow_slow_non_contiguous=True)
                fw.dma(sp, nba[:], lru_b_a.rearrange("(c p) -> p c", p=88), w=[t_small], allow_slow_non_contiguous=True)
                fw.dma(sp, nbx[:], lru_b_x.rearrange("(c p) -> p c", p=88), w=[t_small], allow_slow_non_contiguous=True)
                fw.dma(sp, nsp[:], lru_lambda.rearrange("(c p) -> p c", p=88), w=[t_small], allow_slow_non_contiguous=True)
                fw.op(dve, lambda e: e.tensor_scalar(out=nba[:], in0=nba[:], scalar1=0.5, scalar2=None, op0=ALU.mult), r=[t_small], w=[t_small])
                fw.op(dve, lambda e: e.tensor_scalar(out=nbx[:], in0=nbx[:], scalar1=0.5, scalar2=None, op0=ALU.mult), r=[t_small], w=[t_small])
                fw.op(act, lambda e: e.activation(out=nsp[:], in_=nsp[:], func=AF.Exp, scale=-1.0), r=[t_small], w=[t_small])
                fw.op(act, lambda e: e.activation(out=nsp[:], in_=nsp[:], func=AF.Ln, bias=1.0), r=[t_small], w=[t_small])
                fw.op(dve, lambda e: e.tensor_scalar(out=nsp[:], in0=nsp[:], scalar1=-4.0, scalar2=None, op0=ALU.mult), r=[t_small], w=[t_small])
                halo = fw.sbuf(st, "halo", [88, 16, 3], F32)
                hlast = fw.sbuf(st, "hlast", [88, 16], F32)
                t_halo, t_hl = T(), T()
                fw.op(pool, lambda e: e.memset(halo[:], 0.0), w=[t_halo])
                fw.op(pool, lambda e: e.memset(hlast[:], 0.0), w=[t_hl])
                X2 = [fw.sbuf(st, "xbL", [128, NU, D], F32) for _ in range(2)]
                tX2 = [T(), T()]
                cur = [0]
                hn = [fw.sbuf(st, "hnL", [128, D], BF16) for _ in range(2)]
                thn = [T(), T()]
                junk = fw.sbuf(st, "junkL", [128, D], BF16)
                tjunk = T()
                sm = [[fw.sbuf(st, "smL", [128, 1], F32) for _ in range(3)] for _ in range(2)]
                tsm = [T(), T()]
                hnT_1 = fw.sbuf(st, "hnTL", [128, 8, NB], BF16)
                thnT_1 = T()
                hnT2 = [hnT_1, hnT_1]
                thnT2 = [thnT_1, thnT_1]
                g16 = [fw.sbuf(st, "gate16", [88, 16, NB], BF16) for _ in range(2)]
                t_g16_2 = [T(), T()]

                with contextlib.ExitStack() as st_w:
                    stg = [fw.sbuf(st_w, "stgL", [128, 704], F32) for _ in range(4)]
                    tstg = [T() for _ in range(4)]
                    for k in range(8):
                        load_cast(st_w, wl[:, k, :], t_wl, lru_w_in[k * 128:(k + 1) * 128, :], 128, 2 * D_RNN,
                                  gain=gl[:, k:k + 1], tgain=tgl, stg=stg, tstg=tstg)
                    for (dst, tdst, src) in ((wa, t_wa, lru_w_a), (wx, t_wx, lru_w_x)):
                        for n in range(8):
                            for ih in range(2):
                                load_cast(st_w, dst[:, 2 * n + ih, :], tdst, src[n, ih * 88:(ih + 1) * 88, :], 88, 176, stg=stg, tstg=tstg)
                    for c in range(16):
                        load_cast(st_w, wlo[:, c, :], t_wlo, lru_w_out[c * 88:(c + 1) * 88, :], 88, D, stg=stg, tstg=tstg)
                    fw.barrier()

                def mk(name, dt=F32, w=NB, nbuf=1):
                    return [fw.sbuf(st, name, [88, 2, w], dt) for _ in range(nbuf)], [[T(), T()] for _ in range(nbuf)]
                recb, t_recb = mk("recb", F32, NB + 3, 2)
                xr, t_xr = mk("xr", F32, NB, 2)
                xrb, t_xrb = mk("xrb", BF16, NB, 2)
                rr_, t_rr = mk("rr")
                ii_, t_ii = mk("ii")
                aa, t_aa = mk("aa")
                uu, t_uu = mk("uu")
                hh, t_hh = mk("hh")
                rr_, ii_, aa, uu, hh = rr_[0], ii_[0], aa[0], uu[0], hh[0]
                t_rr, t_ii, t_aa, t_uu, t_hh = t_rr[0], t_ii[0], t_aa[0], t_uu[0], t_hh[0]

                def conv_chain(n, half):
                    q = n % 2
                    c = 2 * n + half
                    pb = half
                    RB, XR, XB = recb[q], xr[q], xrb[q]
                    tRB, tXR, tXB_ = t_recb[q][half], t_xr[q][half], t_xrb[q][half]
                    th = []

                    def f0():
                        for k in range(8):
                            fw.op(pe, lambda e: e.matmul(out=ps[pb][0:88, :], lhsT=wl[:, k, D_RNN + c * 88:D_RNN + (c + 1) * 88], rhs=hnT2[cur[0] % 2][:, k, :],
                                                         start=(k == 0), stop=(k == 7)), r=[t_wl, thnT2[cur[0] % 2]], w=[pst[pb]])
                        fw.op(act, lambda e: e.activation(out=RB[:, half, 0:3], in_=halo[:, c, :], func=AF.Copy), r=[t_halo], w=[tRB])
                    th.append(f0)

                    def f1():
                        fw.op(act, lambda e: e.activation(out=RB[:, half, 3:3 + NB], in_=ps[pb][0:88, :], func=AF.Copy), r=[pst[pb]], w=[tRB])
                        fw.op(act, lambda e: e.activation(out=halo[:, c, :], in_=RB[:, half, NB:NB + 3], func=AF.Copy), r=[tRB], w=[t_halo])
                    th.append(f1)
                    th.append(lambda: fw.op(dve, lambda e: e.tensor_scalar(out=XR[:, half, :], in0=RB[:, half, 0:NB], scalar1=cw[:, c, 0:1],
                                                                           scalar2=cb[:, c:c + 1], op0=ALU.mult, op1=ALU.add),
                                            r=[tRB, t_small], w=[tXR]))
                    for j in range(1, 4):
                        th.append(lambda j=j: fw.op(dve, lambda e: e.scalar_tensor_tensor(out=XR[:, half, :], in0=RB[:, half, j:j + NB], scalar=cw[:, c, j:j + 1],
                                                                                           in1=XR[:, half, :], op0=ALU.mult, op1=ALU.add),
                                                    r=[tRB, t_small, tXR], w=[tXR]))
                    th.append(lambda: fw.op(act, lambda e: e.activation(out=XB[:, half, :], in_=XR[:, half, :], func=AF.Copy), r=[tXR], w=[tXB_]))
                    return th

                def gate_chain(n, oh):
                    q = n % 2
                    XR, XB = xr[q], xrb[q]
                    c = 2 * n + oh
                    pr, pi = 2 + 2 * oh, 3 + 2 * oh
                    R_, I_, A_, U_, H_ = rr_[:, oh, :], ii_[:, oh, :], aa[:, oh, :], uu[:, oh, :], hh[:, oh, :]
                    tr, ti, ta_, tu, th_ = [t_rr[oh]], [t_ii[oh]], [t_aa[oh]], [t_uu[oh]], [t_hh[oh]]
                    th = []

                    def f0():
                        for (wgt, twg, pbG) in ((wa, t_wa, pr), (wx, t_wx, pi)):
                            for ih in range(2):
                                fw.op(pe, lambda e: e.matmul(out=ps[pbG][0:88, :], lhsT=wgt[:, 2 * n + ih, oh * 88:(oh + 1) * 88],
                                                             rhs=XB[:, ih, :], start=(ih == 0), stop=(ih == 1)),
                                      r=[twg, t_xrb[q][0], t_xrb[q][1]], w=[pst[pbG]])
                    th.append(f0)
                    th.append(lambda: fw.op(act, lambda e: e.activation(out=R_, in_=ps[pr][0:88, :], func=AF.Tanh, bias=nba[:, c:c + 1], scale=0.5),
                                            r=[pst[pr], t_small], w=tr))
                    th.append(lambda: fw.op(act, lambda e: e.activation(out=I_, in_=ps[pi][0:88, :], func=AF.Tanh, bias=nbx[:, c:c + 1], scale=0.5),
                                            r=[pst[pi], t_small], w=ti))
                    th.append(lambda: fw.op(act, lambda e: e.activation(out=A_, in_=R_, func=AF.Exp, scale=nsp[:, c:c + 1], bias=nsp[:, c:c + 1]),
                                            r=tr + [t_small], w=ta_))
                    th.append(lambda: fw.op(dve, lambda e: e.scalar_tensor_tensor(out=I_, in0=I_, scalar=1.0, in1=XR[:, oh, :], op0=ALU.add, op1=ALU.mult),
                                            r=ti + [t_xr[q][oh]], w=ti))
                    th.append(lambda: fw.op(dve, lambda e: e.scalar_tensor_tensor(out=U_, in0=A_, scalar=-1.0, in1=A_, op0=ALU.mult, op1=ALU.mult),
                                            r=ta_, w=tu))
                    th.append(lambda: fw.op(dve, lambda e: e.tensor_scalar(out=U_, in0=U_, scalar1=1.0, scalar2=1e-30, op0=ALU.add, op1=ALU.max),
                                            r=tu, w=tu))
                    th.append(lambda: fw.op(act, lambda e: e.activation(out=U_, in_=U_, func=AF.Ln), r=tu, w=tu))
                    th.append(lambda: fw.op(act, lambda e: e.activation(out=U_, in_=U_, func=AF.Exp, scale=0.5), r=tu, w=tu))
                    th.append(lambda: fw.op(dve, lambda e: e.scalar_tensor_tensor(out=U_, in0=U_, scalar=0.5, in1=I_, op0=ALU.mult, op1=ALU.mult),
                                            r=tu + ti, w=tu))

                    def f_scan():
                        fw.op(dve, lambda e: e.tensor_tensor_scan(out=H_, data0=A_, data1=U_, initial=hlast[:, c:c + 1], op0=ALU.mult, op1=ALU.add),
                              r=ta_ + tu + [t_hl], w=th_)
                        fw.op(dve, lambda e: e.tensor_copy(out=hlast[:, c:c + 1], in_=hh[:, oh, NB - 1:NB]), r=th_, w=[t_hl])
                    th.append(f_scan)
                    th.append(lambda: fw.op(dve, lambda e: e.tensor_tensor(out=g16[cur[0] % 2][:, c, :], in0=H_, in1=g16[cur[0] % 2][:, c, :], op=ALU.mult),
                                            r=th_, w=[t_g16_2[cur[0] % 2]]))
                    return th

                def zip_emit(chains):
                    chains = [c for c in chains if c]
                    while chains:
                        for c in chains:
                            c.pop(0)()
                        chains = [c for c in chains if c]

                def blk_rows(b):
                    return slice(b * NB, (b + 1) * NB), [tXB[b * (NB // 256) + q] for q in range(NB // 256)]

                def prep_load(b):
                    rows, tblks = blk_rows(b)
                    fw.dma(sp, X2[b % 2][:], X1[rows, :].rearrange("(u p) d -> p u d", p=128), r=tblks, w=[tX2[b % 2]])

                def prep_norm(b, u):
                    rmsnorm_bf(X2[b % 2][:, u, :], tX2[b % 2], hn[u % 2][:], thn[u % 2], junk[:], tjunk, [z[:] for z in sm[u % 2]], tsm[u % 2])
                    transpose_to(hn[u % 2], thn[u % 2], hnT2[b % 2], thnT2[b % 2], 8, ps[6 + u % 2], pst[6 + u % 2], u * 128)

                def gelu_stage(b):
                    for c in range(16):
                        pb = c % 2
                        for k in range(8):
                            fw.op(pe, lambda e: e.matmul(out=ps[pb][0:88, :], lhsT=wl[:, k, c * 88:(c + 1) * 88], rhs=hnT2[b % 2][:, k, :],
                                                         start=(k == 0), stop=(k == 7)), r=[t_wl, thnT2[b % 2]], w=[pst[pb]])
                        fw.op(act, lambda e: e.activation(out=g16[b % 2][:, c, :], in_=ps[pb][0:88, :], func=AF.Gelu_apprx_tanh),
                              r=[pst[pb]], w=[t_g16_2[b % 2]])

                def outproj_unit(b, u, nh):
                    pb = 6 + nh
                    Xb, tXb = X2[b % 2], tX2[b % 2]
                    for c in range(16):
                        fw.op(pe, lambda e: e.matmul(out=ps[pb][:], lhsT=g16[b % 2][:, c, u * 128:(u + 1) * 128],
                                                     rhs=wlo[:, c, nh * 512:(nh + 1) * 512], start=(c == 0), stop=(c == 15)),
                              r=[t_g16_2[b % 2], t_wlo], w=[pst[pb]])
                    fw.op(dve, lambda e: e.tensor_tensor(out=Xb[:, u, nh * 512:(nh + 1) * 512], in0=Xb[:, u, nh * 512:(nh + 1) * 512],
                                                         in1=ps[pb][:], op=ALU.add), r=[pst[pb], tXb], w=[tXb])

                def store(b):
                    rows, tblks = blk_rows(b)
                    fw.dma(sp, X1[rows, :].rearrange("(u p) d -> p u d", p=128), X2[b % 2][:], r=[tX2[b % 2]], w=tblks)

                NBLK = S // NB
                prep_load(0)
                for u in range(NU):
                    prep_norm(0, u)
                gelu_stage(0)
                for blk in range(NBLK):
                    cur[0] = blk
                    zip_emit([conv_chain(0, 0), conv_chain(0, 1)])
                    for n in range(8):
                        chains = [gate_chain(n, 0), gate_chain(n, 1)]
                        if n + 1 < 8:
                            chains += [conv_chain(n + 1, 0), conv_chain(n + 1, 1)]
                        zip_emit(chains)
                        if blk >= 1 and n < NU:
                            outproj_unit(blk - 1, n, 0)
                            outproj_unit(blk - 1, n, 1)
                            if n == NU - 1:
                                store(blk - 1)
                        if blk + 1 < NBLK:
                            if n == NU - 1:
                                prep_load(blk + 1)
                            if n in (6, 7):
                                prep_norm(blk + 1, 2 * (n - 6))
                                prep_norm(blk + 1, 2 * (n - 6) + 1)
                    if blk + 1 < NBLK:
                        gelu_stage(blk + 1)
                for u in range(NU):
                    outproj_unit(NBLK - 1, u, 0)
                    outproj_unit(NBLK - 1, u, 1)
                store(NBLK - 1)
                fw.barrier()

        if stage >= 6:
            phase_ffn(1, False, True)

        fw.finish()
    return nc


_CONSTS = None


def make_in_map(inp, b):
    global _CONSTS
    if _CONSTS is None:
        _CONSTS = host_consts()
    f = lambda a: np.ascontiguousarray(np.asarray(a, dtype=np.float32))
    m = {
        "x": f(inp["x"][b]),
        "norm_mix": f(inp["norm_mix"]), "norm_ffn": f(inp["norm_ffn"]), "norm_final": f(inp["norm_final"]),
        "nsa_w_in": f(inp["nsa_w_in"][0]), "nsa_b_gate": f(inp["nsa_b_gate"][0]),
        "nsa_cmp_pos": f(inp["nsa_cmp_pos"][0]), "nsa_cmp_w1": f(inp["nsa_cmp_w1"][0]),
        "nsa_cmp_b1": f(inp["nsa_cmp_b1"][0]), "nsa_cmp_w2": f(inp["nsa_cmp_w2"][0]),
        "nsa_cmp_b2": f(inp["nsa_cmp_b2"][0]), "nsa_w_out": f(inp["nsa_w_out"][0]),
        "lru_w_in": f(inp["lru_w_in"][0]), "lru_conv_w": f(inp["lru_conv_w"][0]),
        "lru_conv_b": f(inp["lru_conv_b"][0]), "lru_w_a": f(inp["lru_w_a"][0]), "lru_b_a": f(inp["lru_b_a"][0]),
        "lru_w_x": f(inp["lru_w_x"][0]), "lru_b_x": f(inp["lru_b_x"][0]), "lru_lambda": f(inp["lru_lambda"][0]),
        "lru_w_out": f(inp["lru_w_out"][0]), "ffn_w_in": f(inp["ffn_w_in"]), "ffn_w_out": f(inp["ffn_w_out"]),
    }
    m.update(_CONSTS)
    return m


def kernel(**inputs):
    nc = build(debug=False)
    n = 4
    maps = [make_in_map(inputs, b) for b in range(n)]
    res = run_bass_kernel_spmd(nc, maps, core_ids=list(range(n)))
    return np.stack([np.asarray(res.results[b]["out"], dtype=np.float32) for b in range(n)], axis=0)
```
